# Optimizing a Trainium2 kernel written in Bass

```python
import math
import jax, jax.numpy as jnp
from jax import lax
import numpy as np

D_MODEL = 1024
BATCH = 16
SEQ = 256
DEPTH = 1
DEC_BATCH = 2
DEC_SEQ = 1024
PAST_LEN = 512

GRID_W = 64
ML_HEADS = 4
ML_HEAD_DIM = D_MODEL // ML_HEADS
ML_WIDTH = ML_HEADS * ML_HEAD_DIM
DA_HEADS = 8
DA_HEAD_DIM = D_MODEL // (2 * DA_HEADS)
DA_V_DIM = 2 * DA_HEAD_DIM
DA_QK_WIDTH = 2 * DA_HEADS * DA_HEAD_DIM
DA_WIDTH = DA_HEADS * DA_V_DIM
D_FF = 2816
N_MOD = 9
N_GATE_COLS = 4 * ML_HEADS
CHUNK = 128
Q_BLOCK = 128
ROPE_BASE = 10000.0
EPS = 1e-6
IN_SIZES = (ML_WIDTH, ML_WIDTH, ML_WIDTH, ML_WIDTH, N_GATE_COLS,
            DA_QK_WIDTH, DA_QK_WIDTH, DA_WIDTH, D_MODEL, D_MODEL)
D_IN = 4 * ML_WIDTH + N_GATE_COLS + 2 * DA_QK_WIDTH + DA_WIDTH + 2 * D_MODEL

kernel_name = 'diffusion_hybrid_mlstm_diffattn_step'


def rmsnorm(x, g):
    xf = x.astype(jnp.float32)
    y = xf * lax.rsqrt(jnp.mean(xf * xf, axis=-1, keepdims=True) + EPS)
    return (y * g.astype(jnp.float32)).astype(x.dtype)


def adaln(cvec, w, b):
    m = jax.nn.silu(cvec) @ w + b
    return m.reshape(cvec.shape[0], N_MOD, D_MODEL)


def swiglu(h, w1, w3, w2):
    return (jax.nn.silu(h @ w1) * (h @ w3)) @ w2


def split_cols(proj):
    idx = []
    acc = 0
    for s in IN_SIZES[:-1]:
        acc += s
        idx.append(acc)
    return jnp.split(proj, idx, axis=-1)


def _rotate(xs, pos):
    n = xs.shape[-1] // 2
    freqs = jnp.power(ROPE_BASE, -jnp.arange(n, dtype=jnp.float32) / n)
    ang = pos.astype(jnp.float32)[:, None] * freqs[None, :]
    cos = jnp.cos(ang)[None, :, None, None, :]
    sin = jnp.sin(ang)[None, :, None, None, :]
    x1, x2 = xs[..., :n], xs[..., n:]
    return jnp.concatenate([x1 * cos - x2 * sin, x2 * cos + x1 * sin], axis=-1)


def rope_2d(x):
    T = x.shape[1]
    rows = T // GRID_W
    row = jnp.repeat(jnp.arange(rows), GRID_W)
    col = jnp.tile(jnp.arange(GRID_W), rows)
    half = DA_HEAD_DIM // 2
    xf = x.astype(jnp.float32)
    out = jnp.concatenate([_rotate(xf[..., :half], row), _rotate(xf[..., half:], col)], axis=-1)
    return out.astype(x.dtype)


def diff_attention(q, k, v, lam):
    B, Tq = q.shape[0], q.shape[1]
    nb = Tq // Q_BLOCK
    kf = k.astype(jnp.float32)
    vf = v.astype(jnp.float32)
    scale = DA_HEAD_DIM ** -0.5
    qb = jnp.moveaxis(q.reshape(B, nb, Q_BLOCK, DA_HEADS, 2, DA_HEAD_DIM), 1, 0)

    def block(qi):
        s = jnp.einsum('bqhcd,bkhcd->bhcqk', qi.astype(jnp.float32), kf) * scale
        pr = jax.nn.softmax(s, axis=-1)
        w = pr[:, :, 0] - lam * pr[:, :, 1]
        return jnp.einsum('bhqk,bkhv->bqhv', w, vf)

    out = lax.map(block, qb)
    return jnp.moveaxis(out, 0, 1).reshape(B, Tq, DA_HEADS, DA_V_DIM)


def mlstm_chunkwise(q, k, v, log_i, log_f, state):
    B, H, T, d = q.shape
    nc = T // CHUNK
    chunk = lambda z: jnp.moveaxis(z.reshape(z.shape[:2] + (nc, CHUNK) + z.shape[3:]), 2, 0)
    tril = jnp.tril(jnp.ones((CHUNK, CHUNK), dtype=bool))
    C0, n0, m0 = (s.astype(jnp.float32) for s in state)

    def step(carry, xs):
        C, n, m = carry
        qc, kc, vc, ic, fc = xs
        b = jnp.cumsum(fc, axis=-1)
        log_d = jnp.where(tril, b[..., :, None] - b[..., None, :] + ic[..., None, :], -jnp.inf)
        a = b + m[..., None]
        m_t = jnp.maximum(a, jnp.max(log_d, axis=-1))
        dmat = jnp.exp(log_d - m_t[..., None])
        inter = jnp.exp(a - m_t)
        s = jnp.einsum('bhtd,bhsd->bhts', qc, kc) * dmat
        num = jnp.einsum('bhts,bhsv->bhtv', s, vc) + inter[..., None] * jnp.einsum('bhtd,bhdv->bhtv', qc, C)
        den = jnp.sum(s, axis=-1) + inter * jnp.einsum('bhtd,bhd->bht', qc, n)
        h = num / jnp.maximum(jnp.abs(den), jnp.exp(-m_t))[..., None]
        b_last = b[..., -1]
        g = b_last[..., None] - b + ic
        m_new = jnp.maximum(b_last + m, jnp.max(g, axis=-1))
        w = jnp.exp(g - m_new[..., None])
        decay = jnp.exp(b_last + m - m_new)
        C_new = decay[..., None, None] * C + jnp.einsum('bhs,bhsd,bhsv->bhdv', w, kc, vc)
        n_new = decay[..., None] * n + jnp.einsum('bhs,bhsd->bhd', w, kc)
        return (C_new, n_new, m_new), h

    final, hs = lax.scan(step, (C0, n0, m0), (chunk(q), chunk(k), chunk(v), chunk(log_i), chunk(log_f)))
    h = jnp.moveaxis(hs, 0, 2).reshape(B, H, T, d)
    return h, final


def token_mixer(h, p, l, ctx):
    B, T, _ = h.shape
    lam_init = 0.8 - 0.6 * math.exp(-0.3 * l)
    mq, mk, mv, mo, mg, dq, dk, dv, gm, gd = split_cols(h @ p['w_in'][l])
    dq = rmsnorm(dq.reshape(B, T, DA_HEADS, 2, DA_HEAD_DIM), p['g_qn'][l])
    dk = rmsnorm(dk.reshape(B, T, DA_HEADS, 2, DA_HEAD_DIM), p['g_kn'][l])
    dv = dv.reshape(B, T, DA_HEADS, DA_V_DIM)
    if ctx is None:
        keys, vals = dk, dv
        zero = (jnp.zeros((B, ML_HEADS, ML_HEAD_DIM, ML_HEAD_DIM), jnp.float32),
                jnp.zeros((B, ML_HEADS, ML_HEAD_DIM), jnp.float32),
                jnp.zeros((B, ML_HEADS), jnp.float32))
        st0_f, st0_b = zero, zero
        q_att = dq
    else:
        k_ctx, v_ctx, st0_f, st0_b = ctx
        q_att = rope_2d(dq)
        keys = jnp.concatenate([rope_2d(dk), k_ctx.astype(dk.dtype)], axis=1)
        vals = jnp.concatenate([dv, v_ctx.astype(dv.dtype)], axis=1)
    f32 = lambda z: z.astype(jnp.float32)
    lam = (jnp.exp(jnp.sum(f32(p['lam_q1'][l]) * f32(p['lam_k1'][l])))
           - jnp.exp(jnp.sum(f32(p['lam_q2'][l]) * f32(p['lam_k2'][l]))) + lam_init)
    att = diff_attention(q_att, keys, vals, lam)
    att = (rmsnorm(att, p['g_sub'][l]) * (1.0 - lam_init)).reshape(B, T, DA_WIDTH).astype(h.dtype)
    heads = lambda z: z.reshape(B, T, ML_HEADS, ML_HEAD_DIM).transpose(0, 2, 1, 3).astype(jnp.float32)
    q = heads(mq) * (ML_HEAD_DIM ** -0.5)
    k = heads(mk)
    v = heads(mv)
    gates = (mg + p['b_gate'][l]).astype(jnp.float32).reshape(B, T, 2, 2, ML_HEADS).transpose(2, 3, 0, 4, 1)
    log_i = gates[:, 0]
    log_f = jax.nn.log_sigmoid(gates[:, 1])
    h_f, st_f = mlstm_chunkwise(q, k, v, log_i[0], log_f[0], st0_f)
    rev = lambda z: jnp.flip(z, axis=2)
    h_b, st_b = mlstm_chunkwise(rev(q), rev(k), rev(v), jnp.flip(log_i[1], -1), jnp.flip(log_f[1], -1), st0_b)
    hm = (h_f + rev(h_b)).transpose(0, 2, 1, 3)
    hm = (rmsnorm(hm, p['g_mh'][l]).reshape(B, T, ML_WIDTH) * jax.nn.sigmoid(mo.astype(jnp.float32))).astype(h.dtype)
    y = jax.nn.sigmoid(gm) * (hm @ p['w_br_m'][l]) + jax.nn.sigmoid(gd) * (att @ p['w_br_d'][l])
    out = y @ p['w_out'][l]
    ctx_out = (dk, dv, st_f, st_b) if ctx is None else None
    return out, ctx_out


def layer(x, mods, l, p, ctx):
    m = [mods[:, j][:, None, :] for j in range(N_MOD)]
    g = p['g_norm'][l]
    hh = rmsnorm(x, g[0]) * (1 + m[1]) + m[0]
    x = x + 0.5 * m[2] * swiglu(hh, p['ffn1_w1'][l], p['ffn1_w3'][l], p['ffn1_w2'][l])
    hh = rmsnorm(x, g[1]) * (1 + m[4]) + m[3]
    y, ctx_out = token_mixer(hh, p, l, ctx)
    x = x + m[5] * y
    hh = rmsnorm(x, g[2]) * (1 + m[7]) + m[6]
    x = x + 0.5 * m[8] * swiglu(hh, p['ffn2_w1'][l], p['ffn2_w3'][l], p['ffn2_w2'][l])
    return x, ctx_out


def setup_inputs(seed: int = 0) -> dict:
    key = jax.random.key(seed)
    ks = iter(jax.random.split(key, 48))
    nrm = lambda shape, s: jax.random.normal(next(ks), shape, jnp.float32) * s
    D, F = D_MODEL, D_FF
    i_b = nrm((DEPTH, 2, 1, ML_HEADS), 0.1)
    f_b = jnp.linspace(3.0, 6.0, ML_HEADS) + nrm((DEPTH, 2, 1, ML_HEADS), 0.1)
    return {
        'x_prompt': nrm((BATCH, SEQ, D), 1.0),
        'x_sample': nrm((DEC_BATCH, DEC_SEQ, D), 1.0),
        'c': nrm((DEC_BATCH, D), 1.0),
        'cache_k': nrm((DEC_BATCH, DEPTH, PAST_LEN, DA_HEADS, 2, DA_HEAD_DIM), 1.0),
        'cache_v': nrm((DEC_BATCH, DEPTH, PAST_LEN, DA_HEADS, DA_V_DIM), 1.0),
        'state_C': nrm((DEC_BATCH, DEPTH, 2, ML_HEADS, ML_HEAD_DIM, ML_HEAD_DIM), 0.05),
        'state_n': nrm((DEC_BATCH, DEPTH, 2, ML_HEADS, ML_HEAD_DIM), 0.1),
        'state_m': nrm((DEC_BATCH, DEPTH, 2, ML_HEADS), 1.0),
        'c_ctx': nrm((D,), 1.0),
        'w_ada': nrm((DEPTH, D, N_MOD * D), 0.5 * D ** -0.5),
        'b_ada': nrm((DEPTH, N_MOD * D), 0.02),
        'g_norm': 1.0 + nrm((DEPTH, 3, D), 0.05),
        'ffn1_w1': nrm((DEPTH, D, F), D ** -0.5),
        'ffn1_w3': nrm((DEPTH, D, F), D ** -0.5),
        'ffn1_w2': nrm((DEPTH, F, D), F ** -0.5),
        'ffn2_w1': nrm((DEPTH, D, F), D ** -0.5),
        'ffn2_w3': nrm((DEPTH, D, F), D ** -0.5),
        'ffn2_w2': nrm((DEPTH, F, D), F ** -0.5),
        'w_in': nrm((DEPTH, D, D_IN), D ** -0.5),
        'b_gate': jnp.concatenate([i_b, f_b], axis=2).reshape(DEPTH, N_GATE_COLS),
        'g_qn': 1.0 + nrm((DEPTH, DA_HEAD_DIM), 0.05),
        'g_kn': 1.0 + nrm((DEPTH, DA_HEAD_DIM), 0.05),
        'lam_q1': nrm((DEPTH, DA_HEAD_DIM), 0.1),
        'lam_k1': nrm((DEPTH, DA_HEAD_DIM), 0.1),
        'lam_q2': nrm((DEPTH, DA_HEAD_DIM), 0.1),
        'lam_k2': nrm((DEPTH, DA_HEAD_DIM), 0.1),
        'g_sub': 1.0 + nrm((DEPTH, DA_V_DIM), 0.05),
        'g_mh': 1.0 + nrm((DEPTH, ML_HEADS, ML_HEAD_DIM), 0.05),
        'w_br_m': nrm((DEPTH, ML_WIDTH, D), ML_WIDTH ** -0.5),
        'w_br_d': nrm((DEPTH, DA_WIDTH, D), DA_WIDTH ** -0.5),
        'w_out': nrm((DEPTH, D, D), D ** -0.5),
    }


def reference(x_prompt, x_sample, c, cache_k, cache_v, state_C, state_n, state_m, c_ctx,
              w_ada, b_ada, g_norm, ffn1_w1, ffn1_w3, ffn1_w2, ffn2_w1, ffn2_w3, ffn2_w2,
              w_in, b_gate, g_qn, g_kn, lam_q1, lam_k1, lam_q2, lam_k2, g_sub, g_mh,
              w_br_m, w_br_d, w_out):
    p = dict(g_norm=g_norm, ffn1_w1=ffn1_w1, ffn1_w3=ffn1_w3, ffn1_w2=ffn1_w2,
             ffn2_w1=ffn2_w1, ffn2_w3=ffn2_w3, ffn2_w2=ffn2_w2, w_in=w_in, b_gate=b_gate,
             g_qn=g_qn, g_kn=g_kn, lam_q1=lam_q1, lam_k1=lam_k1, lam_q2=lam_q2, lam_k2=lam_k2,
             g_sub=g_sub, g_mh=g_mh, w_br_m=w_br_m, w_br_d=w_br_d, w_out=w_out)
    xp, xs = x_prompt, x_sample
    ks_, vs_, Cs, ns, ms = [], [], [], [], []
    for l in range(DEPTH):
        mods_ctx = adaln(c_ctx[None, :], w_ada[l], b_ada[l])
        xp, (k_l, v_l, st_f, st_b) = layer(xp, mods_ctx, l, p, None)
        ks_.append(k_l)
        vs_.append(v_l)
        Cs.append(jnp.stack([st_f[0], st_b[0]], axis=1))
        ns.append(jnp.stack([st_f[1], st_b[1]], axis=1))
        ms.append(jnp.stack([st_f[2], st_b[2]], axis=1))
        mods_lat = adaln(c, w_ada[l], b_ada[l])
        ctx = (cache_k[:, l], cache_v[:, l],
               (state_C[:, l, 0], state_n[:, l, 0], state_m[:, l, 0]),
               (state_C[:, l, 1], state_n[:, l, 1], state_m[:, l, 1]))
        xs, _ = layer(xs, mods_lat, l, p, ctx)
    new_k = jnp.stack(ks_, axis=1)
    new_v = jnp.stack(vs_, axis=1)
    new_C = jnp.stack(Cs, axis=1)
    new_n = jnp.stack(ns, axis=1)
    new_m = jnp.stack(ms, axis=1)
    return (xp, xs, new_k, new_v, new_C, new_n, new_m)
```

```python
import math
from contextlib import ExitStack

import numpy as np
import concourse.bass as bass
import concourse.mybir as mybir
from concourse.bass_utils import run_bass_kernel_spmd

F32 = mybir.dt.float32
BF16 = mybir.dt.bfloat16
AF = mybir.ActivationFunctionType
ALU = mybir.AluOpType
AX = mybir.AxisListType

D = 1024
KC = 8
FF = 2816
NFF = 22
NOWN = 768
NFULL = 768
D_IN = 9232
EPS = 1e-6
BIG = 3.0e4
SEM_LIMIT = 30000
SKIP_SAME = ()
LAM_INIT = 0.8 - 0.6 * math.exp(-0.3 * 0)
O_MQ, O_MK, O_MV, O_MO, O_MG, O_DQ, O_DK, O_DV, O_GM, O_GD = (
    0, 1024, 2048, 3072, 4096, 4112, 5136, 6160, 7184, 8208)


class Buf:
    def __init__(self, name="", psum=False):
        self.name = name
        self.w = None
        self.r = {}
        self.psum = psum


class Eng:
    def __init__(self, P, name, eng):
        self.P, self.name, self.eng = P, name, eng
        self.sem, self.count, self.known = None, 0, {}

    def new_sem(self):
        self.sem = self.P.alloc_sem(self.name)
        self.count = 0
        if not hasattr(self, "own"):
            self.own = set()
        self.own.add(id(self.sem))

    def wait(self, ev):
        if ev is None:
            return
        sem, val = ev
        if self.known.get(id(sem), 0) >= val:
            return
        self.eng.wait_ge(sem, val)
        self.known[id(sem)] = val


class Prog:
    def __init__(self, nc, n_dma_sems=32):
        self.nc = nc
        self.sem_i = 0
        self.engs = {}
        for nm, e in (("pe", nc.tensor), ("act", nc.scalar), ("dve", nc.vector),
                      ("pool", nc.gpsimd), ("sp", nc.sync)):
            self.engs[nm] = Eng(self, nm, e)
        self.dma_slots = []
        self.n_dma_sems = n_dma_sems
        self.dma_i = 0
        self.out_evs = []
        self.n_inst = 0

    def start(self, stack):
        self.stack = stack
        for e in self.engs.values():
            e.new_sem()
        self.q_slots = {}
        for q in ("sp", "pool"):
            self.q_slots[q] = [[self.alloc_sem("dma%s%d" % (q, i)), 0, None] for i in range(self.n_dma_sems // 2)]
            self.dma_slots += self.q_slots[q]
        self.q_i = {"sp": 0, "pool": 0}

    def alloc_sem(self, name):
        self.sem_i += 1
        return self.stack.enter_context(self.nc.semaphore("%s_%d" % (name, self.sem_i)))

    def _deps(self, reads, writes):
        evs = {}

        def add(ev):
            if ev is None:
                return
            k = id(ev[0])
            if k not in evs or evs[k][1] < ev[1]:
                evs[k] = ev
        for b in reads:
            add(b.w)
            if b.psum:
                for ev in b.r.values():
                    add(ev)
        for b in writes:
            add(b.w)
            for ev in b.r.values():
                add(ev)
        return list(evs.values())

    def _commit(self, ev, reads, writes):
        for b in writes:
            b.w = ev
            b.r = {}
        for b in reads:
            if b in writes:
                continue
            k = id(ev[0])
            if k not in b.r or b.r[k][1] < ev[1]:
                b.r[k] = ev

    def op(self, engname, fn, reads=(), writes=()):
        E = self.engs[engname]
        for ev in self._deps(reads, writes):
            if SKIP_SAME and engname in SKIP_SAME and id(ev[0]) in E.own:
                continue
            E.wait(ev)
        if E.count + 1 > SEM_LIMIT:
            E.new_sem()
        inst = fn()
        E.count += 1
        inst.then_inc(E.sem, 1)
        ev = (E.sem, E.count)
        self._commit(ev, reads, writes)
        self.n_inst += 1
        return ev

    def dma(self, qname, out, in_, reads=(), writes=(), is_output=False, **kw):
        Q = self.engs[qname]
        for ev in self._deps(reads, writes):
            Q.wait(ev)
        slots = self.q_slots[qname]
        slot = slots[self.q_i[qname] % len(slots)]
        self.q_i[qname] += 1
        if slot[2] is not None:
            Q.wait(slot[2])
        if slot[1] + 16 > SEM_LIMIT:
            slot[0] = self.alloc_sem("dmax")
            slot[1] = 0
        inst = Q.eng.dma_start(out=out, in_=in_, **kw)
        slot[1] += 16
        inst.then_inc(slot[0], 16)
        ev = (slot[0], slot[1])
        slot[2] = ev
        self._commit(ev, reads, writes)
        if is_output:
            self.out_evs.append(ev)
        self.n_inst += 1
        return ev

    def barrier(self):
        evs = [(e.sem, e.count) for e in self.engs.values() if e.count > 0]
        evs += [s[2] for s in self.dma_slots if s[2] is not None]
        for e in self.engs.values():
            for ev in evs:
                if ev[0] is e.sem:
                    continue
                e.wait(ev)

    def finish(self):
        sp = self.engs["sp"]
        for s in self.dma_slots:
            sp.wait(s[2])
        for ev in self.out_evs:
            sp.wait(ev)
        for e in self.engs.values():
            if e is not sp and e.count > 0:
                sp.wait((e.sem, e.count))


class K:
    pass


def build_program(dbg=(), stages=("full", "ffn1", "mixer", "ffn2")):
    nc = bass.Bass("TRN2", target_bir_lowering=False)
    k = K()
    k.nc = nc
    k.stages = set(stages)
    k.dbg = set(dbg)
    k.dbg_out = {}

    def din(name, shape):
        return nc.dram_tensor(name, list(shape), F32, kind="ExternalInput").ap()

    def dout(name, shape):
        return nc.dram_tensor(name, list(shape), F32, kind="ExternalOutput").ap()

    I = k.I = {}
    for name, shape in (
        ("xo", (NOWN, D)), ("xf", (NFULL, D)), ("cv", (2, D)),
        ("ck", (512, D)), ("cvv", (512, D)), ("sC", (2, 4, 256, 256)), ("sn", (2, 4, 256)),
        ("sm", (2, 4)),
        ("w_ada", (D, 9 * D)), ("b_ada", (72, 128)), ("g_norm", (24, 128)),
        ("f1w1", (D, FF)), ("f1w3", (D, FF)), ("f1w2", (FF, D)),
        ("f2w1", (D, FF)), ("f2w3", (D, FF)), ("f2w2", (FF, D)),
        ("w_in", (D, D_IN)), ("b_gate", (1, 16)), ("g_qn", (64,)), ("g_kn", (64,)),
        ("lamv", (4, 64)), ("g_sub", (128,)), ("g_mh", (1024,)),
        ("w_br_m", (D, D)), ("w_br_d", (D, D)), ("w_out", (D, D)),
        ("rope_own", (256, 128)), ("rope_full", (1024, 128)), ("mu", (24,)),
    ):
        I[name] = din(name, shape)
    O = k.O = {}
    for name, shape in (
        ("yo", (NOWN, D)), ("nk", (512, D)), ("nv", (512, D)),
        ("nC", (2, 2, 4, 256, 256)), ("nn", (2, 2, 4, 256)), ("nm", (2, 2, 4)),
    ):
        O[name] = dout(name, shape)

    P = k.P = Prog(nc)
    with ExitStack() as st:
        P.start(st)
        k.st = st
        emit_all(k)
        P.finish()
    return nc, k


def emit_all(k):
    nc, P, st, I, O = k.nc, k.P, k.st, k.I, k.O

    def sb(name, shape, dt=F32, stack=None):
        t = (stack or st).enter_context(nc.sbuf_tensor(name, list(shape), dt))
        return t, Buf(name)

    k.sb = sb

    def dbg_dump(name, tile_ap, buf, shape):
        if name in k.dbg:
            o = nc.dram_tensor("dbg_" + name, list(shape), F32, kind="ExternalOutput").ap()
            k.dbg_out[name] = o
            if tile_ap.dtype != F32:
                for a in range(shape[1]):
                    sg, b_sg = k.stage[a % 2]
                    P.op("act", lambda a=a, sg=sg: nc.scalar.copy(sg[:, 0:shape[2]], tile_ap[:, a, :]),
                         reads=[buf], writes=[b_sg])
                    P.dma("sp", o[:, a, :], sg[:, 0:shape[2]], reads=[b_sg], is_output=True)
            else:
                P.dma("sp", o, tile_ap, reads=[buf], is_output=True)

    k.dbg_dump = dbg_dump

    k.ps = []
    for i in range(6):
        t = st.enter_context(nc.psum_tensor("ps%d" % i, [128, 512], F32))
        k.ps.append((t, Buf("ps%d" % i, psum=True)))
    k.ps_i = 0
    k.pst = []
    for i in range(2):
        t = st.enter_context(nc.psum_tensor("pst%d" % i, [128, 1024], BF16))
        k.pst.append((t, Buf("pst%d" % i, psum=True)))
    k.pst_i = 0

    k.ps_reserved = set()

    def next_ps():
        while True:
            i = k.ps_i % 6
            k.ps_i += 1
            if i not in k.ps_reserved:
                return k.ps[i]

    def next_pst():
        t, b = k.pst[k.pst_i % 2]
        k.pst_i += 1
        return t, b

    k.next_pst = next_pst

    k.next_ps = next_ps

    idf, b_idf = sb("idf", [128, 128])
    P.op("pool", lambda: nc.gpsimd.memset(idf[:], 1.0), writes=[b_idf])
    P.op("pool", lambda: nc.gpsimd.affine_select(idf[:], idf[:], [[-1, 128]], ALU.is_equal, 0.0,
                                                 base=0, channel_multiplier=1),
         reads=[b_idf], writes=[b_idf])
    idb, b_idb = sb("idb", [128, 128], BF16)
    P.op("dve", lambda: nc.vector.tensor_copy(idb[:], idf[:]), reads=[b_idf], writes=[b_idb])
    onesf, b_ones = sb("onesf", [128, 128])
    P.op("pool", lambda: nc.gpsimd.memset(onesf[:], 1.0), writes=[b_ones])
    maskf, b_maskf = sb("maskf", [128, 128])
    P.op("pool", lambda: nc.gpsimd.memset(maskf[:], 1.0), writes=[b_maskf])
    P.op("pool", lambda: nc.gpsimd.affine_select(maskf[:], maskf[:], [[1, 128]], ALU.is_ge, 0.0,
                                                 base=0, channel_multiplier=-1),
         reads=[b_maskf], writes=[b_maskf])
    maskb, b_maskb = sb("maskb", [128, 128])
    P.op("pool", lambda: nc.gpsimd.memset(maskb[:], 1.0), writes=[b_maskb])
    P.op("pool", lambda: nc.gpsimd.affine_select(maskb[:], maskb[:], [[-1, 128]], ALU.is_ge, 0.0,
                                                 base=0, channel_multiplier=1),
         reads=[b_maskb], writes=[b_maskb])
    ntrif, b_ntrif = sb("ntrif", [128, 128])
    P.op("dve", lambda: nc.vector.tensor_scalar(ntrif[:], maskf[:], -1.0, None, ALU.mult),
         reads=[b_maskf], writes=[b_ntrif])
    ntrib, b_ntrib = sb("ntrib", [128, 128])
    P.op("dve", lambda: nc.vector.tensor_scalar(ntrib[:], maskb[:], -1.0, None, ALU.mult),
         reads=[b_maskb], writes=[b_ntrib])
    onesb, b_onesb = sb("onesb", [128, 128], BF16)
    P.op("dve", lambda: nc.vector.tensor_copy(onesb[:], onesf[:]), reads=[b_ones], writes=[b_onesb])
    epst, b_eps = sb("epst", [128, 1])
    P.op("dve", lambda: nc.vector.memset(epst[:], EPS), writes=[b_eps])
    k.c = dict(onesb=(onesb, b_onesb), idf=(idf, b_idf), idb=(idb, b_idb), ones=(onesf, b_ones), maskf=(maskf, b_maskf),
               maskb=(maskb, b_maskb), ntrif=(ntrif, b_ntrif), ntrib=(ntrib, b_ntrib),
               eps=(epst, b_eps))

    k.NRING = 4
    k.ring = [sb("wring%d" % i, [128, 4096], BF16) for i in range(k.NRING)]
    k.ring_i = 0

    k.stage = [sb("stage%d" % i, [128, D]) for i in range(2)]
    k.nm = ([sb("nm_sq%d" % i, [128, 512]) for i in range(2)], sb("nm_rstd", [128, 512]),
            [sb("nm_tmp%d" % i, [128, 512]) for i in range(2)])
    k.hh2f = sb("hh2full", [128, KC, NFULL], BF16)
    k.xT = sb("xT", [128, KC, NOWN])
    k.pre_tiles = {}
    alloc_adaln(k)
    emit_adaln_pre(k)
    if "full" not in k.stages:
        emit_load_x(k)
    if "full" in k.stages:
        with ExitStack() as ph:
            emit_ffn_full(k, ph)
        if "ffn1" in k.stages:
            k.pre_tiles[0] = ffn_load_group(k, I["f1w1"], I["f1w3"], 0)
        P.barrier()
    else:
        for _ in emit_adaln(k):
            pass
    if "ffn1" in k.stages:
        with ExitStack() as ph:
            emit_ffn(k, ph, 0)
        P.barrier()
    if "mixer" in k.stages:
        with ExitStack() as ph:
            emit_mixer(k, ph)
        if "ffn2" in k.stages:
            k.pre_tiles[2] = ffn_load_group(k, I["f2w1"], I["f2w3"], 0)
        P.barrier()
    if "ffn2" in k.stages:
        with ExitStack() as ph:
            emit_ffn(k, ph, 2)
        P.barrier()
    emit_store_x(k)


def ring_load(k, view_shape_fn, dram_ap):
    t, b = k.ring[k.ring_i % k.NRING]
    k.ring_i += 1
    dst = view_shape_fn(t)
    k.P.dma("pool", dst, dram_ap, writes=[b])
    return t, b


def alloc_adaln(k):
    k.ad = {}
    for name, shape, dt in (("craw", [16, 128], F32), ("silu_c", [128, 16], BF16), ("braw", [72, 128], F32),
                            ("bada", [128, 72], F32), ("graw", [24, 128], F32), ("gnorm", [128, 24], F32),
                            ("mods", [128, 72, 2], F32), ("modA", [128, 3, KC, 2], F32), ("modB", [128, 3, KC, 2], F32),
                            ("modG", [128, 3, KC, 2], F32)):
        k.ad[name] = k.sb(name, shape, dt)


def emit_adaln_pre(k):
    nc, P, I = k.nc, k.P, k.I
    sb = lambda name, shape, dt=F32: k.ad[name]
    idf, b_idf = k.c["idf"]
    craw, b_craw = sb("craw", [16, 128])
    P.dma("sp", craw[:], I["cv"].rearrange("w (kc p) -> (w kc) p", p=128), writes=[b_craw])
    ps, b_ps = k.next_ps()
    P.op("pe", lambda: nc.tensor.transpose(ps[:, 0:16], craw[:], idf[0:16, 0:16]),
         reads=[b_craw, b_idf], writes=[b_ps])
    sc, b_sc = sb("silu_c", [128, 16], BF16)
    P.op("act", lambda: nc.scalar.activation(out=sc[:], in_=ps[:, 0:16], func=AF.Silu),
         reads=[b_ps], writes=[b_sc])
    braw, b_braw = sb("braw", [72, 128])
    P.dma("sp", braw[:], I["b_ada"], writes=[b_braw])
    ps2, b_ps2 = k.next_ps()
    P.op("pe", lambda: nc.tensor.transpose(ps2[:, 0:72], braw[:], idf[0:72, 0:72]),
         reads=[b_braw, b_idf], writes=[b_ps2])
    bada, b_bada = sb("bada", [128, 72])
    P.op("dve", lambda: nc.vector.tensor_copy(bada[:], ps2[:, 0:72]), reads=[b_ps2], writes=[b_bada])
    graw, b_graw = sb("graw", [24, 128])
    P.dma("sp", graw[:], I["g_norm"], writes=[b_graw])
    ps3, b_ps3 = k.next_ps()
    P.op("pe", lambda: nc.tensor.transpose(ps3[:, 0:24], graw[:], idf[0:24, 0:24]),
         reads=[b_graw, b_idf], writes=[b_ps3])
    gn, b_gn = sb("gnorm", [128, 24])
    P.op("dve", lambda: nc.vector.tensor_copy(gn[:], ps3[:, 0:24]), reads=[b_ps3], writes=[b_gn])

    k.ad_tmp = dict(sc=(sc, b_sc), bada=(bada, b_bada), gn=(gn, b_gn))


def emit_adaln(k):
    nc, P, I = k.nc, k.P, k.I
    sb = lambda name, shape, dt=F32: k.ad[name]
    sc, b_sc = k.ad_tmp["sc"]
    bada, b_bada = k.ad_tmp["bada"]
    gn, b_gn = k.ad_tmp["gn"]
    pm, b_pm = k.ps[5]
    k.ps_reserved.add(5)
    wada = I["w_ada"].rearrange("(kc p) n -> p kc n", p=128)
    scv = sc[:].rearrange("p (w kc) -> p kc w", w=2)
    for g in range(18):
        wt, b_wt = ring_load(k, lambda t: t[:].rearrange("p (kc n) -> p kc n", kc=KC),
                             wada[:, :, g * 512:(g + 1) * 512])
        wv = wt[:].rearrange("p (kc n) -> p kc n", kc=KC)

        def mm(wv=wv, g=g):
            inst = None
            for j in range(4):
                oc = g * 4 + j
                for kc in range(KC):
                    inst = nc.tensor.matmul(pm[:, 2 * oc:2 * oc + 2], wv[:, kc, j * 128:(j + 1) * 128],
                                            scv[:, kc, :], start=(kc == 0), stop=(kc == KC - 1))
            return inst
        yield P.op("pe", mm, reads=[b_wt, b_sc], writes=[b_pm])
    mods, b_mods = sb("mods", [128, 72, 2])
    P.op("dve", lambda: nc.vector.tensor_tensor(
        mods[:], pm[:, 0:144].rearrange("p (oc w) -> p oc w", w=2),
        bada[:].unsqueeze(2).to_broadcast([128, 72, 2]), ALU.add),
        reads=[b_pm, b_bada], writes=[b_mods])
    k.ps_reserved.discard(5)
    mA, b_mA = sb("modA", [128, 3, KC, 2])
    mB, b_mB = sb("modB", [128, 3, KC, 2])
    mG, b_mG = sb("modG", [128, 3, KC, 2])
    for i in range(3):
        def fA(i=i):
            return nc.vector.scalar_tensor_tensor(
                out=mA[:, i], in0=mods[:, (3 * i + 1) * 8:(3 * i + 2) * 8, :], scalar=1.0,
                in1=gn[:, i * 8:(i + 1) * 8].unsqueeze(2).to_broadcast([128, KC, 2]),
                op0=ALU.add, op1=ALU.mult)
        P.op("dve", fA, reads=[b_mods, b_gn], writes=[b_mA])
        P.op("dve", lambda i=i: nc.vector.tensor_copy(mB[:, i], mods[:, (3 * i) * 8:(3 * i + 1) * 8, :]),
             reads=[b_mods], writes=[b_mB])
        gs = 1.0 if i == 1 else 0.5
        P.op("dve", lambda i=i, gs=gs: nc.vector.tensor_scalar(
            mG[:, i], mods[:, (3 * i + 2) * 8:(3 * i + 3) * 8, :], gs, None, ALU.mult),
            reads=[b_mods], writes=[b_mG])
    k.mod = dict(A=(mA, b_mA), B=(mB, b_mB), G=(mG, b_mG))
    k.dbg_dump("mods", mods[:], b_mods, [128, 72, 2])


def load_tokens_T(k, dram, ntok, xT, b_xT, tag):
    for _ in load_tokens_gen(k, dram, ntok, xT, b_xT, tag):
        pass


def load_tokens_gen(k, dram, ntok, xT, b_xT, tag):
    nc, P = k.nc, k.P
    idf, b_idf = k.c["idf"]
    if not hasattr(k, "stage"):
        k.stage = [k.sb("stage%d" % i, [128, D]) for i in range(2)]
    stg = k.stage
    for t in range(ntok // 128):
        xs, b_xs = stg[t % 2]
        P.dma("sp", xs[:], dram[t * 128:(t + 1) * 128, :], writes=[b_xs])
        for half in range(2):
            ps, b_ps = k.next_ps()

            def tr(ps=ps, xs=xs, half=half):
                inst = None
                for j in range(4):
                    kc = half * 4 + j
                    inst = nc.tensor.transpose(ps[:, j * 128:(j + 1) * 128], xs[:, kc * 128:(kc + 1) * 128],
                                               idf[:])
                return inst
            P.op("pe", tr, reads=[b_xs, b_idf], writes=[b_ps])
            eng = "dve" if half == 0 else "act"

            def cp(ps=ps, half=half, t=t, eng=eng):
                dst = xT[:, half * 4:(half + 1) * 4, t * 128:(t + 1) * 128]
                src = ps[:].rearrange("p (j n) -> p j n", j=4)
                if eng == "dve":
                    return nc.vector.tensor_copy(dst, src)
                return nc.scalar.copy(dst, src)
            yield P.op(eng, cp, reads=[b_ps], writes=[b_xT])


def emit_load_x(k):
    xT, b_xT = k.xT
    load_tokens_T(k, k.I["xo"], NOWN, xT, b_xT, "o")


def emit_store_x(k):
    nc, P = k.nc, k.P
    xT, b_xT = k.xT
    idf, b_idf = k.c["idf"]
    stg = k.stage
    for t in range(NOWN // 128):
        ys, b_ys = stg[t % 2]
        for half in range(2):
            ps, b_ps = k.next_ps()

            def tr(ps=ps, half=half, t=t):
                inst = None
                for j in range(4):
                    kc = half * 4 + j
                    inst = nc.tensor.transpose(ps[:, j * 128:(j + 1) * 128],
                                               xT[:, kc, t * 128:(t + 1) * 128], idf[:])
                return inst
            P.op("pe", tr, reads=[b_xT, b_idf], writes=[b_ps])
            eng = "dve" if half == 0 else "act"

            def cp(ps=ps, half=half, ys=ys, eng=eng):
                dst = ys[:, half * 512:(half + 1) * 512]
                if eng == "dve":
                    return nc.vector.tensor_copy(dst, ps[:])
                return nc.scalar.copy(dst, ps[:])
            P.op(eng, cp, reads=[b_ps], writes=[b_ys])
        P.dma("sp", k.O["yo"][t * 128:(t + 1) * 128, :], ys[:], reads=[b_ys], is_output=True)


def norm_mod(k, ph, xT, b_xT, groups, ni, hh, b_hh, tag):
    nc, P = k.nc, k.P
    onesb, b_onesb = k.c["onesb"]
    epst, b_eps = k.c["eps"]
    mA, b_mA = k.mod["A"]
    mB, b_mB = k.mod["B"]
    sqs, (rstd, b_rstd), tmps = k.nm
    scr = [sqs[0], sqs[1], tmps[0], tmps[1]]
    for (t0, n, which) in groups:
        ps, b_ps = k.next_ps()
        for i in range(4):
            sq, b_sq = scr[i]
            sqv = sq[:].bitcast(BF16).rearrange("p (c n) -> p c n", c=2)[:, :, 0:n]
            P.op("act", lambda i=i, sqv=sqv: nc.scalar.activation(out=sqv, in_=xT[:, 2 * i:2 * i + 2, t0:t0 + n], func=AF.Square),
                 reads=[b_xT], writes=[b_sq])

            def mm(i=i, sqv=sqv):
                nc.tensor.matmul(ps[:, 0:n], onesb[:], sqv[:, 0, :], start=(i == 0), stop=False)
                return nc.tensor.matmul(ps[:, 0:n], onesb[:], sqv[:, 1, :], start=False, stop=(i == 3))
            P.op("pe", mm, reads=[b_sq, b_onesb], writes=[b_ps])
        P.op("act", lambda: nc.scalar.activation(out=rstd[:, 0:n], in_=ps[:, 0:n], func=AF.Ln,
                                                 scale=1.0 / D, bias=epst[:, 0:1]),
             reads=[b_ps, b_eps], writes=[b_rstd])
        P.op("act", lambda: nc.scalar.activation(out=rstd[:, 0:n], in_=rstd[:, 0:n], func=AF.Exp, scale=-0.5),
             reads=[b_rstd], writes=[b_rstd])
        hv = hh[:, :, t0:t0 + n]
        P.op("dve", lambda: nc.vector.tensor_tensor(hv, xT[:, :, t0:t0 + n], rstd[:, 0:n].unsqueeze(1).to_broadcast([128, KC, n]),
                                                    ALU.mult), reads=[b_xT, b_rstd], writes=[b_hh])
        P.op("dve", lambda: nc.vector.tensor_tensor(hv, hv, mA[:, ni, :, which].unsqueeze(2).to_broadcast([128, KC, n]), ALU.mult),
             reads=[b_hh, b_mA], writes=[b_hh])
        P.op("dve", lambda: nc.vector.tensor_tensor(hv, hv, mB[:, ni, :, which].unsqueeze(2).to_broadcast([128, KC, n]), ALU.add),
             reads=[b_hh, b_mB], writes=[b_hh])


def ffn_load_group(k, w1, w3, cg):
    ncol = 512 if cg < 5 else 256
    tiles = []
    for w in (w1, w3):
        wv = w.rearrange("(kc p) n -> p kc n", p=128)
        t, b = ring_load(k, lambda t, ncol=ncol: t[:, 0:KC * ncol].rearrange("p (kc n) -> p kc n", kc=KC),
                         wv[:, :, cg * 512:cg * 512 + ncol])
        tiles.append((t[:, 0:KC * ncol].rearrange("p (kc n) -> p kc n", kc=KC), b))
    return tiles


def ffn_core(k, ph, xT, b_xT, hh, b_hh, groups, w1, w3, w2, ni, tag, tiles0=None):
    nc, P = k.nc, k.P
    mG, b_mG = k.mod["G"]
    ntok = sum(g[1] for g in groups)
    tb = groups[0][0]
    gT, b_gT = k.sb("ffn_g" + tag, [128, NFF, ntok], BF16, stack=ph)
    sils = [k.sb("ffn_sil%s%d" % (tag, i), [128, 512], stack=ph) for i in range(2)]
    sil_i = [0]
    w2t = [k.sb("ffn_w2%s%d" % (tag, i), [128, NFF, 256], BF16, stack=ph) for i in range(2)]
    w2v = w2.rearrange("(fc p) n -> p fc n", p=128)
    for cg in range(6):
        ncol = 512 if cg < 5 else 256
        tiles = tiles0 if (cg == 0 and tiles0 is not None) else ffn_load_group(k, w1, w3, cg)
        if cg == 3:
            for q in range(2):
                P.dma("pool", w2t[q][0][:], w2v[:, :, q * 256:(q + 1) * 256], writes=[w2t[q][1]])
        for j in range(ncol // 128):
            fc = cg * 4 + j
            for (t0, n, which) in groups:
                pss = []
                for (wv, b_w) in tiles:
                    ps, b_ps = k.next_ps()

                    def mm(ps=ps, wv=wv, j=j, t0=t0, n=n):
                        inst = None
                        for kc in range(KC):
                            inst = nc.tensor.matmul(ps[:, 0:n], wv[:, kc, j * 128:(j + 1) * 128],
                                                    hh[:, kc, t0:t0 + n], start=(kc == 0), stop=(kc == KC - 1))
                        return inst
                    P.op("pe", mm, reads=[b_w, b_hh], writes=[b_ps])
                    pss.append((ps, b_ps))
                (p1, b_p1), (p3, b_p3) = pss
                sil, b_sil = sils[sil_i[0] % 2]
                sil_i[0] += 1
                P.op("act", lambda p1=p1, n=n, sil=sil: nc.scalar.activation(out=sil[:, 0:n], in_=p1[:, 0:n], func=AF.Silu),
                     reads=[b_p1], writes=[b_sil])
                P.op("dve", lambda p3=p3, fc=fc, t0=t0, n=n, sil=sil: nc.vector.tensor_tensor(
                    gT[:, fc, t0 - tb:t0 - tb + n], sil[:, 0:n], p3[:, 0:n], ALU.mult),
                    reads=[b_sil, b_p3], writes=[b_gT])
    for q in range(4):
        wt, b_wt = w2t[q % 2]
        if q >= 2:
            P.dma("pool", wt[:], w2v[:, :, q * 256:(q + 1) * 256], writes=[b_wt])
        for j in range(2):
            dc = q * 2 + j
            for (t0, n, which) in groups:
                ps, b_ps = k.next_ps()

                def mm(ps=ps, wt=wt, j=j, t0=t0, n=n):
                    inst = None
                    for fc in range(NFF):
                        inst = nc.tensor.matmul(ps[:, 0:n], wt[:, fc, j * 128:(j + 1) * 128],
                                                gT[:, fc, t0 - tb:t0 - tb + n], start=(fc == 0), stop=(fc == NFF - 1))
                    return inst
                P.op("pe", mm, reads=[b_wt, b_gT], writes=[b_ps])
                P.op("dve", lambda ps=ps, dc=dc, t0=t0, n=n, which=which: nc.vector.scalar_tensor_tensor(
                    out=xT[:, dc, t0:t0 + n], in0=ps[:, 0:n], scalar=mG[:, ni, dc, which:which + 1],
                    in1=xT[:, dc, t0:t0 + n], op0=ALU.mult, op1=ALU.add),
                    reads=[b_ps, b_mG, b_xT], writes=[b_xT])


OWN_GROUPS = [(0, 512, 0), (512, 256, 1)]
FULL_GROUPS = [(0, 512, 1), (512, 256, 1)]


def emit_ffn(k, ph, ni):
    I = k.I
    xT, b_xT = k.xT
    hh, b_hh = k.sb("hh_own%d" % ni, [128, KC, NOWN], BF16, stack=ph)
    pre = "f1" if ni == 0 else "f2"
    tiles0 = k.pre_tiles.pop(ni, None)
    if tiles0 is None:
        tiles0 = ffn_load_group(k, I[pre + "w1"], I[pre + "w3"], 0)
    norm_mod(k, ph, xT, b_xT, OWN_GROUPS, ni, hh, b_hh, "o%d" % ni)
    ffn_core(k, ph, xT, b_xT, hh, b_hh, OWN_GROUPS, I[pre + "w1"], I[pre + "w3"], I[pre + "w2"], ni,
             "o%d" % ni, tiles0=tiles0)
    if ni == 0:
        k.dbg_dump("x1", xT[:], b_xT, [128, KC, NOWN])


def emit_ffn_full(k, ph):
    I = k.I
    hh2f, b_hh2f = k.hh2f
    xf, b_xf = k.sb("xTfull", [128, KC, NFULL], stack=ph)
    xT, b_xT = k.xT

    def loads():
        yield from load_tokens_gen(k, I["xo"], NOWN, xT, b_xT, "o")
        yield from load_tokens_gen(k, I["xf"], NFULL, xf, b_xf, "f")
    run_lanes([emit_adaln(k), loads()], [1, 2])
    hh, b_hh = k.sb("hh_full", [128, KC, NFULL], BF16, stack=ph)
    norm_mod(k, ph, xf, b_xf, FULL_GROUPS, 0, hh, b_hh, "f0")
    ffn_core(k, ph, xf, b_xf, hh, b_hh, FULL_GROUPS, I["f1w1"], I["f1w3"], I["f1w2"], 0, "f")
    norm_mod(k, ph, xf, b_xf, FULL_GROUPS, 1, hh2f, b_hh2f, "f1")
    k.dbg_dump("hh2f", hh2f[:], b_hh2f, [128, KC, NFULL])


def slot_p(u, d, step):
    return step * 4 + u * 2 + d


def slot_scan(d, j):
    return 8 + j * 2 + d


def slot_s(d, step):
    return 20 + step * 2 + d


NSLOT = 24


def emit_gates(k, ph_keep, ph, hh2, b_hh2):
    nc, P, I, sb = k.nc, k.P, k.I, k.sb
    idf, b_idf = k.c["idf"]
    onesf, b_ones = k.c["ones"]
    hh2f, b_hh2f = k.hh2f
    wg, b_wg = sb("wg", [128, KC, 16], BF16, stack=ph)
    P.dma("pool", wg[:], I["w_in"].rearrange("(kc p) n -> p kc n", p=128)[:, :, O_MG:O_MG + 16], writes=[b_wg])
    bg, b_bg = sb("bgate", [128, 16], stack=ph)
    P.dma("sp", bg[:], I["b_gate"][0, :].partition_broadcast(128), writes=[b_bg])
    mu, b_mu = sb("mu_sb", [128, 24], stack=ph)
    P.dma("sp", mu[:], I["mu"].partition_broadcast(128), writes=[b_mu])
    mneg = []
    for d, src in ((0, "maskb"), (1, "maskf")):
        mt, b_mt = sb("mneg%d" % d, [128, 128], stack=ph)
        sm_, b_sm = k.c[src]
        P.op("dve", lambda mt=mt, sm_=sm_: nc.vector.tensor_scalar(mt[:], sm_[:], -1.0, 1.0e30, ALU.add, ALU.mult),
             reads=[b_sm], writes=[b_mt])
        mneg.append((mt, b_mt))
    gall, b_gall = sb("gall", [128, 12, 16], stack=ph)
    for T in range(12):
        if T < 6:
            src, b_src, t0 = hh2, b_hh2, T * 128
        else:
            src, b_src, t0 = hh2f, b_hh2f, (T - 6) * 128
        ps, b_ps = k.next_ps()

        def mm(ps=ps, src=src, t0=t0):
            inst = None
            for kc in range(KC):
                inst = nc.tensor.matmul(ps[:, 0:16], src[:, kc, t0:t0 + 128], wg[:, kc, :],
                                        start=(kc == 0), stop=(kc == KC - 1))
            return inst
        P.op("pe", mm, reads=[b_src, b_wg], writes=[b_ps])
        P.op("dve", lambda ps=ps, T=T: nc.vector.tensor_tensor(gall[:, T, :], ps[:, 0:16], bg[:], ALU.add),
             reads=[b_ps, b_bg], writes=[b_gall])
    gf, b_gf = sb("gf", [128, 12, 8], stack=ph)
    gview = gall[:].rearrange("p t (d g h) -> p t d g h", d=2, g=2)
    P.op("act", lambda: nc.scalar.activation(out=gf[:].rearrange("p t (d h) -> p t d h", d=2), in_=gview[:, :, :, 1, :],
                                             func=AF.Exp, scale=-1.0),
         reads=[b_gall], writes=[b_gf])
    P.op("act", lambda: nc.scalar.activation(out=gf[:], in_=gf[:], func=AF.Ln, bias=1.0),
         reads=[b_gf], writes=[b_gf])
    P.op("dve", lambda: nc.vector.tensor_scalar(gf[:], gf[:], -1.0, None, ALU.mult),
         reads=[b_gf], writes=[b_gf])
    R1, b_R1 = sb("R1", [128, 12, 2, 16], stack=ph)
    R2, b_R2 = sb("R2", [128, 12, 2, 16], stack=ph)
    P.op("pool", lambda: nc.gpsimd.memset(R1[:], 0.0), writes=[b_R1])
    P.op("pool", lambda: nc.gpsimd.memset(R2[:], 0.0), writes=[b_R2])
    P.op("dve", lambda: nc.vector.tensor_copy(R1[:, :, :, 0:4], gview[:, :, :, 0, :]),
         reads=[b_gall], writes=[b_R1])
    for o in (0, 4):
        P.op("dve", lambda o=o: nc.vector.tensor_copy(R2[:, :, :, o:o + 4],
                                                      gf[:].rearrange("p t (d h) -> p t d h", d=2)),
             reads=[b_gf], writes=[b_R2])
    R1s, b_R1s = sb("R1s", [128, 12, 16], stack=ph)
    R2s, b_R2s = sb("R2s", [128, 12, 16], stack=ph)
    P.op("pool", lambda: nc.gpsimd.memset(R1s[:], 0.0), writes=[b_R1s])
    P.op("pool", lambda: nc.gpsimd.memset(R2s[:], 0.0), writes=[b_R2s])
    for d in range(2):
        for j in range(6):
            ft = j if d == 0 else 5 - j
            idx = d * 6 + j
            P.op("dve", lambda d=d, ft=ft, idx=idx: nc.vector.tensor_scalar(
                R1s[:, idx, 0:4], R1[:, 6 + ft, d, 0:4], mu[:, idx:idx + 1], mu[:, 12 + idx:13 + idx],
                ALU.mult, ALU.add), reads=[b_R1, b_mu], writes=[b_R1s])
            P.op("dve", lambda d=d, ft=ft, idx=idx: nc.vector.tensor_scalar(
                R2s[:, idx, 0:8], R2[:, 6 + ft, d, 0:8], mu[:, idx:idx + 1], None, ALU.mult),
                reads=[b_R2, b_mu], writes=[b_R2s])

    ecol, b_ecol = k.gate_keep[0]
    P.op("pool", lambda: nc.gpsimd.memset(ecol[:], 0.0), writes=[b_ecol])
    dmat, b_dmat = k.gate_keep[1]
    ms, b_ms = k.gate_keep[2]
    P.op("dve", lambda: nc.vector.memset(ms[:], 0.0), writes=[b_ms])
    with nc.allow_non_contiguous_dma(reason="tiny state_m load"):
        P.dma("sp", ms[:, NSLOT + 1:NSLOT + 3], I["sm"].rearrange("d h -> h d"), writes=[b_ms])
    i4, b_i4 = sb("i4rep", [4, 16], stack=ph)
    for o in (0, 4, 8, 12):
        P.op("dve", lambda o=o: nc.vector.tensor_copy(i4[:, o:o + 4], idf[0:4, 0:4]), reads=[b_idf], writes=[b_i4])
    def make_lane(tag, banks):
        L = {}
        for nm, shape in (("gs", [4, 8, 3]), ("dg", [4, 3, 16]), ("cdiag", [4, 4, 128]), ("ldm", [128, 4, 128]),
                          ("nbm", [128, 8]), ("uu", [128, 4]), ("t8", [128, 8])):
            L[nm] = sb("g_%s_%s" % (nm, tag), shape, stack=ph)
        dg_, b_dg_ = L["dg"]
        P.op("dve", lambda: nc.vector.memset(dg_[:], 0.0), writes=[b_dg_])
        L["prow"], L["pcol"], L["cb"] = (k.ps[b] for b in banks)
        return L

    k.ecol, k.ms, k.dmat = (ecol, b_ecol), (ms, b_ms), (dmat, b_dmat)

    def gate_batch(insts, L):
        n = len(insts)
        gs, b_gs = L["gs"]
        dg, b_dg = L["dg"]
        cdiag, b_cdiag = L["cdiag"]
        ldm, b_ldm = L["ldm"]
        nbm, b_nbm = L["nbm"]
        uu, b_uu = L["uu"]
        t8, b_t8 = L["t8"]
        prow, b_prow = L["prow"]
        for i, (r1, r2, rb, d, mcol, slot, oi) in enumerate(insts):
            ntri, b_ntri = k.c["ntrif" if d == 0 else "ntrib"]

            def mm(i=i, r1=r1, r2=r2, ntri=ntri):
                nc.tensor.matmul(prow[0:4, i * 128:(i + 1) * 128], r1[:, 0:4], idf[:], start=True, stop=False)
                nc.tensor.matmul(prow[0:4, i * 128:(i + 1) * 128], r2[:, 0:4], ntri[:], start=False, stop=True)
                return nc.tensor.matmul(prow[0:4, 384 + i:385 + i], r2[:, 0:4], onesf[:, 0:1], start=True, stop=True)
            yield P.op("pe", mm, reads=list(rb) + [b_idf, b_ntri, b_ones], writes=[b_prow])
        yield P.op("dve", lambda: nc.vector.tensor_reduce(gs[:, 0:n, 0], prow[0:4, 0:n * 128].rearrange("p (i t) -> p i t", i=n),
                                                    AX.X, ALU.max), reads=[b_prow], writes=[b_gs])
        mcol0, slot0 = insts[0][4], insts[0][5]
        allz = all(x[4] == NSLOT for x in insts)
        assert allz or all(x[4] == mcol0 + i for i, x in enumerate(insts)), insts
        assert all(x[5] == slot0 + i for i, x in enumerate(insts))
        msin = ms[:, NSLOT:NSLOT + 1].to_broadcast([4, n]) if allz else ms[:, mcol0:mcol0 + n]
        msin3 = (ms[:, NSLOT:NSLOT + 1] if allz else ms[:, mcol0:mcol0 + n]).unsqueeze(2).to_broadcast([4, n, 4])
        i4b = i4[:, 0:4].unsqueeze(1).to_broadcast([4, n, 4])
        yield P.op("dve", lambda: nc.vector.tensor_tensor(gs[:, 0:n, 0], gs[:, 0:n, 0], msin, ALU.max), reads=[b_gs, b_ms], writes=[b_gs])
        yield P.op("dve", lambda: nc.vector.tensor_tensor(ms[:, slot0:slot0 + n], prow[0:4, 384:384 + n], gs[:, 0:n, 0], ALU.add),
                   reads=[b_prow, b_gs, b_ms], writes=[b_ms])
        yield P.op("dve", lambda: nc.vector.tensor_scalar(gs[:, 0:n, 1], gs[:, 0:n, 0], -1.0, None, ALU.mult), reads=[b_gs], writes=[b_gs])
        yield P.op("dve", lambda: nc.vector.tensor_tensor(gs[:, 0:n, 2], msin, gs[:, 0:n, 0], ALU.subtract), reads=[b_gs, b_ms], writes=[b_gs])
        yield P.op("dve", lambda: nc.vector.tensor_tensor(dg[:, 0:n, 0:4], i4b, gs[:, 0:n, 1:2].to_broadcast([4, n, 4]), ALU.mult),
                   reads=[b_gs, b_i4], writes=[b_dg])
        yield P.op("dve", lambda: nc.vector.tensor_tensor(dg[:, 0:n, 8:12], i4b, msin3, ALU.mult), reads=[b_ms, b_i4], writes=[b_dg])
        yield P.op("dve", lambda: nc.vector.tensor_tensor(dg[:, 0:n, 12:16], i4b, gs[:, 0:n, 2:3].to_broadcast([4, n, 4]), ALU.mult),
                   reads=[b_gs, b_i4], writes=[b_dg])
        pcol, b_pcol = L["pcol"]
        for i, (r1, r2, rb, d, mcol, slot, oi) in enumerate(insts):
            ntri, b_ntri = k.c["ntrif" if d == 0 else "ntrib"]

            def mm2(i=i, r1=r1, r2=r2, ntri=ntri):
                nc.tensor.matmul(pcol[:, i * 16:(i + 1) * 16], idf[:], r1, start=True, stop=False)
                nc.tensor.matmul(pcol[:, i * 16:(i + 1) * 16], ntri[:], r2, start=False, stop=False)
                return nc.tensor.matmul(pcol[:, i * 16:(i + 1) * 16], onesf[0:4, :], dg[:, i, :], start=False, stop=True)
            yield P.op("pe", mm2, reads=list(rb) + [b_idf, b_ntri, b_ones, b_dg], writes=[b_pcol])
        for i, (r1, r2, rb, d, mcol, slot, oi) in enumerate(insts):
            yield P.op("act", lambda i=i, slot=slot: nc.scalar.activation(out=ecol[:, slot, 0:4], in_=pcol[:, i * 16:i * 16 + 4],
                                                                    func=AF.Exp), reads=[b_pcol], writes=[b_ecol])
            yield P.op("act", lambda i=i, slot=slot: nc.scalar.activation(out=ecol[:, slot, 4:8], in_=pcol[:, i * 16 + 12:i * 16 + 16],
                                                                    func=AF.Exp), reads=[b_pcol], writes=[b_ecol])
            if oi is None:
                continue
            yield P.op("dve", lambda i=i: nc.vector.tensor_copy(nbm[:], pcol[:, i * 16 + 4:i * 16 + 12]), reads=[b_pcol], writes=[b_nbm])
            yield P.op("dve", lambda i=i: nc.vector.tensor_tensor(
                cdiag[:], prow[0:4, i * 128:(i + 1) * 128].unsqueeze(1).to_broadcast([4, 4, 128]),
                i4[:, 0:4].unsqueeze(2).to_broadcast([4, 4, 128]), ALU.mult), reads=[b_prow, b_i4], writes=[b_cdiag])
            cb, b_cb = L["cb"]
            yield P.op("pe", lambda cb=cb: nc.tensor.matmul(cb[:, 0:512], onesf[0:4, :], cdiag[:].rearrange("p m s -> p (m s)"),
                                                      start=True, stop=True), reads=[b_ones, b_cdiag], writes=[b_cb])
            mt, b_mt = mneg[d]
            yield P.op("dve", lambda cb=cb, mt=mt: nc.vector.tensor_tensor(
                ldm[:], cb[:, 0:512].rearrange("p (m s) -> p m s", m=4), mt[:].unsqueeze(1).to_broadcast([128, 4, 128]), ALU.add),
                reads=[b_cb, b_mt], writes=[b_ldm])
            yield P.op("dve", lambda: nc.vector.tensor_reduce(uu[:], ldm[:], AX.X, ALU.max), reads=[b_ldm], writes=[b_uu])
            yield P.op("dve", lambda: nc.vector.tensor_tensor(uu[:], uu[:], nbm[:, 4:8], ALU.max), reads=[b_uu, b_nbm], writes=[b_uu])
            yield P.op("dve", lambda: nc.vector.tensor_tensor(ldm[:], ldm[:], uu[:].unsqueeze(2).to_broadcast([128, 4, 128]), ALU.subtract),
                 reads=[b_ldm, b_uu], writes=[b_ldm])
            yield P.op("act", lambda oi=oi: nc.scalar.activation(out=dmat[:, oi], in_=ldm[:], func=AF.Exp), reads=[b_ldm], writes=[b_dmat])
            yield P.op("dve", lambda: nc.vector.tensor_tensor(t8[:, 0:4], nbm[:, 0:4], uu[:], ALU.subtract), reads=[b_nbm, b_uu], writes=[b_t8])
            yield P.op("dve", lambda: nc.vector.tensor_tensor(t8[:, 4:8], nbm[:, 4:8], uu[:], ALU.subtract), reads=[b_nbm, b_uu], writes=[b_t8])
            yield P.op("act", lambda slot=slot: nc.scalar.activation(out=ecol[:, slot, 8:16], in_=t8[:], func=AF.Exp),
                 reads=[b_t8], writes=[b_ecol])

    def own_inst(T, d, mcol, slot):
        return (R1[:, T, d, :], R2[:, T, d, :], (b_R1, b_R2), d, mcol, slot, own_index(slot))

    def scan_inst(d, j, mcol):
        idx = d * 6 + j
        return (R1s[:, idx, :], R2s[:, idx, :], (b_R1s, b_R2s), d, mcol, slot_scan(d, j), None)

    Z = NSLOT
    LP, LS = make_lane("p", (0, 1, 2)), make_lane("s", (3, 4, 5))

    def lane_prompt():
        yield from gate_batch([own_inst(0, 0, Z, slot_p(0, 0, 0)), own_inst(1, 1, Z, slot_p(0, 1, 0)),
                               own_inst(2, 0, Z, slot_p(1, 0, 0))], LP)
        yield from gate_batch([own_inst(3, 1, Z, slot_p(1, 1, 0))], LP)
        yield from gate_batch([own_inst(1, 0, slot_p(0, 0, 0), slot_p(0, 0, 1)), own_inst(0, 1, slot_p(0, 1, 0), slot_p(0, 1, 1)),
                               own_inst(3, 0, slot_p(1, 0, 0), slot_p(1, 0, 1))], LP)
        yield from gate_batch([own_inst(2, 1, slot_p(1, 1, 0), slot_p(1, 1, 1))], LP)

    def lane_sample():
        yield from gate_batch([scan_inst(0, 0, NSLOT + 1), scan_inst(1, 0, NSLOT + 2)], LS)
        for j in range(1, 6):
            yield from gate_batch([scan_inst(0, j, slot_scan(0, j - 1)), scan_inst(1, j, slot_scan(1, j - 1))], LS)
        yield from gate_batch([own_inst(4, 0, slot_scan(0, 5), slot_s(0, 0)), own_inst(5, 1, slot_scan(1, 5), slot_s(1, 0))], LS)
        yield from gate_batch([own_inst(5, 0, slot_s(0, 0), slot_s(0, 1)), own_inst(4, 1, slot_s(1, 0), slot_s(1, 1))], LS)

    run_lanes([lane_prompt(), lane_sample()], [1, 1])
    with nc.allow_non_contiguous_dma(reason="tiny state_m store"):
        for u in range(2):
            for d in range(2):
                sl = slot_p(u, d, 1)
                P.dma("sp", k.O["nm"][u, d, :].unsqueeze(1), ms[:, sl:sl + 1], reads=[b_ms], is_output=True)
    k.dbg_dump("ecol", ecol[:], b_ecol, [128, NSLOT, 16])
    k.dbg_dump("ms", ms[:], b_ms, [4, NSLOT + 3])
    k.dbg_dump("gall", gall[:], b_gall, [128, 12, 16])


def own_index(slot):
    return slot if slot < 8 else slot - 12


class StopEmit(Exception):
    pass


def emit_mlstm(k, ph, hh2, b_hh2, hmT, b_hmT):
    nc, P, I, O, sb = k.nc, k.P, k.I, k.O, k.sb
    idb, b_idb = k.c["idb"]
    epst, b_eps = k.c["eps"]
    hh2f, b_hh2f = k.hh2f
    ecol, b_ecol = k.ecol
    dmat, b_dmat = k.dmat
    win = I["w_in"].rearrange("(kc p) n -> p kc n", p=128)
    gmh, b_gmh = k.stage[0]
    P.dma("sp", gmh[:], I["g_mh"].partition_broadcast(128), writes=[b_gmh])

    def bufset(tag, sample):
        B = {}
        for nm, shape, dt in (("qT", [128, 2, 256], BF16), ("kT", [128, 2, 256], BF16), ("ktok", [128, 2, 256], BF16),
                              ("vb", [128, 2, 272], BF16), ("sgo", [128, 2, 256], BF16),
                              ("cst0", [128, 2, 272], F32), ("cst1", [128, 2, 272], F32), ("csb", [128, 2, 272], BF16),
                              ("vaug0", [128, 272], BF16), ("vaug1", [128, 272], BF16),
                              ("sbf", [128, 128], BF16), ("sT", [128, 128], BF16),
                              ("x2s", [128, 257], F32), ("nums", [128, 272], F32), ("hm", [128, 2, 256], F32),
                              ("hb", [128, 256], BF16), ("sc1", [128, 4], F32),
                              ("kf", [128, 8, 256], BF16), ("vf", [128, 8, 272], BF16)):
            if not sample and nm in ("kf", "vf"):
                continue
            B[nm] = sb("m_%s_%s" % (nm, tag), shape, dt, stack=ph)
        if not sample:
            for nm, shape, dt in (("sbf1", [128, 128], BF16), ("sT1", [128, 128], BF16), ("csb1", [128, 2, 272], BF16),
                                  ("x2s1", [128, 257], F32), ("nums1", [128, 272], F32), ("sc11", [128, 4], F32)):
                B[nm] = sb("m_%s_%s" % (nm, tag), shape, dt, stack=ph)
        B["va_i"] = 0
        if sample:
            for nm in ("y_sbf", "y_sT", "y_csb", "y_x2s", "y_nums", "y_sc1"):
                B[nm] = Buf(nm)
        vb_, b_vb_ = B["vb"]
        P.op("pool", lambda: nc.gpsimd.memset(vb_[:], 1.0), writes=[b_vb_])
        if sample:
            vf_, b_vf_ = B["vf"]
            P.op("pool", lambda: nc.gpsimd.memset(vf_[:], 1.0), writes=[b_vf_])
        return B

    def mlstm_iter(h, u, B, wmA, b_wmA, wmB, b_wmB):
        qT, b_qT = B["qT"]
        kT, b_kT = B["kT"]
        ktok, b_ktok = B["ktok"]
        vb, b_vb = B["vb"]
        sgo, b_sgo = B["sgo"]
        cst = [B["cst0"], B["cst1"]]
        csb, b_csb = B["csb"]
        sbf, b_sbf = B["sbf"]
        sT, b_sT = B["sT"]
        x2s, b_x2s = B["x2s"]
        nums, b_nums = B["nums"]
        hm, b_hm = B["hm"]
        tmp_, b_tmp = B["nums"]
        tmp = tmp_[:, 0:256]
        hb, b_hb = B["hb"]
        sc1, b_sc1 = B["sc1"]
        tok0 = u * 256

        def build_vaug(vsrc, b_vsrc, slot):
            va, b_va = B["vaug%d" % (B["va_i"] % 2)]
            B["va_i"] += 1
            P.op("dve", lambda: nc.vector.tensor_scalar(va[:, 0:257], vsrc, ecol[:, slot, h:h + 1], None, ALU.mult),
                 reads=[b_vsrc, b_ecol], writes=[b_va])
            return va, b_va

        def state_update(d, ksrc_fn, b_ksrc, va, b_va, slot, has_state):
            C, b_C = cst[d]
            for dc in range(2):
                ps, b_ps = k.next_ps()
                P.op("pe", lambda ps=ps, dc=dc: nc.tensor.matmul(ps[:, 0:257], ksrc_fn(dc), va[:, 0:257], start=True, stop=True),
                     reads=[b_ksrc, b_va], writes=[b_ps])
                if has_state:
                    P.op("dve", lambda ps=ps, dc=dc: nc.vector.scalar_tensor_tensor(
                        out=C[:, dc, 0:257], in0=C[:, dc, 0:257], scalar=ecol[:, slot, 4 + h:5 + h], in1=ps[:, 0:257],
                        op0=ALU.mult, op1=ALU.add), reads=[b_C, b_ecol, b_ps], writes=[b_C])
                else:
                    P.op("dve", lambda ps=ps, dc=dc: nc.vector.tensor_copy(C[:, dc, 0:257], ps[:, 0:257]),
                         reads=[b_ps], writes=[b_C])
                yield None

        for (dst, b_dst, wsrc, b_wsrc, scl) in ((qT, b_qT, wmB, b_wmB, 1.0 / 16.0), (kT, b_kT, wmA, b_wmA, 1.0)):
            for dc in range(2):
                ps, b_ps = k.next_ps()

                def mm(ps=ps, wsrc=wsrc, dc=dc):
                    inst = None
                    for kc in range(KC):
                        inst = nc.tensor.matmul(ps[:, 0:256], wsrc[:, 0, kc, dc * 128:(dc + 1) * 128],
                                                hh2[:, kc, tok0:tok0 + 256], start=(kc == 0), stop=(kc == KC - 1))
                    return inst
                P.op("pe", mm, reads=[b_wsrc, b_hh2], writes=[b_ps])
                yield P.op("act", lambda ps=ps, dst=dst, dc=dc, scl=scl: nc.scalar.activation(
                    out=dst[:, dc, :], in_=ps[:, 0:256], func=AF.Copy, scale=scl), reads=[b_ps], writes=[b_dst])
        for t in range(2):
            ps, b_ps = k.next_ps()

            def mm(ps=ps, t=t):
                inst = None
                for kc in range(KC):
                    inst = nc.tensor.matmul(ps[:, 0:512].rearrange("p (j n) -> p j n", j=2),
                                            hh2[:, kc, tok0 + t * 128:tok0 + (t + 1) * 128],
                                            wmA[:, :, kc, :], start=(kc == 0), stop=(kc == KC - 1))
                return inst
            P.op("pe", mm, reads=[b_wmA, b_hh2], writes=[b_ps])
            P.op("act", lambda ps=ps, t=t: nc.scalar.copy(ktok[:, t, :], ps[:, 0:256]), reads=[b_ps], writes=[b_ktok])
            yield P.op("dve", lambda ps=ps, t=t: nc.vector.tensor_copy(vb[:, t, 0:256], ps[:, 256:512]), reads=[b_ps], writes=[b_vb])
            ps2, b_ps2 = k.next_ps()

            def mm2(ps2=ps2, t=t):
                inst = None
                for kc in range(KC):
                    inst = nc.tensor.matmul(ps2[:, 0:256], hh2[:, kc, tok0 + t * 128:tok0 + (t + 1) * 128],
                                            wmB[:, 1, kc, :], start=(kc == 0), stop=(kc == KC - 1))
                return inst
            P.op("pe", mm2, reads=[b_wmB, b_hh2], writes=[b_ps2])
            yield P.op("act", lambda ps2=ps2, t=t: nc.scalar.activation(out=sgo[:, t, :], in_=ps2[:, 0:256], func=AF.Sigmoid),
                       reads=[b_ps2], writes=[b_sgo])
        has_state = [False, False]
        if u == 2:
            kf, b_kf = B["kf"]
            vf, b_vf = B["vf"]
            for ft in range(6):
                ps, b_ps = k.next_ps()

                def mm(ps=ps, ft=ft):
                    inst = None
                    for kc in range(KC):
                        inst = nc.tensor.matmul(ps[:, 0:512].rearrange("p (j n) -> p j n", j=2),
                                                hh2f[:, kc, ft * 128:(ft + 1) * 128],
                                                wmA[:, :, kc, :], start=(kc == 0), stop=(kc == KC - 1))
                    return inst
                P.op("pe", mm, reads=[b_wmA, b_hh2f], writes=[b_ps])
                P.op("act", lambda ps=ps, ft=ft: nc.scalar.copy(kf[:, ft, :], ps[:, 0:256]), reads=[b_ps], writes=[b_kf])
                yield P.op("dve", lambda ps=ps, ft=ft: nc.vector.tensor_copy(vf[:, ft, 0:256], ps[:, 256:512]),
                           reads=[b_ps], writes=[b_vf])
            for d in range(2):
                C, b_C = cst[d]
                P.dma("sp", C[:, :, 0:256], I["sC"][d, h].rearrange("(dc p) v -> p dc v", p=128), writes=[b_C])
                with nc.allow_non_contiguous_dma(reason="state_n column"):
                    P.dma("sp", C[:, :, 256:257], I["sn"][d, h].rearrange("(dc p one) -> p dc one", p=128, one=1),
                          writes=[b_C])
                has_state[d] = True
            for j in range(6):
                for d in range(2):
                    ft = j if d == 0 else 5 - j
                    sl = slot_scan(d, j)
                    va, b_va = build_vaug(vf[:, ft, 0:257], b_vf, sl)
                    yield None
                    yield from state_update(d, lambda dc, ft=ft: kf[:, ft, dc * 128:(dc + 1) * 128], b_kf, va, b_va, sl, True)
        P.op("pool", lambda: nc.gpsimd.memset(hm[:], 0.0), writes=[b_hm])

        def step_ops(d, step, X):
            sbf, b_sbf = X["sbf"]
            sT, b_sT = X["sT"]
            csb, b_csb = X["csb"]
            x2s, b_x2s = X["x2s"]
            nums, b_nums = X["nums"]
            sc1, b_sc1 = X["sc1"]
            C, b_C = cst[d]
            c = step if d == 0 else 1 - step
            sl = slot_p(u, d, step) if u < 2 else slot_s(d, step)
            oi = own_index(sl)
            pq, b_pq = k.next_ps()

            def mmqk():
                nc.tensor.matmul(pq[:, 0:128], qT[:, 0, c * 128:(c + 1) * 128], kT[:, 0, c * 128:(c + 1) * 128],
                                 start=True, stop=False)
                return nc.tensor.matmul(pq[:, 0:128], qT[:, 1, c * 128:(c + 1) * 128], kT[:, 1, c * 128:(c + 1) * 128],
                                        start=False, stop=True)
            P.op("pe", mmqk, reads=[b_kT, b_qT], writes=[b_pq])
            yield P.op("dve", lambda: nc.vector.tensor_tensor(sbf[:], pq[:, 0:128], dmat[:, oi, h, :], ALU.mult),
                       reads=[b_pq, b_dmat], writes=[b_sbf])
            pt, b_pt = k.next_pst()
            P.op("pe", lambda: nc.tensor.transpose(pt[:, 0:128], sbf[:], idb[:]), reads=[b_sbf, b_idb], writes=[b_pt])
            yield P.op("act", lambda: nc.scalar.copy(sT[:], pt[:, 0:128]), reads=[b_pt], writes=[b_sT])
            hs = has_state[d]
            if hs:
                yield P.op("act", lambda: nc.scalar.copy(csb[:, :, 0:257], C[:, :, 0:257]), reads=[b_C], writes=[b_csb])
                p2, b_p2 = k.next_ps()

                def mm2():
                    nc.tensor.matmul(p2[:, 0:257], qT[:, 0, c * 128:(c + 1) * 128], csb[:, 0, 0:257], start=True, stop=False)
                    return nc.tensor.matmul(p2[:, 0:257], qT[:, 1, c * 128:(c + 1) * 128], csb[:, 1, 0:257],
                                            start=False, stop=True)
                P.op("pe", mm2, reads=[b_qT, b_csb], writes=[b_p2])
                yield P.op("dve", lambda: nc.vector.tensor_scalar(x2s[:], p2[:, 0:257], ecol[:, sl, 12 + h:13 + h], None, ALU.mult),
                           reads=[b_p2, b_ecol], writes=[b_x2s])
            px, b_px = k.next_ps()
            P.op("pe", lambda: nc.tensor.matmul(px[:, 0:257], sT[:], vb[:, c, 0:257], start=True, stop=True),
                 reads=[b_sT, b_vb], writes=[b_px])
            if hs:
                yield P.op("dve", lambda: nc.vector.tensor_tensor(nums[:, 0:257], px[:, 0:257], x2s[:, 0:257], ALU.add),
                           reads=[b_px, b_x2s], writes=[b_nums])
            else:
                yield P.op("act", lambda: nc.scalar.copy(nums[:, 0:257], px[:, 0:257]), reads=[b_px], writes=[b_nums])
            yield P.op("dve", lambda: nc.vector.tensor_tensor(sc1[:, 0:1], nums[:, 256:257], ecol[:, sl, 8 + h:9 + h], ALU.max),
                       reads=[b_nums, b_ecol], writes=[b_sc1])
            yield P.op("dve", lambda: nc.vector.scalar_tensor_tensor(out=sc1[:, 0:1], in0=nums[:, 256:257], scalar=-1.0, in1=sc1[:, 0:1],
                                                                     op0=ALU.mult, op1=ALU.max),
                       reads=[b_nums, b_sc1], writes=[b_sc1])
            yield P.op("dve", lambda: nc.vector.reciprocal(sc1[:, 0:1], sc1[:, 0:1]), reads=[b_sc1], writes=[b_sc1])
            yield P.op("dve", lambda: nc.vector.scalar_tensor_tensor(
                out=hm[:, c, :], in0=nums[:, 0:256], scalar=sc1[:, 0:1], in1=hm[:, c, :], op0=ALU.mult, op1=ALU.add),
                reads=[b_nums, b_sc1, b_hm], writes=[b_hm])
            if step == 0 or u < 2:
                va, b_va = B["vaug%d" % d]
                P.op("dve", lambda: nc.vector.tensor_scalar(va[:, 0:257], vb[:, c, 0:257], ecol[:, sl, h:h + 1], None, ALU.mult),
                     reads=[b_vb, b_ecol], writes=[b_va])
                yield None
                yield from state_update(d, lambda dc: ktok[:, c, dc * 128:(dc + 1) * 128], b_ktok, va, b_va, sl, hs)
                has_state[d] = True

        Xs = dict(sbf=B["sbf"], sT=B["sT"], csb=B["csb"], x2s=B["x2s"], nums=B["nums"], sc1=B["sc1"])
        if u < 2:
            X1 = dict(sbf=B["sbf1"], sT=B["sT1"], csb=B["csb1"], x2s=B["x2s1"], nums=B["nums1"], sc1=B["sc11"])
        else:
            (n_sq0, _), (n_sq1, _) = k.nm[0]
            n_rs, _ = k.nm[1]
            (n_t0, _), (n_t1, _) = k.nm[2]
            t0b = n_t0[:].bitcast(BF16)
            X1 = dict(sbf=(t0b[:, 0:128], B["y_sbf"]), sT=(t0b[:, 128:256], B["y_sT"]),
                      csb=(n_rs[:].bitcast(BF16)[:, 0:544].rearrange("p (c n) -> p c n", c=2), B["y_csb"]),
                      x2s=(n_sq1[:, 0:257], B["y_x2s"]), nums=(n_sq0[:, 0:272], B["y_nums"]), sc1=(n_t1[:, 0:4], B["y_sc1"]))
        if True:

            def dir_stream(d, X):
                for step in range(2):
                    yield from step_ops(d, step, X)
            streams = [dir_stream(0, Xs), dir_stream(1, X1)]
            while streams:
                alive = []
                for g in streams:
                    try:
                        yield next(g)
                        alive.append(g)
                    except StopIteration:
                        pass
                streams = alive
        if u < 2:
            for d in range(2):
                C, b_C = cst[d]
                P.dma("sp", O["nC"][u, d, h].rearrange("(dc p) v -> p dc v", p=128), C[:, :, 0:256], reads=[b_C], is_output=True)
                with nc.allow_non_contiguous_dma(reason="state_n column"):
                    P.dma("sp", O["nn"][u, d, h].rearrange("(dc p one) -> p dc one", p=128, one=1), C[:, :, 256:257],
                          reads=[b_C], is_output=True)
        for c in range(2):
            yield P.op("act", lambda c=c: nc.scalar.activation(out=tmp, in_=hm[:, c, :], func=AF.Square), reads=[b_hm], writes=[b_tmp])
            yield P.op("dve", lambda: nc.vector.tensor_reduce(sc1[:, 1:2], tmp, AX.X, ALU.add), reads=[b_tmp], writes=[b_sc1])
            yield P.op("act", lambda: nc.scalar.activation(out=sc1[:, 1:2], in_=sc1[:, 1:2], func=AF.Sqrt, scale=1.0 / 256.0,
                                                           bias=epst[:, 0:1]), reads=[b_sc1, b_eps], writes=[b_sc1])
            yield P.op("dve", lambda: nc.vector.reciprocal(sc1[:, 1:2], sc1[:, 1:2]), reads=[b_sc1], writes=[b_sc1])
            yield P.op("dve", lambda c=c: nc.vector.scalar_tensor_tensor(
                out=tmp, in0=hm[:, c, :], scalar=sc1[:, 1:2], in1=gmh[:, h * 256:(h + 1) * 256], op0=ALU.mult, op1=ALU.mult),
                reads=[b_hm, b_sc1, b_gmh], writes=[b_tmp])
            yield P.op("dve", lambda c=c: nc.vector.tensor_tensor(hb[:], tmp, sgo[:, c, :], ALU.mult),
                       reads=[b_tmp, b_sgo], writes=[b_hb])
            for dc in range(2):
                pt, b_pt = k.next_pst()
                P.op("pe", lambda pt=pt, dc=dc: nc.tensor.transpose(pt[:, 0:128], hb[:, dc * 128:(dc + 1) * 128], idb[:]),
                     reads=[b_hb, b_idb], writes=[b_pt])
                yield P.op("act", lambda pt=pt, dc=dc, c=c: nc.scalar.copy(
                    hmT[:, 2 * h + dc, tok0 + c * 128:tok0 + (c + 1) * 128], pt[:, 0:128]), reads=[b_pt], writes=[b_hmT])

    BP0, BP1, BS = bufset("p0", False), bufset("p1", False), bufset("s", True)
    for h in range(4):
        (rA, b_wmA), (rB, b_wmB) = k.ring[(h % 2) * 2], k.ring[(h % 2) * 2 + 1]
        wmA = rA[:].rearrange("p (j kc n) -> p j kc n", j=2, kc=KC)
        wmB = rB[:].rearrange("p (j kc n) -> p j kc n", j=2, kc=KC)
        for (wt, b_wt, j, off) in ((wmA, b_wmA, 0, O_MK), (wmA, b_wmA, 1, O_MV), (wmB, b_wmB, 0, O_MQ), (wmB, b_wmB, 1, O_MO)):
            if h == 0 and getattr(k, "mlstm_w0_loaded", False):
                break
            P.dma("pool", wt[:, j], win[:, :, off + h * 256:off + (h + 1) * 256], writes=[b_wt])
        run_lanes([mlstm_iter(h, 0, BP0, wmA, b_wmA, wmB, b_wmB), mlstm_iter(h, 1, BP1, wmA, b_wmA, wmB, b_wmB),
                   mlstm_iter(h, 2, BS, wmA, b_wmA, wmB, b_wmB)], [2, 2, 3])


def run_lanes(lanes, weights):
    active = list(zip(lanes, weights))
    while active:
        nxt = []
        for g, w in active:
            alive = True
            for _ in range(w):
                try:
                    next(g)
                except StopIteration:
                    alive = False
                    break
            if alive:
                nxt.append((g, w))
        active = nxt


def rope_apply(k, dst, src, tab, nt, bufs_r, bufs_w, t1, b_t1, t2, b_t2):
    nc, P = k.nc, k.P
    sv = src.rearrange("p t (q h f) -> p t q h f", h=2, f=16)
    dv = dst.rearrange("p t (q h f) -> p t q h f", h=2, f=16)
    cos = tab[:, :, 0:64].rearrange("p t (q f) -> p t q f", f=16)
    sin = tab[:, :, 64:128].rearrange("p t (q f) -> p t q f", f=16)
    x1, x2 = sv[:, :, :, 0, :], sv[:, :, :, 1, :]
    o1, o2 = dv[:, :, :, 0, :], dv[:, :, :, 1, :]
    a = t1[:, 0:nt * 64].rearrange("p (t q f) -> p t q f", t=nt, f=16)
    b = t2[:, 0:nt * 64].rearrange("p (t q f) -> p t q f", t=nt, f=16)
    a2 = t1[:, nt * 64:nt * 128].rearrange("p (t q f) -> p t q f", t=nt, f=16)
    b2 = t2[:, nt * 64:nt * 128].rearrange("p (t q f) -> p t q f", t=nt, f=16)
    yield P.op("dve", lambda: nc.vector.tensor_tensor(a, x1, cos, ALU.mult), reads=bufs_r, writes=[b_t1])
    yield P.op("dve", lambda: nc.vector.tensor_tensor(b, x2, sin, ALU.mult), reads=bufs_r, writes=[b_t2])
    yield P.op("dve", lambda: nc.vector.tensor_tensor(a2, x2, cos, ALU.mult), reads=bufs_r, writes=[b_t1])
    yield P.op("dve", lambda: nc.vector.tensor_tensor(b2, x1, sin, ALU.mult), reads=bufs_r, writes=[b_t2])
    yield P.op("dve", lambda: nc.vector.tensor_tensor(o1, a, b, ALU.subtract), reads=[b_t1, b_t2], writes=bufs_w)
    yield P.op("dve", lambda: nc.vector.tensor_tensor(o2, a2, b2, ALU.add), reads=[b_t1, b_t2], writes=bufs_w)


def emit_attn(k, ph, hh2, b_hh2, attT, b_attT):
    nc, P, I, O, sb = k.nc, k.P, k.I, k.O, k.sb
    idb, b_idb = k.c["idb"]
    epst, b_eps = k.c["eps"]
    hh2f, b_hh2f = k.hh2f
    win = I["w_in"].rearrange("(kc p) n -> p kc n", p=128)
    was = [sb("wa%d" % i, [128, KC, 384], BF16, stack=ph) for i in range(2)]
    gqk, b_gqk = sb("gqk", [128, 256], stack=ph)
    for j, nm in enumerate(("g_qn", "g_qn", "g_kn", "g_kn")):
        P.dma("sp", gqk[:, j * 64:(j + 1) * 64], I[nm].partition_broadcast(128), writes=[b_gqk])
    gsub, b_gsub = sb("gsub", [128, 128], stack=ph)
    P.dma("sp", gsub[:], I["g_sub"].partition_broadcast(128), writes=[b_gsub])
    P.op("dve", lambda: nc.vector.tensor_scalar(gsub[:], gsub[:], 1.0 - LAM_INIT, None, ALU.mult), reads=[b_gsub], writes=[b_gsub])
    lamt, b_lamt = sb("lamt", [128, 256], stack=ph)
    P.dma("sp", lamt[:], I["lamv"].rearrange("a b -> (a b)").partition_broadcast(128), writes=[b_lamt])
    lsc, b_lsc = sb("lsc", [128, 4], stack=ph)
    lpr, b_lpr = sb("lpr", [128, 128], stack=ph)
    for i in range(2):
        P.op("dve", lambda i=i: nc.vector.tensor_tensor(lpr[:, i * 64:(i + 1) * 64], lamt[:, i * 128:i * 128 + 64],
                                                        lamt[:, i * 128 + 64:i * 128 + 128], ALU.mult),
             reads=[b_lamt], writes=[b_lpr])
    P.op("dve", lambda: nc.vector.tensor_reduce(lsc[:, 0:2], lpr[:].rearrange("p (a b) -> p a b", a=2), AX.X, ALU.add),
         reads=[b_lpr], writes=[b_lsc])
    P.op("act", lambda: nc.scalar.activation(out=lsc[:, 0:2], in_=lsc[:, 0:2], func=AF.Exp), reads=[b_lsc], writes=[b_lsc])
    P.op("dve", lambda: nc.vector.tensor_tensor(lsc[:, 2:3], lsc[:, 1:2], lsc[:, 0:1], ALU.subtract), reads=[b_lsc], writes=[b_lsc])
    P.op("dve", lambda: nc.vector.tensor_scalar(lsc[:, 2:3], lsc[:, 2:3], -LAM_INIT, None, ALU.add), reads=[b_lsc], writes=[b_lsc])
    ropo, b_ropo = sb("rope_o", [128, 2, 128], stack=ph)
    P.dma("sp", ropo[:], I["rope_own"].rearrange("(t p) c -> p t c", p=128), writes=[b_ropo])
    ropf, b_ropf = sb("rope_f", [128, 8, 128], stack=ph)
    P.dma("sp", ropf[:], I["rope_full"].rearrange("(t p) c -> p t c", p=128), writes=[b_ropf])

    def bufset(tag, sample):
        B = {}
        nkt = 12 if sample else 2
        spec = [("ssc", [128, 16], F32), ("qT", [64, 2, 256], BF16), ("kT", [64, 2, nkt * 128], BF16),
                ("vaug", [128, nkt, 144], BF16), ("osb", [128, 256], F32), ("ot2", [128, 256], F32), ("ab", [128, 256], BF16)]
        if sample:
            spec += [("qs", [128, 2, 128], F32), ("qr", [128, 2, 128], F32), ("qb", [128, 2, 128], BF16),
                     ("ck4", [128, 4, 128], F32), ("cv4", [128, 4, 128], F32), ("kb4", [128, 4, 128], BF16),
                     ("kb4b", [128, 4, 128], BF16), ("kbq", [128, 4, 128], BF16), ("ssc2", [128, 16], F32), ("sscq", [128, 16], F32)]
        else:
            spec += [("qkv", [128, 2, 384], F32), ("sq", [128, 2, 256], F32), ("qkn", [128, 2, 256], F32),
                     ("qkb", [128, 2, 256], BF16), ("E", [128, 2, 512], BF16)]
        for nm, shape, dt in spec:
            B[nm] = sb("a_%s_%s" % (nm, tag), shape, dt, stack=ph)
        va, b_va = B["vaug"]
        P.op("pool", lambda: nc.gpsimd.memset(va[:], 1.0), writes=[b_va])
        if sample:
            for nm in ("x_r2a", "x_r2b", "x_r2c", "x_r2d", "x_r3a", "x_r3b", "x_r3c"):
                B[nm] = Buf(nm)
        return B

    def rstd_inplace(ssc, b_ssc, n, inv):
        yield P.op("act", lambda: nc.scalar.activation(out=ssc[:, 0:n], in_=ssc[:, 0:n], func=AF.Ln, scale=inv,
                                                       bias=epst[:, 0:1]), reads=[b_ssc, b_eps], writes=[b_ssc])
        yield P.op("act", lambda: nc.scalar.activation(out=ssc[:, 0:n], in_=ssc[:, 0:n], func=AF.Exp, scale=-0.5),
                   reads=[b_ssc], writes=[b_ssc])

    def transposes(srcs, b_src, dst_ap, b_dst, nblk):
        pt, b_pt = k.next_pst()

        def tr():
            inst = None
            for i, s_ in enumerate(srcs):
                inst = nc.tensor.transpose(pt[0:64, i * 128:(i + 1) * 128], s_, idb[:])
            return inst
        P.op("pe", tr, reads=[b_src, b_idb], writes=[b_pt])
        yield P.op("act", lambda: nc.scalar.copy(dst_ap, pt[0:64, 0:nblk * 128]), reads=[b_pt], writes=[b_dst])

    def tail(h, u, B, Eget, nkt):
        ssc, b_ssc = B["ssc"]
        qT, b_qT = B["qT"]
        kT, b_kT = B["kT"]
        vaug, b_vaug = B["vaug"]
        osb, b_osb = B["osb"]
        ot2, b_ot2 = B["ot2"]
        ab, b_ab = B["ab"]
        tok0 = u * 256
        for kt in range(nkt):
            ps, b_ps = k.next_ps()
            Et, b_E = Eget(kt)

            def mm(ps=ps, kt=kt):
                nc.tensor.matmul(ps[:, 0:256], kT[:, 0, kt * 128:(kt + 1) * 128], qT[:, 0, :], start=True, stop=True)
                return nc.tensor.matmul(ps[:, 256:512], kT[:, 1, kt * 128:(kt + 1) * 128], qT[:, 1, :], start=True, stop=True)
            P.op("pe", mm, reads=[b_kT, b_qT], writes=[b_ps])
            yield P.op("act", lambda ps=ps, Et=Et: nc.scalar.activation(out=Et, in_=ps[:, 0:512], func=AF.Exp, scale=0.125),
                       reads=[b_ps], writes=[b_E])
        psA, b_psA = k.next_ps()
        psB, b_psB = k.next_ps()

        def mm():
            inst = None
            for c, pb in ((0, psA), (1, psB)):
                for qt in range(2):
                    for kt in range(nkt):
                        Et, _ = Eget(kt)
                        inst = nc.tensor.matmul(pb[:, qt * 256:qt * 256 + 129], Et[:, c * 256 + qt * 128:c * 256 + (qt + 1) * 128],
                                                vaug[:, kt, 0:129], start=(kt == 0), stop=(kt == nkt - 1))
            return inst
        P.op("pe", mm, reads=[Eget(0)[1], Eget(nkt - 1)[1], b_vaug], writes=[b_psA, b_psB])
        vA = psA[:, 0:512].rearrange("p (t n) -> p t n", t=2)
        vB = psB[:, 0:512].rearrange("p (t n) -> p t n", t=2)
        o3 = osb[:].rearrange("p (t n) -> p t n", t=2)
        t3 = ot2[:].rearrange("p (t n) -> p t n", t=2)
        a3 = ab[:].rearrange("p (t n) -> p t n", t=2)
        P.op("dve", lambda: nc.vector.reciprocal(ssc[:, 8:10], vA[:, :, 128]), reads=[b_psA], writes=[b_ssc])
        P.op("dve", lambda: nc.vector.reciprocal(ssc[:, 10:12], vB[:, :, 128]), reads=[b_psB], writes=[b_ssc])
        P.op("dve", lambda: nc.vector.tensor_tensor(ssc[:, 10:12], ssc[:, 10:12], lsc[:, 2:3].to_broadcast([128, 2]), ALU.mult),
             reads=[b_ssc, b_lsc], writes=[b_ssc])
        P.op("dve", lambda: nc.vector.tensor_tensor(t3, vB[:, :, 0:128], ssc[:, 10:12].unsqueeze(2).to_broadcast([128, 2, 128]), ALU.mult),
             reads=[b_psB, b_ssc], writes=[b_ot2])
        yield P.op("dve", lambda: nc.vector.tensor_tensor(o3, vA[:, :, 0:128], ssc[:, 8:10].unsqueeze(2).to_broadcast([128, 2, 128]), ALU.mult),
                   reads=[b_psA, b_ssc], writes=[b_osb])
        yield P.op("dve", lambda: nc.vector.tensor_tensor(osb[:], osb[:], ot2[:], ALU.add), reads=[b_osb, b_ot2], writes=[b_osb])
        yield P.op("act", lambda: nc.scalar.activation(out=ot2[:], in_=osb[:], func=AF.Square), reads=[b_osb], writes=[b_ot2])
        yield P.op("dve", lambda: nc.vector.tensor_reduce(ssc[:, 12:14], t3, AX.X, ALU.add), reads=[b_ot2], writes=[b_ssc])
        yield P.op("act", lambda: nc.scalar.activation(out=ssc[:, 12:14], in_=ssc[:, 12:14], func=AF.Ln, scale=1.0 / 128.0,
                                                       bias=epst[:, 0:1]), reads=[b_ssc, b_eps], writes=[b_ssc])
        yield P.op("act", lambda: nc.scalar.activation(out=ssc[:, 12:14], in_=ssc[:, 12:14], func=AF.Exp, scale=-0.5),
                   reads=[b_ssc], writes=[b_ssc])
        yield P.op("dve", lambda: nc.vector.tensor_tensor(o3, o3, ssc[:, 12:14].unsqueeze(2).to_broadcast([128, 2, 128]), ALU.mult),
                   reads=[b_osb, b_ssc], writes=[b_osb])
        yield P.op("dve", lambda: nc.vector.tensor_tensor(a3, o3, gsub[:].unsqueeze(1).to_broadcast([128, 2, 128]), ALU.mult),
                   reads=[b_osb, b_gsub], writes=[b_ab])
        pt, b_pt = k.next_pst()

        def tr():
            nc.tensor.transpose(pt[:, 0:128], ab[:, 0:128], idb[:])
            return nc.tensor.transpose(pt[:, 128:256], ab[:, 128:256], idb[:])
        P.op("pe", tr, reads=[b_ab, b_idb], writes=[b_pt])
        yield P.op("act", lambda: nc.scalar.copy(attT[:, h, tok0:tok0 + 256], pt[:, 0:256]), reads=[b_pt], writes=[b_attT])

    def prompt_iter(h, u, B, wa, b_wa):
        qkv, b_qkv = B["qkv"]
        sq, b_sq = B["sq"]
        qkn, b_qkn = B["qkn"]
        qkb, b_qkb = B["qkb"]
        ssc, b_ssc = B["ssc"]
        qT, b_qT = B["qT"]
        kT, b_kT = B["kT"]
        vaug, b_vaug = B["vaug"]
        E, b_E = B["E"]
        tok0 = u * 256
        for t in range(2):
            ps, b_ps = k.next_ps()

            def mm(ps=ps, t=t):
                inst = None
                for kc in range(KC):
                    inst = nc.tensor.matmul(ps[:, 0:384], hh2[:, kc, tok0 + t * 128:tok0 + (t + 1) * 128], wa[:, kc, :],
                                            start=(kc == 0), stop=(kc == KC - 1))
                return inst
            P.op("pe", mm, reads=[b_wa, b_hh2], writes=[b_ps])
            yield P.op("act", lambda ps=ps, t=t: nc.scalar.copy(qkv[:, t, :], ps[:, 0:384]), reads=[b_ps], writes=[b_qkv])
        yield P.op("act", lambda: nc.scalar.activation(out=sq[:], in_=qkv[:, :, 0:256], func=AF.Square), reads=[b_qkv], writes=[b_sq])
        yield P.op("dve", lambda: nc.vector.tensor_reduce(ssc[:, 0:8], sq[:].rearrange("p t (g d) -> p (t g) d", d=64), AX.X, ALU.add),
                   reads=[b_sq], writes=[b_ssc])
        yield from rstd_inplace(ssc, b_ssc, 8, 1.0 / 64.0)
        yield P.op("dve", lambda: nc.vector.tensor_tensor(
            qkn[:].rearrange("p t (g d) -> p t g d", d=64), qkv[:, :, 0:256].rearrange("p t (g d) -> p t g d", d=64),
            ssc[:, 0:8].rearrange("p (t g) -> p t g", t=2).unsqueeze(3).to_broadcast([128, 2, 4, 64]), ALU.mult),
            reads=[b_qkv, b_ssc], writes=[b_qkn])
        yield P.op("dve", lambda: nc.vector.tensor_tensor(qkn[:], qkn[:], gqk[:].unsqueeze(1).to_broadcast([128, 2, 256]), ALU.mult),
                   reads=[b_qkn, b_gqk], writes=[b_qkn])
        r0 = u * 256
        P.dma("sp", O["nk"][r0:r0 + 256, h * 128:(h + 1) * 128].rearrange("(t p) c -> p t c", p=128), qkn[:, :, 128:256],
              reads=[b_qkn], is_output=True)
        P.dma("sp", O["nv"][r0:r0 + 256, h * 128:(h + 1) * 128].rearrange("(t p) c -> p t c", p=128), qkv[:, :, 256:384],
              reads=[b_qkv], is_output=True)
        yield P.op("dve", lambda: nc.vector.tensor_copy(qkb[:], qkn[:]), reads=[b_qkn], writes=[b_qkb])
        srcs = [qkb[:, t, w * 128 + c * 64:w * 128 + (c + 1) * 64] for w in range(2) for c in range(2) for t in range(2)]
        pt, b_pt = k.next_pst()

        def tr():
            inst = None
            for i, s_ in enumerate(srcs):
                inst = nc.tensor.transpose(pt[0:64, i * 128:(i + 1) * 128], s_, idb[:])
            return inst
        P.op("pe", tr, reads=[b_qkb, b_idb], writes=[b_pt])
        P.op("act", lambda: nc.scalar.copy(qT[:].rearrange("p c n -> p (c n)"), pt[0:64, 0:512]), reads=[b_pt], writes=[b_qT])
        yield P.op("dve", lambda: nc.vector.tensor_copy(kT[:].rearrange("p c n -> p (c n)"), pt[0:64, 512:1024]), reads=[b_pt], writes=[b_kT])
        yield P.op("dve", lambda: nc.vector.tensor_copy(vaug[:, :, 0:128], qkv[:, :, 256:384]), reads=[b_qkv], writes=[b_vaug])
        yield from tail(h, u, B, lambda kt: (E[:, kt, :], b_E), 2)

    def interleave(gens):
        gens = list(gens)
        while gens:
            nxt = []
            for g in gens:
                try:
                    yield next(g)
                    nxt.append(g)
                except StopIteration:
                    pass
            gens = nxt

    def sample_iter(h, B, wa, b_wa):
        u = 2
        tok0 = 512
        qT, b_qT = B["qT"]
        kT, b_kT = B["kT"]
        vaug, b_vaug = B["vaug"]
        qs, b_qs = B["qs"]
        qr, b_qr = B["qr"]
        qb, b_qb = B["qb"]
        ck4, b_ck4 = B["ck4"]
        cv4, b_cv4 = B["cv4"]
        (E0r, b_E0), (E1r, b_E1) = k.ring[0], k.ring[1]
        E0 = E0r[:, 0:3072].rearrange("p (t n) -> p t n", t=6)
        E1 = E1r[:, 0:3072].rearrange("p (t n) -> p t n", t=6)
        Eget = lambda kt: ((E0[:, kt, :], b_E0) if kt < 6 else (E1[:, kt - 6, :], b_E1))
        r2 = k.ring[2][0][:].bitcast(F32)
        r3 = k.ring[3][0][:].bitcast(F32)
        (sqt, b_sqt), (knt, b_knt) = k.nm[0]
        krt, b_krt = k.nm[1]
        (rtt, b_rtt), (qnt, b_qnt) = k.nm[2]
        KS = [dict(kv=k.stage[0], sq=(sqt[:], b_sqt), kn=(knt[:], b_knt), kr=(krt[:], b_krt), t1=(rtt[:], b_rtt), t2=(qnt[:], b_qnt),
                   kb=B["kb4"], ssc=B["ssc"]),
              dict(kv=k.stage[1], sq=(r2[:, 0:512], B["x_r2a"]), kn=(r2[:, 512:1024], B["x_r2b"]), kr=(r2[:, 1024:1536], B["x_r2c"]),
                   t1=(r2[:, 1536:2048], B["x_r2d"]), t2=(r3[:, 0:512], B["x_r3a"]), kb=B["kb4b"], ssc=B["ssc2"])]

        def q_lane():
            ssc, b_ssc = B["sscq"]
            kbq, b_kbq = B["kbq"]
            sqq = r3[:, 512:768].rearrange("p (t n) -> p t n", t=2)
            qn = r3[:, 768:1024].rearrange("p (t n) -> p t n", t=2)
            b_qn = B["x_r3b"]
            yield P.dma("sp", ck4[:], I["ck"][:, h * 128:(h + 1) * 128].rearrange("(t p) c -> p t c", p=128), writes=[b_ck4])
            yield P.dma("sp", cv4[:], I["cvv"][:, h * 128:(h + 1) * 128].rearrange("(t p) c -> p t c", p=128), writes=[b_cv4])
            for t in range(2):
                ps, b_ps = k.next_ps()

                def mm(ps=ps, t=t):
                    inst = None
                    for kc in range(KC):
                        inst = nc.tensor.matmul(ps[:, 0:128], hh2[:, kc, tok0 + t * 128:tok0 + (t + 1) * 128], wa[:, kc, 0:128],
                                                start=(kc == 0), stop=(kc == KC - 1))
                    return inst
                P.op("pe", mm, reads=[b_wa, b_hh2], writes=[b_ps])
                yield P.op("act", lambda ps=ps, t=t: nc.scalar.copy(qs[:, t, :], ps[:, 0:128]), reads=[b_ps], writes=[b_qs])
            yield P.op("act", lambda: nc.scalar.activation(out=sqq, in_=qs[:], func=AF.Square), reads=[b_qs], writes=[b_qn])
            yield P.op("dve", lambda: nc.vector.tensor_reduce(ssc[:, 0:4], sqq.rearrange("p t (g d) -> p (t g) d", d=64), AX.X, ALU.add),
                       reads=[b_qn], writes=[b_ssc])
            yield from rstd_inplace(ssc, b_ssc, 4, 1.0 / 64.0)
            yield P.op("dve", lambda: nc.vector.tensor_tensor(
                qn.rearrange("p t (g d) -> p t g d", d=64), qs[:].rearrange("p t (g d) -> p t g d", d=64),
                ssc[:, 0:4].rearrange("p (t g) -> p t g", t=2).unsqueeze(3).to_broadcast([128, 2, 2, 64]), ALU.mult),
                reads=[b_qs, b_ssc, b_qn], writes=[b_qn])
            yield P.op("dve", lambda: nc.vector.tensor_tensor(qn, qn, gqk[:, 0:128].unsqueeze(1).to_broadcast([128, 2, 128]), ALU.mult),
                       reads=[b_qn, b_gqk], writes=[b_qn])
            yield from rope_apply(k, qr[:], qn, ropo[:], 2, [b_qn, b_ropo], [b_qr], r3[:, 1024:1280], B["x_r3c"], r3[:, 1280:1536], B["x_r3c"])
            yield P.op("dve", lambda: nc.vector.tensor_copy(qb[:], qr[:]), reads=[b_qr], writes=[b_qb])
            yield from transposes([qb[:, t, c * 64:(c + 1) * 64] for c in range(2) for t in range(2)], b_qb,
                                  qT[:].rearrange("p c n -> p (c n)"), b_qT, 4)
            yield P.op("dve", lambda: nc.vector.tensor_copy(kbq[:], ck4[:]), reads=[b_ck4], writes=[b_kbq])
            yield from transposes([kbq[:, j, c * 64:(c + 1) * 64] for c in range(2) for j in range(4)], b_kbq,
                                  kT[:, :, 1024:1536], b_kT, 8)
            yield P.op("dve", lambda: nc.vector.tensor_copy(vaug[:, 8:12, 0:128], cv4[:]), reads=[b_cv4], writes=[b_vaug])

        def k_lane(half):
            S_ = KS[half]
            kv4t, b_kv4 = S_["kv"]
            kv4 = kv4t[:].rearrange("p (t n) -> p t n", t=4)
            (sqa, b_sq), (kna, b_kn), (kra, b_kr) = S_["sq"], S_["kn"], S_["kr"]
            (t1a, b_t1), (t2a, b_t2) = S_["t1"], S_["t2"]
            kb4, b_kb4 = S_["kb"]
            ssc, b_ssc = S_["ssc"]
            for j in range(4):
                ft = half * 4 + j
                ps, b_ps = k.next_ps()

                def mm(ps=ps, ft=ft):
                    inst = None
                    for kc in range(KC):
                        src = hh2[:, kc, tok0 + ft * 128:tok0 + (ft + 1) * 128] if ft < 2 else \
                            hh2f[:, kc, (ft - 2) * 128:(ft - 1) * 128]
                        inst = nc.tensor.matmul(ps[:, 0:256], src, wa[:, kc, 128:384],
                                                start=(kc == 0), stop=(kc == KC - 1))
                    return inst
                P.op("pe", mm, reads=[b_wa, b_hh2f, b_hh2], writes=[b_ps])
                yield P.op("act", lambda ps=ps, j=j: nc.scalar.copy(kv4[:, j, :], ps[:, 0:256]), reads=[b_ps], writes=[b_kv4])
            sq4 = sqa.rearrange("p (t n) -> p t n", t=4)
            kn4 = kna.rearrange("p (t n) -> p t n", t=4)
            kr4 = kra.rearrange("p (t n) -> p t n", t=4)
            yield P.op("act", lambda: nc.scalar.activation(out=sq4, in_=kv4[:, :, 0:128], func=AF.Square), reads=[b_kv4], writes=[b_sq])
            yield P.op("dve", lambda: nc.vector.tensor_reduce(ssc[:, 0:8], sq4.rearrange("p t (g d) -> p (t g) d", d=64), AX.X, ALU.add),
                       reads=[b_sq], writes=[b_ssc])
            yield from rstd_inplace(ssc, b_ssc, 8, 1.0 / 64.0)
            yield P.op("dve", lambda: nc.vector.tensor_tensor(
                kn4.rearrange("p t (g d) -> p t g d", d=64), kv4[:, :, 0:128].rearrange("p t (g d) -> p t g d", d=64),
                ssc[:, 0:8].rearrange("p (t g) -> p t g", t=4).unsqueeze(3).to_broadcast([128, 4, 2, 64]), ALU.mult),
                reads=[b_kv4, b_ssc], writes=[b_kn])
            yield P.op("dve", lambda: nc.vector.tensor_tensor(kn4, kn4, gqk[:, 128:256].unsqueeze(1).to_broadcast([128, 4, 128]), ALU.mult),
                       reads=[b_kn, b_gqk], writes=[b_kn])
            yield from rope_apply(k, kr4, kn4, ropf[:, half * 4:(half + 1) * 4, :], 4, [b_kn, b_ropf], [b_kr], t1a, b_t1, t2a, b_t2)
            yield P.op("dve", lambda: nc.vector.tensor_copy(kb4[:], kr4), reads=[b_kr], writes=[b_kb4])
            yield from transposes([kb4[:, j, c * 64:(c + 1) * 64] for c in range(2) for j in range(4)], b_kb4,
                                  kT[:, :, half * 512:(half + 1) * 512], b_kT, 8)
            yield P.op("dve", lambda: nc.vector.tensor_copy(vaug[:, half * 4:(half + 1) * 4, 0:128], kv4[:, :, 128:256]),
                       reads=[b_kv4], writes=[b_vaug])

        yield from interleave([k_lane(0), k_lane(1), q_lane()])
        yield from tail(h, u, B, Eget, 12)

    BP0, BP1, BS = bufset("p0", False), bufset("p1", False), bufset("s", True)
    for h in range(8):
        wa, b_wa = was[h % 2]
        for j, off in enumerate((O_DQ, O_DK, O_DV)):
            P.dma("pool", wa[:, :, j * 128:(j + 1) * 128], win[:, :, off + h * 128:off + (h + 1) * 128], writes=[b_wa])
        run_lanes([sample_iter(h, BS, wa, b_wa), prompt_iter(h, 0, BP0, wa, b_wa), prompt_iter(h, 1, BP1, wa, b_wa)], [4, 1, 1])


def emit_merge(k, ph, hh2, b_hh2, attT, b_attT, hmT, b_hmT):
    nc, P, I, sb = k.nc, k.P, k.I, k.sb
    xT, b_xT = k.xT
    mG, b_mG = k.mod["G"]
    win = I["w_in"].rearrange("(kc p) n -> p kc n", p=128)
    yT, b_yT = sb("yT", [128, KC, NOWN], BF16, stack=ph)
    sg = [sb("mg_sg%d" % i, [128, 512], stack=ph) for i in range(2)]
    t1, b_t1 = sb("mg_t1", [128, 512], stack=ph)
    t2, b_t2 = sb("mg_t2", [128, 512], stack=ph)
    vw = lambda t: t[:].rearrange("p (kc n) -> p kc n", kc=KC)
    for og in range(2):
        tiles = []
        for src in (I["w_br_m"].rearrange("(kc p) n -> p kc n", p=128)[:, :, og * 512:(og + 1) * 512],
                    I["w_br_d"].rearrange("(kc p) n -> p kc n", p=128)[:, :, og * 512:(og + 1) * 512],
                    win[:, :, O_GM + og * 512:O_GM + (og + 1) * 512],
                    win[:, :, O_GD + og * 512:O_GD + (og + 1) * 512]):
            t, b = ring_load(k, vw, src)
            tiles.append((vw(t), b))
        (wmv, b_wmv), (wdv, b_wdv), (wgm, b_wgm), (wgd, b_wgd) = tiles
        for j in range(4):
            oc = og * 4 + j
            for (t0, n, which) in OWN_GROUPS:
                pss = []
                for (wv, b_w, act, b_act) in ((wgm, b_wgm, hh2, b_hh2), (wmv, b_wmv, hmT, b_hmT),
                                             (wgd, b_wgd, hh2, b_hh2), (wdv, b_wdv, attT, b_attT)):
                    ps, b_ps = k.next_ps()

                    def mm(ps=ps, wv=wv, act=act):
                        inst = None
                        for kc in range(KC):
                            inst = nc.tensor.matmul(ps[:, 0:n], wv[:, kc, j * 128:(j + 1) * 128], act[:, kc, t0:t0 + n],
                                                    start=(kc == 0), stop=(kc == KC - 1))
                        return inst
                    P.op("pe", mm, reads=[b_w, b_act], writes=[b_ps])
                    pss.append((ps, b_ps))
                (pgm, b_pgm), (pm, b_pm), (pgd, b_pgd), (pd, b_pd) = pss
                (s0, b_s0), (s1, b_s1) = sg
                P.op("act", lambda: nc.scalar.activation(out=s0[:, 0:n], in_=pgm[:, 0:n], func=AF.Sigmoid), reads=[b_pgm], writes=[b_s0])
                P.op("act", lambda: nc.scalar.activation(out=s1[:, 0:n], in_=pgd[:, 0:n], func=AF.Sigmoid), reads=[b_pgd], writes=[b_s1])
                P.op("dve", lambda: nc.vector.tensor_tensor(t1[:, 0:n], s0[:, 0:n], pm[:, 0:n], ALU.mult), reads=[b_s0, b_pm], writes=[b_t1])
                P.op("dve", lambda: nc.vector.tensor_tensor(t2[:, 0:n], s1[:, 0:n], pd[:, 0:n], ALU.mult), reads=[b_s1, b_pd], writes=[b_t2])
                P.op("dve", lambda: nc.vector.tensor_tensor(yT[:, oc, t0:t0 + n], t1[:, 0:n], t2[:, 0:n], ALU.add),
                     reads=[b_t1, b_t2], writes=[b_yT])
    wo = I["w_out"].rearrange("(kc p) n -> p kc n", p=128)
    for og in range(2):
        t, b_wo = ring_load(k, vw, wo[:, :, og * 512:(og + 1) * 512])
        wov = vw(t)
        for j in range(4):
            oc = og * 4 + j
            for (t0, n, which) in OWN_GROUPS:
                ps, b_ps = k.next_ps()

                def mm(ps=ps):
                    inst = None
                    for kc in range(KC):
                        inst = nc.tensor.matmul(ps[:, 0:n], wov[:, kc, j * 128:(j + 1) * 128], yT[:, kc, t0:t0 + n],
                                                start=(kc == 0), stop=(kc == KC - 1))
                    return inst
                P.op("pe", mm, reads=[b_wo, b_yT], writes=[b_ps])
                P.op("dve", lambda ps=ps: nc.vector.scalar_tensor_tensor(
                    out=xT[:, oc, t0:t0 + n], in0=ps[:, 0:n], scalar=mG[:, 1, oc, which:which + 1], in1=xT[:, oc, t0:t0 + n],
                    op0=ALU.mult, op1=ALU.add), reads=[b_ps, b_mG, b_xT], writes=[b_xT])


def emit_mixer(k, ph):
    P, sb = k.P, k.sb
    xT, b_xT = k.xT
    hh2, b_hh2 = sb("hh2", [128, KC, NOWN], BF16, stack=ph)
    norm_mod(k, ph, xT, b_xT, OWN_GROUPS, 1, hh2, b_hh2, "o1")
    k.dbg_dump("hh2", hh2[:], b_hh2, [128, KC, NOWN])
    attT, b_attT = sb("attT", [128, KC, NOWN], BF16, stack=ph)
    hmT, b_hmT = sb("hmT", [128, KC, NOWN], BF16, stack=ph)
    with ExitStack() as ph2:
        if "nomlstm" not in k.stages:
            k.gate_keep = [sb("ecol", [128, NSLOT, 16], stack=ph2), sb("dmat", [128, 12, 4, 128], BF16, stack=ph2),
                           sb("mslots", [4, NSLOT + 3], stack=ph2)]
            if "gates_only" not in k.stages:
                win_ = k.I["w_in"].rearrange("(kc p) n -> p kc n", p=128)
                (rA_, b_rA_), (rB_, b_rB_) = k.ring[0], k.ring[1]
                wA_ = rA_[:].rearrange("p (j kc n) -> p j kc n", j=2, kc=KC)
                wB_ = rB_[:].rearrange("p (j kc n) -> p j kc n", j=2, kc=KC)
                for (wt_, b_wt_, j_, off_) in ((wA_, b_rA_, 0, O_MK), (wA_, b_rA_, 1, O_MV), (wB_, b_rB_, 0, O_MQ), (wB_, b_rB_, 1, O_MO)):
                    P.dma("pool", wt_[:, j_], win_[:, :, off_:off_ + 256], writes=[b_wt_])
                k.mlstm_w0_loaded = True
            with ExitStack() as ph3:
                emit_gates(k, ph2, ph3, hh2, b_hh2)
            P.barrier()
            if "gates_only" not in k.stages:
                try:
                    emit_mlstm(k, ph2, hh2, b_hh2, hmT, b_hmT)
                except StopEmit:
                    pass
    P.barrier()
    k.dbg_dump("hmT", hmT[:], b_hmT, [128, KC, NOWN])
    with ExitStack() as ph2:
        if "noattn" not in k.stages:
            emit_attn(k, ph2, hh2, b_hh2, attT, b_attT)
    P.barrier()
    k.dbg_dump("attT", attT[:], b_attT, [128, KC, NOWN])
    with ExitStack() as ph2:
        if "nomerge" not in k.stages:
            emit_merge(k, ph2, hh2, b_hh2, attT, b_attT, hmT, b_hmT)
    k.dbg_dump("x2", xT[:], b_xT, [128, KC, NOWN])


def _rope_table(pos_tokens):
    t = np.asarray(pos_tokens)
    row = (t // 64).astype(np.float32)
    col = (t % 64).astype(np.float32)
    freqs = np.power(np.float32(10000.0), -np.arange(16, dtype=np.float32) / np.float32(16))
    ar = row[:, None] * freqs[None, :]
    ac = col[:, None] * freqs[None, :]
    cos4 = np.concatenate([np.cos(ar), np.cos(ac), np.cos(ar), np.cos(ac)], axis=1)
    sin4 = np.concatenate([np.sin(ar), np.sin(ac), np.sin(ar), np.sin(ac)], axis=1)
    return np.ascontiguousarray(np.concatenate([cos4, sin4], axis=1).astype(np.float32))


def make_in_maps(inp):
    f = lambda a: np.ascontiguousarray(np.asarray(a, dtype=np.float32))
    xp, xs = f(inp["x_prompt"]), f(inp["x_sample"])
    shared = dict(
        w_ada=f(inp["w_ada"][0]), b_ada=f(inp["b_ada"][0]).reshape(72, 128),
        g_norm=f(inp["g_norm"][0]).reshape(24, 128),
        f1w1=f(inp["ffn1_w1"][0]), f1w3=f(inp["ffn1_w3"][0]), f1w2=f(inp["ffn1_w2"][0]),
        f2w1=f(inp["ffn2_w1"][0]), f2w3=f(inp["ffn2_w3"][0]), f2w2=f(inp["ffn2_w2"][0]),
        w_in=f(inp["w_in"][0]), b_gate=f(inp["b_gate"][0]).reshape(1, 16),
        g_qn=f(inp["g_qn"][0]), g_kn=f(inp["g_kn"][0]),
        lamv=f(np.stack([inp["lam_q1"][0], inp["lam_k1"][0], inp["lam_q2"][0], inp["lam_k2"][0]])),
        g_sub=f(inp["g_sub"][0]), g_mh=f(inp["g_mh"][0]).reshape(1024),
        w_br_m=f(inp["w_br_m"][0]), w_br_d=f(inp["w_br_d"][0]), w_out=f(inp["w_out"][0]),
    )
    maps = []
    for i in range(8):
        b, r = i // 4, i % 4
        m = dict(shared)
        m["xo"] = np.ascontiguousarray(np.concatenate(
            [xp[2 * i], xp[2 * i + 1], xs[b, r * 256:(r + 1) * 256]], axis=0))
        others = [q for q in range(4) if q != r]
        m["xf"] = np.ascontiguousarray(np.concatenate([xs[b, q * 256:(q + 1) * 256] for q in others], axis=0))
        m["rope_full"] = _rope_table(np.concatenate([np.arange(q * 256, (q + 1) * 256) for q in [r] + others]))
        m["cv"] = np.ascontiguousarray(np.stack([f(inp["c_ctx"]), f(inp["c"])[b]]))
        m["ck"] = f(inp["cache_k"][b, 0]).reshape(512, 1024)
        m["cvv"] = f(inp["cache_v"][b, 0]).reshape(512, 1024)
        m["sC"] = f(inp["state_C"][b, 0])
        m["sn"] = f(inp["state_n"][b, 0])
        m["sm"] = f(inp["state_m"][b, 0])
        m["rope_own"] = _rope_table(np.arange(r * 256, (r + 1) * 256))
        mu = np.zeros(24, np.float32)
        for j in range(6):
            mu[j] = 1.0 if j < 2 * r else 0.0
            mu[6 + j] = 1.0 if (5 - j) >= 2 * r else 0.0
        mu[12:] = (mu[:12] - 1.0) * BIG
        m["mu"] = mu
        maps.append(m)
    return maps


_CACHE = {}


def kernel(**inputs):
    if "nc" not in _CACHE:
        _CACHE["nc"] = build_program()[0]
    nc = _CACHE["nc"]
    maps = make_in_maps(inputs)
    res = run_bass_kernel_spmd(nc, maps, core_ids=list(range(8)))
    R = res.results
    y_p = np.zeros((16, 256, 1024), np.float32)
    y_s = np.zeros((2, 1024, 1024), np.float32)
    nk = np.zeros((16, 1, 256, 8, 2, 64), np.float32)
    nv = np.zeros((16, 1, 256, 8, 128), np.float32)
    nC = np.zeros((16, 1, 2, 4, 256, 256), np.float32)
    nn = np.zeros((16, 1, 2, 4, 256), np.float32)
    nm = np.zeros((16, 1, 2, 4), np.float32)
    for i in range(8):
        b, r = i // 4, i % 4
        yo = R[i]["yo"]
        y_p[2 * i] = yo[0:256]
        y_p[2 * i + 1] = yo[256:512]
        y_s[b, r * 256:(r + 1) * 256] = yo[512:768]
        nk[2 * i:2 * i + 2, 0] = R[i]["nk"].reshape(2, 256, 8, 2, 64)
        nv[2 * i:2 * i + 2, 0] = R[i]["nv"].reshape(2, 256, 8, 128)
        nC[2 * i:2 * i + 2, 0] = R[i]["nC"]
        nn[2 * i:2 * i + 2, 0] = R[i]["nn"]
        nm[2 * i:2 * i + 2, 0] = R[i]["nm"]
    return (y_p, y_s, nk, nv, nC, nn, nm)
```

```python
import math
from contextlib import ExitStack

import numpy as np
import concourse.bass as bass
import concourse.mybir as mybir
from concourse.bass_utils import run_bass_kernel_spmd

F32 = mybir.dt.float32
BF16 = mybir.dt.bfloat16
AF = mybir.ActivationFunctionType
ALU = mybir.AluOpType
AX = mybir.AxisListType

D = 1024
KC = 8
FF = 2816
NFF = 22
NOWN = 768
NFULL = 768
D_IN = 9232
EPS = 1e-6
BIG = 3.0e4
SEM_LIMIT = 30000
SKIP_SAME = ()
LAM_INIT = 0.8 - 0.6 * math.exp(-0.3 * 0)
O_MQ, O_MK, O_MV, O_MO, O_MG, O_DQ, O_DK, O_DV, O_GM, O_GD = (
    0, 1024, 2048, 3072, 4096, 4112, 5136, 6160, 7184, 8208)


class Buf:
    def __init__(self, name="", psum=False):
        self.name = name
        self.w = None
        self.r = {}
        self.psum = psum


class Eng:
    def __init__(self, P, name, eng):
        self.P, self.name, self.eng = P, name, eng
        self.sem, self.count, self.known = None, 0, {}

    def new_sem(self):
        self.sem = self.P.alloc_sem(self.name)
        self.count = 0
        if not hasattr(self, "own"):
            self.own = set()
        self.own.add(id(self.sem))

    def wait(self, ev):
        if ev is None:
            return
        sem, val = ev
        if self.known.get(id(sem), 0) >= val:
            return
        self.eng.wait_ge(sem, val)
        self.known[id(sem)] = val


class Prog:
    def __init__(self, nc, n_dma_sems=32):
        self.nc = nc
        self.sem_i = 0
        self.engs = {}
        for nm, e in (("pe", nc.tensor), ("act", nc.scalar), ("dve", nc.vector),
                      ("pool", nc.gpsimd), ("sp", nc.sync)):
            self.engs[nm] = Eng(self, nm, e)
        self.dma_slots = []
        self.n_dma_sems = n_dma_sems
        self.dma_i = 0
        self.out_evs = []
        self.n_inst = 0

    def start(self, stack):
        self.stack = stack
        for e in self.engs.values():
            e.new_sem()
        self.q_slots = {}
        for q in ("sp", "pool"):
            self.q_slots[q] = [[self.alloc_sem("dma%s%d" % (q, i)), 0, None] for i in range(self.n_dma_sems // 2)]
            self.dma_slots += self.q_slots[q]
        self.q_i = {"sp": 0, "pool": 0}

    def alloc_sem(self, name):
        self.sem_i += 1
        return self.stack.enter_context(self.nc.semaphore("%s_%d" % (name, self.sem_i)))

    def _deps(self, reads, writes):
        evs = {}

        def add(ev):
            if ev is None:
                return
            k = id(ev[0])
            if k not in evs or evs[k][1] < ev[1]:
                evs[k] = ev
        for b in reads:
            add(b.w)
            if b.psum:
                for ev in b.r.values():
                    add(ev)
        for b in writes:
            add(b.w)
            for ev in b.r.values():
                add(ev)
        return list(evs.values())

    def _commit(self, ev, reads, writes):
        for b in writes:
            b.w = ev
            b.r = {}
        for b in reads:
            if b in writes:
                continue
            k = id(ev[0])
            if k not in b.r or b.r[k][1] < ev[1]:
                b.r[k] = ev

    def op(self, engname, fn, reads=(), writes=()):
        E = self.engs[engname]
        for ev in self._deps(reads, writes):
            if SKIP_SAME and engname in SKIP_SAME and id(ev[0]) in E.own:
                continue
            E.wait(ev)
        if E.count + 1 > SEM_LIMIT:
            E.new_sem()
        inst = fn()
        E.count += 1
        inst.then_inc(E.sem, 1)
        ev = (E.sem, E.count)
        self._commit(ev, reads, writes)
        self.n_inst += 1
        return ev

    def dma(self, qname, out, in_, reads=(), writes=(), is_output=False, **kw):
        Q = self.engs[qname]
        for ev in self._deps(reads, writes):
            Q.wait(ev)
        slots = self.q_slots[qname]
        slot = slots[self.q_i[qname] % len(slots)]
        self.q_i[qname] += 1
        if slot[2] is not None:
            Q.wait(slot[2])
        if slot[1] + 16 > SEM_LIMIT:
            slot[0] = self.alloc_sem("dmax")
            slot[1] = 0
        inst = Q.eng.dma_start(out=out, in_=in_, **kw)
        slot[1] += 16
        inst.then_inc(slot[0], 16)
        ev = (slot[0], slot[1])
        slot[2] = ev
        self._commit(ev, reads, writes)
        if is_output:
            self.out_evs.append(ev)
        self.n_inst += 1
        return ev

    def barrier(self):
        evs = [(e.sem, e.count) for e in self.engs.values() if e.count > 0]
        evs += [s[2] for s in self.dma_slots if s[2] is not None]
        for e in self.engs.values():
            for ev in evs:
                if ev[0] is e.sem:
                    continue
                e.wait(ev)

    def finish(self):
        sp = self.engs["sp"]
        for s in self.dma_slots:
            sp.wait(s[2])
        for ev in self.out_evs:
            sp.wait(ev)
        for e in self.engs.values():
            if e is not sp and e.count > 0:
                sp.wait((e.sem, e.count))


class K:
    pass


def build_program(dbg=(), stages=("full", "ffn1", "mixer", "ffn2")):
    nc = bass.Bass("TRN2", target_bir_lowering=False)
    k = K()
    k.nc = nc
    k.stages = set(stages)
    k.dbg = set(dbg)
    k.dbg_out = {}

    def din(name, shape):
        return nc.dram_tensor(name, list(shape), F32, kind="ExternalInput").ap()

    def dout(name, shape):
        return nc.dram_tensor(name, list(shape), F32, kind="ExternalOutput").ap()

    I = k.I = {}
    for name, shape in (
        ("xo", (NOWN, D)), ("xf", (NFULL, D)), ("cv", (2, D)),
        ("ck", (512, D)), ("cvv", (512, D)), ("sC", (2, 4, 256, 256)), ("sn", (2, 4, 256)),
        ("sm", (2, 4)),
        ("w_ada", (D, 9 * D)), ("b_ada", (72, 128)), ("g_norm", (24, 128)),
        ("f1w1", (D, FF)), ("f1w3", (D, FF)), ("f1w2", (FF, D)),
        ("f2w1", (D, FF)), ("f2w3", (D, FF)), ("f2w2", (FF, D)),
        ("w_in", (D, D_IN)), ("b_gate", (1, 16)), ("g_qn", (64,)), ("g_kn", (64,)),
        ("lamv", (4, 64)), ("g_sub", (128,)), ("g_mh", (1024,)),
        ("w_br_m", (D, D)), ("w_br_d", (D, D)), ("w_out", (D, D)),
        ("rope_own", (256, 128)), ("rope_full", (1024, 128)), ("mu", (24,)),
    ):
        I[name] = din(name, shape)
    O = k.O = {}
    for name, shape in (
        ("yo", (NOWN, D)), ("nk", (512, D)), ("nv", (512, D)),
        ("nC", (2, 2, 4, 256, 256)), ("nn", (2, 2, 4, 256)), ("nm", (2, 2, 4)),
    ):
        O[name] = dout(name, shape)

    P = k.P = Prog(nc)
    with ExitStack() as st:
        P.start(st)
        k.st = st
        emit_all(k)
        P.finish()
    return nc, k


def emit_all(k):
    nc, P, st, I, O = k.nc, k.P, k.st, k.I, k.O

    def sb(name, shape, dt=F32, stack=None):
        t = (stack or st).enter_context(nc.sbuf_tensor(name, list(shape), dt))
        return t, Buf(name)

    k.sb = sb

    def dbg_dump(name, tile_ap, buf, shape):
        if name in k.dbg:
            o = nc.dram_tensor("dbg_" + name, list(shape), F32, kind="ExternalOutput").ap()
            k.dbg_out[name] = o
            if tile_ap.dtype != F32:
                for a in range(shape[1]):
                    sg, b_sg = k.stage[a % 2]
                    P.op("act", lambda a=a, sg=sg: nc.scalar.copy(sg[:, 0:shape[2]], tile_ap[:, a, :]),
                         reads=[buf], writes=[b_sg])
                    P.dma("sp", o[:, a, :], sg[:, 0:shape[2]], reads=[b_sg], is_output=True)
            else:
                P.dma("sp", o, tile_ap, reads=[buf], is_output=True)

    k.dbg_dump = dbg_dump

    k.ps = []
    for i in range(6):
        t = st.enter_context(nc.psum_tensor("ps%d" % i, [128, 512], F32))
        k.ps.append((t, Buf("ps%d" % i, psum=True)))
    k.ps_i = 0
    k.pst = []
    for i in range(2):
        t = st.enter_context(nc.psum_tensor("pst%d" % i, [128, 1024], BF16))
        k.pst.append((t, Buf("pst%d" % i, psum=True)))
    k.pst_i = 0

    k.ps_reserved = set()

    def next_ps():
        while True:
            i = k.ps_i % 6
            k.ps_i += 1
            if i not in k.ps_reserved:
                return k.ps[i]

    def next_pst():
        t, b = k.pst[k.pst_i % 2]
        k.pst_i += 1
        return t, b

    k.next_pst = next_pst

    k.next_ps = next_ps

    idf, b_idf = sb("idf", [128, 128])
    P.op("pool", lambda: nc.gpsimd.memset(idf[:], 1.0), writes=[b_idf])
    P.op("pool", lambda: nc.gpsimd.affine_select(idf[:], idf[:], [[-1, 128]], ALU.is_equal, 0.0,
                                                 base=0, channel_multiplier=1),
         reads=[b_idf], writes=[b_idf])
    idb, b_idb = sb("idb", [128, 128], BF16)
    P.op("dve", lambda: nc.vector.tensor_copy(idb[:], idf[:]), reads=[b_idf], writes=[b_idb])
    onesf, b_ones = sb("onesf", [128, 128])
    P.op("pool", lambda: nc.gpsimd.memset(onesf[:], 1.0), writes=[b_ones])
    maskf, b_maskf = sb("maskf", [128, 128])
    P.op("pool", lambda: nc.gpsimd.memset(maskf[:], 1.0), writes=[b_maskf])
    P.op("pool", lambda: nc.gpsimd.affine_select(maskf[:], maskf[:], [[1, 128]], ALU.is_ge, 0.0,
                                                 base=0, channel_multiplier=-1),
         reads=[b_maskf], writes=[b_maskf])
    maskb, b_maskb = sb("maskb", [128, 128])
    P.op("pool", lambda: nc.gpsimd.memset(maskb[:], 1.0), writes=[b_maskb])
    P.op("pool", lambda: nc.gpsimd.affine_select(maskb[:], maskb[:], [[-1, 128]], ALU.is_ge, 0.0,
                                                 base=0, channel_multiplier=1),
         reads=[b_maskb], writes=[b_maskb])
    ntrif, b_ntrif = sb("ntrif", [128, 128])
    P.op("dve", lambda: nc.vector.tensor_scalar(ntrif[:], maskf[:], -1.0, None, ALU.mult),
         reads=[b_maskf], writes=[b_ntrif])
    ntrib, b_ntrib = sb("ntrib", [128, 128])
    P.op("dve", lambda: nc.vector.tensor_scalar(ntrib[:], maskb[:], -1.0, None, ALU.mult),
         reads=[b_maskb], writes=[b_ntrib])
    onesb, b_onesb = sb("onesb", [128, 128], BF16)
    P.op("dve", lambda: nc.vector.tensor_copy(onesb[:], onesf[:]), reads=[b_ones], writes=[b_onesb])
    epst, b_eps = sb("epst", [128, 1])
    P.op("dve", lambda: nc.vector.memset(epst[:], EPS), writes=[b_eps])
    k.c = dict(onesb=(onesb, b_onesb), idf=(idf, b_idf), idb=(idb, b_idb), ones=(onesf, b_ones), maskf=(maskf, b_maskf),
               maskb=(maskb, b_maskb), ntrif=(ntrif, b_ntrif), ntrib=(ntrib, b_ntrib),
               eps=(epst, b_eps))

    k.NRING = 4
    k.ring = [sb("wring%d" % i, [128, 4096], BF16) for i in range(k.NRING)]
    k.ring_i = 0

    k.stage = [sb("stage%d" % i, [128, D]) for i in range(2)]
    k.nm = ([sb("nm_sq%d" % i, [128, 512]) for i in range(2)], sb("nm_rstd", [128, 512]),
            [sb("nm_tmp%d" % i, [128, 512]) for i in range(2)])
    k.hh2f = sb("hh2full", [128, KC, NFULL], BF16)
    k.xT = sb("xT", [128, KC, NOWN])
    k.pre_tiles = {}
    alloc_adaln(k)
    emit_adaln_pre(k)
    if "full" not in k.stages:
        emit_load_x(k)
    if "full" in k.stages:
        with ExitStack() as ph:
            emit_ffn_full(k, ph)
        if "ffn1" in k.stages:
            k.pre_tiles[0] = ffn_load_group(k, I["f1w1"], I["f1w3"], 0)
        P.barrier()
    else:
        for _ in emit_adaln(k):
            pass
    if "ffn1" in k.stages:
        with ExitStack() as ph:
            emit_ffn(k, ph, 0)
        P.barrier()
    if "mixer" in k.stages:
        with ExitStack() as ph:
            emit_mixer(k, ph)
        if "ffn2" in k.stages:
            k.pre_tiles[2] = ffn_load_group(k, I["f2w1"], I["f2w3"], 0)
        P.barrier()
    if "ffn2" in k.stages:
        with ExitStack() as ph:
            emit_ffn(k, ph, 2)
        P.barrier()
    emit_store_x(k)


def ring_load(k, view_shape_fn, dram_ap):
    t, b = k.ring[k.ring_i % k.NRING]
    k.ring_i += 1
    dst = view_shape_fn(t)
    k.P.dma("pool", dst, dram_ap, writes=[b])
    return t, b


def alloc_adaln(k):
    k.ad = {}
    for name, shape, dt in (("craw", [16, 128], F32), ("silu_c", [128, 16], BF16), ("braw", [72, 128], F32),
                            ("bada", [128, 72], F32), ("graw", [24, 128], F32), ("gnorm", [128, 24], F32),
                            ("mods", [128, 72, 2], F32), ("modA", [128, 3, KC, 2], F32), ("modB", [128, 3, KC, 2], F32),
                            ("modG", [128, 3, KC, 2], F32)):
        k.ad[name] = k.sb(name, shape, dt)


def emit_adaln_pre(k):
    nc, P, I = k.nc, k.P, k.I
    sb = lambda name, shape, dt=F32: k.ad[name]
    idf, b_idf = k.c["idf"]
    craw, b_craw = sb("craw", [16, 128])
    P.dma("sp", craw[:], I["cv"].rearrange("w (kc p) -> (w kc) p", p=128), writes=[b_craw])
    ps, b_ps = k.next_ps()
    P.op("pe", lambda: nc.tensor.transpose(ps[:, 0:16], craw[:], idf[0:16, 0:16]),
         reads=[b_craw, b_idf], writes=[b_ps])
    sc, b_sc = sb("silu_c", [128, 16], BF16)
    P.op("act", lambda: nc.scalar.activation(out=sc[:], in_=ps[:, 0:16], func=AF.Silu),
         reads=[b_ps], writes=[b_sc])
    braw, b_braw = sb("braw", [72, 128])
    P.dma("sp", braw[:], I["b_ada"], writes=[b_braw])
    ps2, b_ps2 = k.next_ps()
    P.op("pe", lambda: nc.tensor.transpose(ps2[:, 0:72], braw[:], idf[0:72, 0:72]),
         reads=[b_braw, b_idf], writes=[b_ps2])
    bada, b_bada = sb("bada", [128, 72])
    P.op("dve", lambda: nc.vector.tensor_copy(bada[:], ps2[:, 0:72]), reads=[b_ps2], writes=[b_bada])
    graw, b_graw = sb("graw", [24, 128])
    P.dma("sp", graw[:], I["g_norm"], writes=[b_graw])
    ps3, b_ps3 = k.next_ps()
    P.op("pe", lambda: nc.tensor.transpose(ps3[:, 0:24], graw[:], idf[0:24, 0:24]),
         reads=[b_graw, b_idf], writes=[b_ps3])
    gn, b_gn = sb("gnorm", [128, 24])
    P.op("dve", lambda: nc.vector.tensor_copy(gn[:], ps3[:, 0:24]), reads=[b_ps3], writes=[b_gn])

    k.ad_tmp = dict(sc=(sc, b_sc), bada=(bada, b_bada), gn=(gn, b_gn))


def emit_adaln(k):
    nc, P, I = k.nc, k.P, k.I
    sb = lambda name, shape, dt=F32: k.ad[name]
    sc, b_sc = k.ad_tmp["sc"]
    bada, b_bada = k.ad_tmp["bada"]
    gn, b_gn = k.ad_tmp["gn"]
    pm, b_pm = k.ps[5]
    k.ps_reserved.add(5)
    wada = I["w_ada"].rearrange("(kc p) n -> p kc n", p=128)
    scv = sc[:].rearrange("p (w kc) -> p kc w", w=2)
    for g in range(18):
        wt, b_wt = ring_load(k, lambda t: t[:].rearrange("p (kc n) -> p kc n", kc=KC),
                             wada[:, :, g * 512:(g + 1) * 512])
        wv = wt[:].rearrange("p (kc n) -> p kc n", kc=KC)

        def mm(wv=wv, g=g):
            inst = None
            for j in range(4):
                oc = g * 4 + j
                for kc in range(KC):
                    inst = nc.tensor.matmul(pm[:, 2 * oc:2 * oc + 2], wv[:, kc, j * 128:(j + 1) * 128],
                                            scv[:, kc, :], start=(kc == 0), stop=(kc == KC - 1))
            return inst
        yield P.op("pe", mm, reads=[b_wt, b_sc], writes=[b_pm])
    mods, b_mods = sb("mods", [128, 72, 2])
    P.op("dve", lambda: nc.vector.tensor_tensor(
        mods[:], pm[:, 0:144].rearrange("p (oc w) -> p oc w", w=2),
        bada[:].unsqueeze(2).to_broadcast([128, 72, 2]), ALU.add),
        reads=[b_pm, b_bada], writes=[b_mods])
    k.ps_reserved.discard(5)
    mA, b_mA = sb("modA", [128, 3, KC, 2])
    mB, b_mB = sb("modB", [128, 3, KC, 2])
    mG, b_mG = sb("modG", [128, 3, KC, 2])
    for i in range(3):
        def fA(i=i):
            return nc.vector.scalar_tensor_tensor(
                out=mA[:, i], in0=mods[:, (3 * i + 1) * 8:(3 * i + 2) * 8, :], scalar=1.0,
                in1=gn[:, i * 8:(i + 1) * 8].unsqueeze(2).to_broadcast([128, KC, 2]),
                op0=ALU.add, op1=ALU.mult)
        P.op("dve", fA, reads=[b_mods, b_gn], writes=[b_mA])
        P.op("dve", lambda i=i: nc.vector.tensor_copy(mB[:, i], mods[:, (3 * i) * 8:(3 * i + 1) * 8, :]),
             reads=[b_mods], writes=[b_mB])
        gs = 1.0 if i == 1 else 0.5
        P.op("dve", lambda i=i, gs=gs: nc.vector.tensor_scalar(
            mG[:, i], mods[:, (3 * i + 2) * 8:(3 * i + 3) * 8, :], gs, None, ALU.mult),
            reads=[b_mods], writes=[b_mG])
    k.mod = dict(A=(mA, b_mA), B=(mB, b_mB), G=(mG, b_mG))
    k.dbg_dump("mods", mods[:], b_mods, [128, 72, 2])


def load_tokens_T(k, dram, ntok, xT, b_xT, tag):
    for _ in load_tokens_gen(k, dram, ntok, xT, b_xT, tag):
        pass


def load_tokens_gen(k, dram, ntok, xT, b_xT, tag):
    nc, P = k.nc, k.P
    idf, b_idf = k.c["idf"]
    if not hasattr(k, "stage"):
        k.stage = [k.sb("stage%d" % i, [128, D]) for i in range(2)]
    stg = k.stage
    for t in range(ntok // 128):
        xs, b_xs = stg[t % 2]
        P.dma("sp", xs[:], dram[t * 128:(t + 1) * 128, :], writes=[b_xs])
        for half in range(2):
            ps, b_ps = k.next_ps()

            def tr(ps=ps, xs=xs, half=half):
                inst = None
                for j in range(4):
                    kc = half * 4 + j
                    inst = nc.tensor.transpose(ps[:, j * 128:(j + 1) * 128], xs[:, kc * 128:(kc + 1) * 128],
                                               idf[:])
                return inst
            P.op("pe", tr, reads=[b_xs, b_idf], writes=[b_ps])
            eng = "dve" if half == 0 else "act"

            def cp(ps=ps, half=half, t=t, eng=eng):
                dst = xT[:, half * 4:(half + 1) * 4, t * 128:(t + 1) * 128]
                src = ps[:].rearrange("p (j n) -> p j n", j=4)
                if eng == "dve":
                    return nc.vector.tensor_copy(dst, src)
                return nc.scalar.copy(dst, src)
            yield P.op(eng, cp, reads=[b_ps], writes=[b_xT])


def emit_load_x(k):
    xT, b_xT = k.xT
    load_tokens_T(k, k.I["xo"], NOWN, xT, b_xT, "o")


def emit_store_x(k):
    nc, P = k.nc, k.P
    xT, b_xT = k.xT
    idf, b_idf = k.c["idf"]
    stg = k.stage
    for t in range(NOWN // 128):
        ys, b_ys = stg[t % 2]
        for half in range(2):
            ps, b_ps = k.next_ps()

            def tr(ps=ps, half=half, t=t):
                inst = None
                for j in range(4):
                    kc = half * 4 + j
                    inst = nc.tensor.transpose(ps[:, j * 128:(j + 1) * 128],
                                               xT[:, kc, t * 128:(t + 1) * 128], idf[:])
                return inst
            P.op("pe", tr, reads=[b_xT, b_idf], writes=[b_ps])
            eng = "dve" if half == 0 else "act"

            def cp(ps=ps, half=half, ys=ys, eng=eng):
                dst = ys[:, half * 512:(half + 1) * 512]
                if eng == "dve":
                    return nc.vector.tensor_copy(dst, ps[:])
                return nc.scalar.copy(dst, ps[:])
            P.op(eng, cp, reads=[b_ps], writes=[b_ys])
        P.dma("sp", k.O["yo"][t * 128:(t + 1) * 128, :], ys[:], reads=[b_ys], is_output=True)


def norm_mod(k, ph, xT, b_xT, groups, ni, hh, b_hh, tag):
    nc, P = k.nc, k.P
    onesb, b_onesb = k.c["onesb"]
    epst, b_eps = k.c["eps"]
    mA, b_mA = k.mod["A"]
    mB, b_mB = k.mod["B"]
    sqs, (rstd, b_rstd), tmps = k.nm
    scr = [sqs[0], sqs[1], tmps[0], tmps[1]]
    for (t0, n, which) in groups:
        ps, b_ps = k.next_ps()
        for i in range(4):
            sq, b_sq = scr[i]
            sqv = sq[:].bitcast(BF16).rearrange("p (c n) -> p c n", c=2)[:, :, 0:n]
            P.op("act", lambda i=i, sqv=sqv: nc.scalar.activation(out=sqv, in_=xT[:, 2 * i:2 * i + 2, t0:t0 + n], func=AF.Square),
                 reads=[b_xT], writes=[b_sq])

            def mm(i=i, sqv=sqv):
                nc.tensor.matmul(ps[:, 0:n], onesb[:], sqv[:, 0, :], start=(i == 0), stop=False)
                return nc.tensor.matmul(ps[:, 0:n], onesb[:], sqv[:, 1, :], start=False, stop=(i == 3))
            P.op("pe", mm, reads=[b_sq, b_onesb], writes=[b_ps])
        P.op("act", lambda: nc.scalar.activation(out=rstd[:, 0:n], in_=ps[:, 0:n], func=AF.Ln,
                                                 scale=1.0 / D, bias=epst[:, 0:1]),
             reads=[b_ps, b_eps], writes=[b_rstd])
        P.op("act", lambda: nc.scalar.activation(out=rstd[:, 0:n], in_=rstd[:, 0:n], func=AF.Exp, scale=-0.5),
             reads=[b_rstd], writes=[b_rstd])
        hv = hh[:, :, t0:t0 + n]
        P.op("dve", lambda: nc.vector.tensor_tensor(hv, xT[:, :, t0:t0 + n], rstd[:, 0:n].unsqueeze(1).to_broadcast([128, KC, n]),
                                                    ALU.mult), reads=[b_xT, b_rstd], writes=[b_hh])
        P.op("dve", lambda: nc.vector.tensor_tensor(hv, hv, mA[:, ni, :, which].unsqueeze(2).to_broadcast([128, KC, n]), ALU.mult),
             reads=[b_hh, b_mA], writes=[b_hh])
        P.op("dve", lambda: nc.vector.tensor_tensor(hv, hv, mB[:, ni, :, which].unsqueeze(2).to_broadcast([128, KC, n]), ALU.add),
             reads=[b_hh, b_mB], writes=[b_hh])


def ffn_load_group(k, w1, w3, cg):
    ncol = 512 if cg < 5 else 256
    tiles = []
    for w in (w1, w3):
        wv = w.rearrange("(kc p) n -> p kc n", p=128)
        t, b = ring_load(k, lambda t, ncol=ncol: t[:, 0:KC * ncol].rearrange("p (kc n) -> p kc n", kc=KC),
                         wv[:, :, cg * 512:cg * 512 + ncol])
        tiles.append((t[:, 0:KC * ncol].rearrange("p (kc n) -> p kc n", kc=KC), b))
    return tiles


def ffn_core(k, ph, xT, b_xT, hh, b_hh, groups, w1, w3, w2, ni, tag, tiles0=None):
    nc, P = k.nc, k.P
    mG, b_mG = k.mod["G"]
    ntok = sum(g[1] for g in groups)
    tb = groups[0][0]
    gT, b_gT = k.sb("ffn_g" + tag, [128, NFF, ntok], BF16, stack=ph)
    sils = [k.sb("ffn_sil%s%d" % (tag, i), [128, 512], stack=ph) for i in range(2)]
    sil_i = [0]
    w2t = [k.sb("ffn_w2%s%d" % (tag, i), [128, NFF, 256], BF16, stack=ph) for i in range(2)]
    w2v = w2.rearrange("(fc p) n -> p fc n", p=128)
    for cg in range(6):
        ncol = 512 if cg < 5 else 256
        tiles = tiles0 if (cg == 0 and tiles0 is not None) else ffn_load_group(k, w1, w3, cg)
        if cg == 3:
            for q in range(2):
                P.dma("pool", w2t[q][0][:], w2v[:, :, q * 256:(q + 1) * 256], writes=[w2t[q][1]])
        for j in range(ncol // 128):
            fc = cg * 4 + j
            for (t0, n, which) in groups:
                pss = []
                for (wv, b_w) in tiles:
                    ps, b_ps = k.next_ps()

                    def mm(ps=ps, wv=wv, j=j, t0=t0, n=n):
                        inst = None
                        for kc in range(KC):
                            inst = nc.tensor.matmul(ps[:, 0:n], wv[:, kc, j * 128:(j + 1) * 128],
                                                    hh[:, kc, t0:t0 + n], start=(kc == 0), stop=(kc == KC - 1))
                        return inst
                    P.op("pe", mm, reads=[b_w, b_hh], writes=[b_ps])
                    pss.append((ps, b_ps))
                (p1, b_p1), (p3, b_p3) = pss
                sil, b_sil = sils[sil_i[0] % 2]
                sil_i[0] += 1
                P.op("act", lambda p1=p1, n=n, sil=sil: nc.scalar.activation(out=sil[:, 0:n], in_=p1[:, 0:n], func=AF.Silu),
                     reads=[b_p1], writes=[b_sil])
                P.op("dve", lambda p3=p3, fc=fc, t0=t0, n=n, sil=sil: nc.vector.tensor_tensor(
                    gT[:, fc, t0 - tb:t0 - tb + n], sil[:, 0:n], p3[:, 0:n], ALU.mult),
                    reads=[b_sil, b_p3], writes=[b_gT])
    for q in range(4):
        wt, b_wt = w2t[q % 2]
        if q >= 2:
            P.dma("pool", wt[:], w2v[:, :, q * 256:(q + 1) * 256], writes=[b_wt])
        for j in range(2):
            dc = q * 2 + j
            for (t0, n, which) in groups:
                ps, b_ps = k.next_ps()

                def mm(ps=ps, wt=wt, j=j, t0=t0, n=n):
                    inst = None
                    for fc in range(NFF):
                        inst = nc.tensor.matmul(ps[:, 0:n], wt[:, fc, j * 128:(j + 1) * 128],
                                                gT[:, fc, t0 - tb:t0 - tb + n], start=(fc == 0), stop=(fc == NFF - 1))
                    return inst
                P.op("pe", mm, reads=[b_wt, b_gT], writes=[b_ps])
                P.op("dve", lambda ps=ps, dc=dc, t0=t0, n=n, which=which: nc.vector.scalar_tensor_tensor(
                    out=xT[:, dc, t0:t0 + n], in0=ps[:, 0:n], scalar=mG[:, ni, dc, which:which + 1],
                    in1=xT[:, dc, t0:t0 + n], op0=ALU.mult, op1=ALU.add),
                    reads=[b_ps, b_mG, b_xT], writes=[b_xT])


OWN_GROUPS = [(0, 512, 0), (512, 256, 1)]
FULL_GROUPS = [(0, 512, 1), (512, 256, 1)]


def emit_ffn(k, ph, ni):
    I = k.I
    xT, b_xT = k.xT
    hh, b_hh = k.sb("hh_own%d" % ni, [128, KC, NOWN], BF16, stack=ph)
    pre = "f1" if ni == 0 else "f2"
    tiles0 = k.pre_tiles.pop(ni, None)
    if tiles0 is None:
        tiles0 = ffn_load_group(k, I[pre + "w1"], I[pre + "w3"], 0)
    norm_mod(k, ph, xT, b_xT, OWN_GROUPS, ni, hh, b_hh, "o%d" % ni)
    ffn_core(k, ph, xT, b_xT, hh, b_hh, OWN_GROUPS, I[pre + "w1"], I[pre + "w3"], I[pre + "w2"], ni,
             "o%d" % ni, tiles0=tiles0)
    if ni == 0:
        k.dbg_dump("x1", xT[:], b_xT, [128, KC, NOWN])


def emit_ffn_full(k, ph):
    I = k.I
    hh2f, b_hh2f = k.hh2f
    xf, b_xf = k.sb("xTfull", [128, KC, NFULL], stack=ph)
    xT, b_xT = k.xT

    def loads():
        yield from load_tokens_gen(k, I["xo"], NOWN, xT, b_xT, "o")
        yield from load_tokens_gen(k, I["xf"], NFULL, xf, b_xf, "f")
    run_lanes([emit_adaln(k), loads()], [1, 2])
    hh, b_hh = k.sb("hh_full", [128, KC, NFULL], BF16, stack=ph)
    norm_mod(k, ph, xf, b_xf, FULL_GROUPS, 0, hh, b_hh, "f0")
    ffn_core(k, ph, xf, b_xf, hh, b_hh, FULL_GROUPS, I["f1w1"], I["f1w3"], I["f1w2"], 0, "f")
    norm_mod(k, ph, xf, b_xf, FULL_GROUPS, 1, hh2f, b_hh2f, "f1")
    k.dbg_dump("hh2f", hh2f[:], b_hh2f, [128, KC, NFULL])


def slot_p(u, d, step):
    return step * 4 + u * 2 + d


def slot_scan(d, j):
    return 8 + j * 2 + d


def slot_s(d, step):
    return 20 + step * 2 + d


NSLOT = 24


def emit_gates(k, ph_keep, ph, hh2, b_hh2):
    nc, P, I, sb = k.nc, k.P, k.I, k.sb
    idf, b_idf = k.c["idf"]
    onesf, b_ones = k.c["ones"]
    hh2f, b_hh2f = k.hh2f
    wg, b_wg = sb("wg", [128, KC, 16], BF16, stack=ph)
    P.dma("pool", wg[:], I["w_in"].rearrange("(kc p) n -> p kc n", p=128)[:, :, O_MG:O_MG + 16], writes=[b_wg])
    bg, b_bg = sb("bgate", [128, 16], stack=ph)
    P.dma("sp", bg[:], I["b_gate"][0, :].partition_broadcast(128), writes=[b_bg])
    mu, b_mu = sb("mu_sb", [128, 24], stack=ph)
    P.dma("sp", mu[:], I["mu"].partition_broadcast(128), writes=[b_mu])
    mneg = []
    for d, src in ((0, "maskb"), (1, "maskf")):
        mt, b_mt = sb("mneg%d" % d, [128, 128], stack=ph)
        sm_, b_sm = k.c[src]
        P.op("dve", lambda mt=mt, sm_=sm_: nc.vector.tensor_scalar(mt[:], sm_[:], -1.0, 1.0e30, ALU.add, ALU.mult),
             reads=[b_sm], writes=[b_mt])
        mneg.append((mt, b_mt))
    gall, b_gall = sb("gall", [128, 12, 16], stack=ph)
    for T in range(12):
        if T < 6:
            src, b_src, t0 = hh2, b_hh2, T * 128
        else:
            src, b_src, t0 = hh2f, b_hh2f, (T - 6) * 128
        ps, b_ps = k.next_ps()

        def mm(ps=ps, src=src, t0=t0):
            inst = None
            for kc in range(KC):
                inst = nc.tensor.matmul(ps[:, 0:16], src[:, kc, t0:t0 + 128], wg[:, kc, :],
                                        start=(kc == 0), stop=(kc == KC - 1))
            return inst
        P.op("pe", mm, reads=[b_src, b_wg], writes=[b_ps])
        P.op("dve", lambda ps=ps, T=T: nc.vector.tensor_tensor(gall[:, T, :], ps[:, 0:16], bg[:], ALU.add),
             reads=[b_ps, b_bg], writes=[b_gall])
    gf, b_gf = sb("gf", [128, 12, 8], stack=ph)
    gview = gall[:].rearrange("p t (d g h) -> p t d g h", d=2, g=2)
    P.op("act", lambda: nc.scalar.activation(out=gf[:].rearrange("p t (d h) -> p t d h", d=2), in_=gview[:, :, :, 1, :],
                                             func=AF.Exp, scale=-1.0),
         reads=[b_gall], writes=[b_gf])
    P.op("act", lambda: nc.scalar.activation(out=gf[:], in_=gf[:], func=AF.Ln, bias=1.0),
         reads=[b_gf], writes=[b_gf])
    P.op("dve", lambda: nc.vector.tensor_scalar(gf[:], gf[:], -1.0, None, ALU.mult),
         reads=[b_gf], writes=[b_gf])
    R1, b_R1 = sb("R1", [128, 12, 2, 16], stack=ph)
    R2, b_R2 = sb("R2", [128, 12, 2, 16], stack=ph)
    P.op("pool", lambda: nc.gpsimd.memset(R1[:], 0.0), writes=[b_R1])
    P.op("pool", lambda: nc.gpsimd.memset(R2[:], 0.0), writes=[b_R2])
    P.op("dve", lambda: nc.vector.tensor_copy(R1[:, :, :, 0:4], gview[:, :, :, 0, :]),
         reads=[b_gall], writes=[b_R1])
    for o in (0, 4):
        P.op("dve", lambda o=o: nc.vector.tensor_copy(R2[:, :, :, o:o + 4],
                                                      gf[:].rearrange("p t (d h) -> p t d h", d=2)),
             reads=[b_gf], writes=[b_R2])
    R1s, b_R1s = sb("R1s", [128, 12, 16], stack=ph)
    R2s, b_R2s = sb("R2s", [128, 12, 16], stack=ph)
    P.op("pool", lambda: nc.gpsimd.memset(R1s[:], 0.0), writes=[b_R1s])
    P.op("pool", lambda: nc.gpsimd.memset(R2s[:], 0.0), writes=[b_R2s])
    for d in range(2):
        for j in range(6):
            ft = j if d == 0 else 5 - j
            idx = d * 6 + j
            P.op("dve", lambda d=d, ft=ft, idx=idx: nc.vector.tensor_scalar(
                R1s[:, idx, 0:4], R1[:, 6 + ft, d, 0:4], mu[:, idx:idx + 1], mu[:, 12 + idx:13 + idx],
                ALU.mult, ALU.add), reads=[b_R1, b_mu], writes=[b_R1s])
            P.op("dve", lambda d=d, ft=ft, idx=idx: nc.vector.tensor_scalar(
                R2s[:, idx, 0:8], R2[:, 6 + ft, d, 0:8], mu[:, idx:idx + 1], None, ALU.mult),
                reads=[b_R2, b_mu], writes=[b_R2s])

    ecol, b_ecol = k.gate_keep[0]
    P.op("pool", lambda: nc.gpsimd.memset(ecol[:], 0.0), writes=[b_ecol])
    dmat, b_dmat = k.gate_keep[1]
    ms, b_ms = k.gate_keep[2]
    P.op("dve", lambda: nc.vector.memset(ms[:], 0.0), writes=[b_ms])
    with nc.allow_non_contiguous_dma(reason="tiny state_m load"):
        P.dma("sp", ms[:, NSLOT + 1:NSLOT + 3], I["sm"].rearrange("d h -> h d"), writes=[b_ms])
    i4, b_i4 = sb("i4rep", [4, 16], stack=ph)
    for o in (0, 4, 8, 12):
        P.op("dve", lambda o=o: nc.vector.tensor_copy(i4[:, o:o + 4], idf[0:4, 0:4]), reads=[b_idf], writes=[b_i4])
    def make_lane(tag, banks):
        L = {}
        for nm, shape in (("gs", [4, 8, 3]), ("dg", [4, 3, 16]), ("cdiag", [4, 4, 128]), ("ldm", [128, 4, 128]),
                          ("nbm", [128, 8]), ("uu", [128, 4]), ("t8", [128, 8])):
            L[nm] = sb("g_%s_%s" % (nm, tag), shape, stack=ph)
        dg_, b_dg_ = L["dg"]
        P.op("dve", lambda: nc.vector.memset(dg_[:], 0.0), writes=[b_dg_])
        L["prow"], L["pcol"], L["cb"] = (k.ps[b] for b in banks)
        return L

    k.ecol, k.ms, k.dmat = (ecol, b_ecol), (ms, b_ms), (dmat, b_dmat)

    def gate_batch(insts, L):
        n = len(insts)
        gs, b_gs = L["gs"]
        dg, b_dg = L["dg"]
        cdiag, b_cdiag = L["cdiag"]
        ldm, b_ldm = L["ldm"]
        nbm, b_nbm = L["nbm"]
        uu, b_uu = L["uu"]
        t8, b_t8 = L["t8"]
        prow, b_prow = L["prow"]
        for i, (r1, r2, rb, d, mcol, slot, oi) in enumerate(insts):
            ntri, b_ntri = k.c["ntrif" if d == 0 else "ntrib"]

            def mm(i=i, r1=r1, r2=r2, ntri=ntri):
                nc.tensor.matmul(prow[0:4, i * 128:(i + 1) * 128], r1[:, 0:4], idf[:], start=True, stop=False)
                nc.tensor.matmul(prow[0:4, i * 128:(i + 1) * 128], r2[:, 0:4], ntri[:], start=False, stop=True)
                return nc.tensor.matmul(prow[0:4, 384 + i:385 + i], r2[:, 0:4], onesf[:, 0:1], start=True, stop=True)
            yield P.op("pe", mm, reads=list(rb) + [b_idf, b_ntri, b_ones], writes=[b_prow])
        yield P.op("dve", lambda: nc.vector.tensor_reduce(gs[:, 0:n, 0], prow[0:4, 0:n * 128].rearrange("p (i t) -> p i t", i=n),
                                                    AX.X, ALU.max), reads=[b_prow], writes=[b_gs])
        mcol0, slot0 = insts[0][4], insts[0][5]
        allz = all(x[4] == NSLOT for x in insts)
        assert allz or all(x[4] == mcol0 + i for i, x in enumerate(insts)), insts
        assert all(x[5] == slot0 + i for i, x in enumerate(insts))
        msin = ms[:, NSLOT:NSLOT + 1].to_broadcast([4, n]) if allz else ms[:, mcol0:mcol0 + n]
        msin3 = (ms[:, NSLOT:NSLOT + 1] if allz else ms[:, mcol0:mcol0 + n]).unsqueeze(2).to_broadcast([4, n, 4])
        i4b = i4[:, 0:4].unsqueeze(1).to_broadcast([4, n, 4])
        yield P.op("dve", lambda: nc.vector.tensor_tensor(gs[:, 0:n, 0], gs[:, 0:n, 0], msin, ALU.max), reads=[b_gs, b_ms], writes=[b_gs])
        yield P.op("dve", lambda: nc.vector.tensor_tensor(ms[:, slot0:slot0 + n], prow[0:4, 384:384 + n], gs[:, 0:n, 0], ALU.add),
                   reads=[b_prow, b_gs, b_ms], writes=[b_ms])
        yield P.op("dve", lambda: nc.vector.tensor_scalar(gs[:, 0:n, 1], gs[:, 0:n, 0], -1.0, None, ALU.mult), reads=[b_gs], writes=[b_gs])
        yield P.op("dve", lambda: nc.vector.tensor_tensor(gs[:, 0:n, 2], msin, gs[:, 0:n, 0], ALU.subtract), reads=[b_gs, b_ms], writes=[b_gs])
        yield P.op("dve", lambda: nc.vector.tensor_tensor(dg[:, 0:n, 0:4], i4b, gs[:, 0:n, 1:2].to_broadcast([4, n, 4]), ALU.mult),
                   reads=[b_gs, b_i4], writes=[b_dg])
        yield P.op("dve", lambda: nc.vector.tensor_tensor(dg[:, 0:n, 8:12], i4b, msin3, ALU.mult), reads=[b_ms, b_i4], writes=[b_dg])
        yield P.op("dve", lambda: nc.vector.tensor_tensor(dg[:, 0:n, 12:16], i4b, gs[:, 0:n, 2:3].to_broadcast([4, n, 4]), ALU.mult),
                   reads=[b_gs, b_i4], writes=[b_dg])
        pcol, b_pcol = L["pcol"]
        for i, (r1, r2, rb, d, mcol, slot, oi) in enumerate(insts):
            ntri, b_ntri = k.c["ntrif" if d == 0 else "ntrib"]

            def mm2(i=i, r1=r1, r2=r2, ntri=ntri):
                nc.tensor.matmul(pcol[:, i * 16:(i + 1) * 16], idf[:], r1, start=True, stop=False)
                nc.tensor.matmul(pcol[:, i * 16:(i + 1) * 16], ntri[:], r2, start=False, stop=False)
                return nc.tensor.matmul(pcol[:, i * 16:(i + 1) * 16], onesf[0:4, :], dg[:, i, :], start=False, stop=True)
            yield P.op("pe", mm2, reads=list(rb) + [b_idf, b_ntri, b_ones, b_dg], writes=[b_pcol])
        for i, (r1, r2, rb, d, mcol, slot, oi) in enumerate(insts):
            yield P.op("act", lambda i=i, slot=slot: nc.scalar.activation(out=ecol[:, slot, 0:4], in_=pcol[:, i * 16:i * 16 + 4],
                                                                    func=AF.Exp), reads=[b_pcol], writes=[b_ecol])
            yield P.op("act", lambda i=i, slot=slot: nc.scalar.activation(out=ecol[:, slot, 4:8], in_=pcol[:, i * 16 + 12:i * 16 + 16],
                                                                    func=AF.Exp), reads=[b_pcol], writes=[b_ecol])
            if oi is None:
                continue
            yield P.op("dve", lambda i=i: nc.vector.tensor_copy(nbm[:], pcol[:, i * 16 + 4:i * 16 + 12]), reads=[b_pcol], writes=[b_nbm])
            yield P.op("dve", lambda i=i: nc.vector.tensor_tensor(
                cdiag[:], prow[0:4, i * 128:(i + 1) * 128].unsqueeze(1).to_broadcast([4, 4, 128]),
                i4[:, 0:4].unsqueeze(2).to_broadcast([4, 4, 128]), ALU.mult), reads=[b_prow, b_i4], writes=[b_cdiag])
            cb, b_cb = L["cb"]
            yield P.op("pe", lambda cb=cb: nc.tensor.matmul(cb[:, 0:512], onesf[0:4, :], cdiag[:].rearrange("p m s -> p (m s)"),
                                                      start=True, stop=True), reads=[b_ones, b_cdiag], writes=[b_cb])
            mt, b_mt = mneg[d]
            yield P.op("dve", lambda cb=cb, mt=mt: nc.vector.tensor_tensor(
                ldm[:], cb[:, 0:512].rearrange("p (m s) -> p m s", m=4), mt[:].unsqueeze(1).to_broadcast([128, 4, 128]), ALU.add),
                reads=[b_cb, b_mt], writes=[b_ldm])
            yield P.op("dve", lambda: nc.vector.tensor_reduce(uu[:], ldm[:], AX.X, ALU.max), reads=[b_ldm], writes=[b_uu])
            yield P.op("dve", lambda: nc.vector.tensor_tensor(uu[:], uu[:], nbm[:, 4:8], ALU.max), reads=[b_uu, b_nbm], writes=[b_uu])
            yield P.op("dve", lambda: nc.vector.tensor_tensor(ldm[:], ldm[:], uu[:].unsqueeze(2).to_broadcast([128, 4, 128]), ALU.subtract),
                 reads=[b_ldm, b_uu], writes=[b_ldm])
            yield P.op("act", lambda oi=oi: nc.scalar.activation(out=dmat[:, oi], in_=ldm[:], func=AF.Exp), reads=[b_ldm], writes=[b_dmat])
            yield P.op("dve", lambda: nc.vector.tensor_tensor(t8[:, 0:4], nbm[:, 0:4], uu[:], ALU.subtract), reads=[b_nbm, b_uu], writes=[b_t8])
            yield P.op("dve", lambda: nc.vector.tensor_tensor(t8[:, 4:8], nbm[:, 4:8], uu[:], ALU.subtract), reads=[b_nbm, b_uu], writes=[b_t8])
            yield P.op("act", lambda slot=slot: nc.scalar.activation(out=ecol[:, slot, 8:16], in_=t8[:], func=AF.Exp),
                 reads=[b_t8], writes=[b_ecol])

    def own_inst(T, d, mcol, slot):
        return (R1[:, T, d, :], R2[:, T, d, :], (b_R1, b_R2), d, mcol, slot, own_index(slot))

    def scan_inst(d, j, mcol):
        idx = d * 6 + j
        return (R1s[:, idx, :], R2s[:, idx, :], (b_R1s, b_R2s), d, mcol, slot_scan(d, j), None)

    Z = NSLOT
    LP, LS = make_lane("p", (0, 1, 2)), make_lane("s", (3, 4, 5))

    def lane_prompt():
        yield from gate_batch([own_inst(0, 0, Z, slot_p(0, 0, 0)), own_inst(1, 1, Z, slot_p(0, 1, 0)),
                               own_inst(2, 0, Z, slot_p(1, 0, 0))], LP)
        yield from gate_batch([own_inst(3, 1, Z, slot_p(1, 1, 0))], LP)
        yield from gate_batch([own_inst(1, 0, slot_p(0, 0, 0), slot_p(0, 0, 1)), own_inst(0, 1, slot_p(0, 1, 0), slot_p(0, 1, 1)),
                               own_inst(3, 0, slot_p(1, 0, 0), slot_p(1, 0, 1))], LP)
        yield from gate_batch([own_inst(2, 1, slot_p(1, 1, 0), slot_p(1, 1, 1))], LP)

    def lane_sample():
        yield from gate_batch([scan_inst(0, 0, NSLOT + 1), scan_inst(1, 0, NSLOT + 2)], LS)
        for j in range(1, 6):
            yield from gate_batch([scan_inst(0, j, slot_scan(0, j - 1)), scan_inst(1, j, slot_scan(1, j - 1))], LS)
        yield from gate_batch([own_inst(4, 0, slot_scan(0, 5), slot_s(0, 0)), own_inst(5, 1, slot_scan(1, 5), slot_s(1, 0))], LS)
        yield from gate_batch([own_inst(5, 0, slot_s(0, 0), slot_s(0, 1)), own_inst(4, 1, slot_s(1, 0), slot_s(1, 1))], LS)

    run_lanes([lane_prompt(), lane_sample()], [1, 1])
    with nc.allow_non_contiguous_dma(reason="tiny state_m store"):
        for u in range(2):
            for d in range(2):
                sl = slot_p(u, d, 1)
                P.dma("sp", k.O["nm"][u, d, :].unsqueeze(1), ms[:, sl:sl + 1], reads=[b_ms], is_output=True)
    k.dbg_dump("ecol", ecol[:], b_ecol, [128, NSLOT, 16])
    k.dbg_dump("ms", ms[:], b_ms, [4, NSLOT + 3])
    k.dbg_dump("gall", gall[:], b_gall, [128, 12, 16])


def own_index(slot):
    return slot if slot < 8 else slot - 12


class StopEmit(Exception):
    pass


def emit_mlstm(k, ph, hh2, b_hh2, hmT, b_hmT):
    nc, P, I, O, sb = k.nc, k.P, k.I, k.O, k.sb
    idb, b_idb = k.c["idb"]
    epst, b_eps = k.c["eps"]
    hh2f, b_hh2f = k.hh2f
    ecol, b_ecol = k.ecol
    dmat, b_dmat = k.dmat
    win = I["w_in"].rearrange("(kc p) n -> p kc n", p=128)
    gmh, b_gmh = k.stage[0]
    P.dma("sp", gmh[:], I["g_mh"].partition_broadcast(128), writes=[b_gmh])

    def bufset(tag, sample):
        B = {}
        for nm, shape, dt in (("qT", [128, 2, 256], BF16), ("kT", [128, 2, 256], BF16), ("ktok", [128, 2, 256], BF16),
                              ("vb", [128, 2, 272], BF16), ("sgo", [128, 2, 256], BF16),
                              ("cst0", [128, 2, 272], F32), ("cst1", [128, 2, 272], F32), ("csb", [128, 2, 272], BF16),
                              ("vaug0", [128, 272], BF16), ("vaug1", [128, 272], BF16),
                              ("sbf", [128, 128], BF16), ("sT", [128, 128], BF16),
                              ("x2s", [128, 257], F32), ("nums", [128, 272], F32), ("hm", [128, 2, 256], F32),
                              ("hb", [128, 256], BF16), ("sc1", [128, 4], F32),
                              ("kf", [128, 8, 256], BF16), ("vf", [128, 8, 272], BF16)):
            if not sample and nm in ("kf", "vf"):
                continue
            B[nm] = sb("m_%s_%s" % (nm, tag), shape, dt, stack=ph)
        if not sample:
            for nm, shape, dt in (("sbf1", [128, 128], BF16), ("sT1", [128, 128], BF16), ("csb1", [128, 2, 272], BF16),
                                  ("x2s1", [128, 257], F32), ("nums1", [128, 272], F32), ("sc11", [128, 4], F32)):
                B[nm] = sb("m_%s_%s" % (nm, tag), shape, dt, stack=ph)
        B["va_i"] = 0
        if sample:
            for nm in ("y_sbf", "y_sT", "y_csb", "y_x2s", "y_nums", "y_sc1"):
                B[nm] = Buf(nm)
        vb_, b_vb_ = B["vb"]
        P.op("pool", lambda: nc.gpsimd.memset(vb_[:], 1.0), writes=[b_vb_])
        if sample:
            vf_, b_vf_ = B["vf"]
            P.op("pool", lambda: nc.gpsimd.memset(vf_[:], 1.0), writes=[b_vf_])
        return B

    def mlstm_iter(h, u, B, wmA, b_wmA, wmB, b_wmB):
        qT, b_qT = B["qT"]
        kT, b_kT = B["kT"]
        ktok, b_ktok = B["ktok"]
        vb, b_vb = B["vb"]
        sgo, b_sgo = B["sgo"]
        cst = [B["cst0"], B["cst1"]]
        csb, b_csb = B["csb"]
        sbf, b_sbf = B["sbf"]
        sT, b_sT = B["sT"]
        x2s, b_x2s = B["x2s"]
        nums, b_nums = B["nums"]
        hm, b_hm = B["hm"]
        tmp_, b_tmp = B["nums"]
        tmp = tmp_[:, 0:256]
        hb, b_hb = B["hb"]
        sc1, b_sc1 = B["sc1"]
        tok0 = u * 256

        def build_vaug(vsrc, b_vsrc, slot):
            va, b_va = B["vaug%d" % (B["va_i"] % 2)]
            B["va_i"] += 1
            P.op("dve", lambda: nc.vector.tensor_scalar(va[:, 0:257], vsrc, ecol[:, slot, h:h + 1], None, ALU.mult),
                 reads=[b_vsrc, b_ecol], writes=[b_va])
            return va, b_va

        def state_update(d, ksrc_fn, b_ksrc, va, b_va, slot, has_state):
            C, b_C = cst[d]
            for dc in range(2):
                ps, b_ps = k.next_ps()
                P.op("pe", lambda ps=ps, dc=dc: nc.tensor.matmul(ps[:, 0:257], ksrc_fn(dc), va[:, 0:257], start=True, stop=True),
                     reads=[b_ksrc, b_va], writes=[b_ps])
                if has_state:
                    P.op("dve", lambda ps=ps, dc=dc: nc.vector.scalar_tensor_tensor(
                        out=C[:, dc, 0:257], in0=C[:, dc, 0:257], scalar=ecol[:, slot, 4 + h:5 + h], in1=ps[:, 0:257],
                        op0=ALU.mult, op1=ALU.add), reads=[b_C, b_ecol, b_ps], writes=[b_C])
                else:
                    P.op("dve", lambda ps=ps, dc=dc: nc.vector.tensor_copy(C[:, dc, 0:257], ps[:, 0:257]),
                         reads=[b_ps], writes=[b_C])
                yield None

        for (dst, b_dst, wsrc, b_wsrc, scl) in ((qT, b_qT, wmB, b_wmB, 1.0 / 16.0), (kT, b_kT, wmA, b_wmA, 1.0)):
            for dc in range(2):
                ps, b_ps = k.next_ps()

                def mm(ps=ps, wsrc=wsrc, dc=dc):
                    inst = None
                    for kc in range(KC):
                        inst = nc.tensor.matmul(ps[:, 0:256], wsrc[:, 0, kc, dc * 128:(dc + 1) * 128],
                                                hh2[:, kc, tok0:tok0 + 256], start=(kc == 0), stop=(kc == KC - 1))
                    return inst
                P.op("pe", mm, reads=[b_wsrc, b_hh2], writes=[b_ps])
                yield P.op("act", lambda ps=ps, dst=dst, dc=dc, scl=scl: nc.scalar.activation(
                    out=dst[:, dc, :], in_=ps[:, 0:256], func=AF.Copy, scale=scl), reads=[b_ps], writes=[b_dst])
        for t in range(2):
            ps, b_ps = k.next_ps()

            def mm(ps=ps, t=t):
                inst = None
                for kc in range(KC):
                    inst = nc.tensor.matmul(ps[:, 0:512].rearrange("p (j n) -> p j n", j=2),
                                            hh2[:, kc, tok0 + t * 128:tok0 + (t + 1) * 128],
                                            wmA[:, :, kc, :], start=(kc == 0), stop=(kc == KC - 1))
                return inst
            P.op("pe", mm, reads=[b_wmA, b_hh2], writes=[b_ps])
            P.op("act", lambda ps=ps, t=t: nc.scalar.copy(ktok[:, t, :], ps[:, 0:256]), reads=[b_ps], writes=[b_ktok])
            yield P.op("dve", lambda ps=ps, t=t: nc.vector.tensor_copy(vb[:, t, 0:256], ps[:, 256:512]), reads=[b_ps], writes=[b_vb])
            ps2, b_ps2 = k.next_ps()

            def mm2(ps2=ps2, t=t):
                inst = None
                for kc in range(KC):
                    inst = nc.tensor.matmul(ps2[:, 0:256], hh2[:, kc, tok0 + t * 128:tok0 + (t + 1) * 128],
                                            wmB[:, 1, kc, :], start=(kc == 0), stop=(kc == KC - 1))
                return inst
            P.op("pe", mm2, reads=[b_wmB, b_hh2], writes=[b_ps2])
            yield P.op("act", lambda ps2=ps2, t=t: nc.scalar.activation(out=sgo[:, t, :], in_=ps2[:, 0:256], func=AF.Sigmoid),
                       reads=[b_ps2], writes=[b_sgo])
        has_state = [False, False]
        if u == 2:
            kf, b_kf = B["kf"]
            vf, b_vf = B["vf"]
            for ft in range(6):
                ps, b_ps = k.next_ps()

                def mm(ps=ps, ft=ft):
                    inst = None
                    for kc in range(KC):
                        inst = nc.tensor.matmul(ps[:, 0:512].rearrange("p (j n) -> p j n", j=2),
                                                hh2f[:, kc, ft * 128:(ft + 1) * 128],
                                                wmA[:, :, kc, :], start=(kc == 0), stop=(kc == KC - 1))
                    return inst
                P.op("pe", mm, reads=[b_wmA, b_hh2f], writes=[b_ps])
                P.op("act", lambda ps=ps, ft=ft: nc.scalar.copy(kf[:, ft, :], ps[:, 0:256]), reads=[b_ps], writes=[b_kf])
                yield P.op("dve", lambda ps=ps, ft=ft: nc.vector.tensor_copy(vf[:, ft, 0:256], ps[:, 256:512]),
                           reads=[b_ps], writes=[b_vf])
            for d in range(2):
                C, b_C = cst[d]
                P.dma("sp", C[:, :, 0:256], I["sC"][d, h].rearrange("(dc p) v -> p dc v", p=128), writes=[b_C])
                with nc.allow_non_contiguous_dma(reason="state_n column"):
                    P.dma("sp", C[:, :, 256:257], I["sn"][d, h].rearrange("(dc p one) -> p dc one", p=128, one=1),
                          writes=[b_C])
                has_state[d] = True
            for j in range(6):
                for d in range(2):
                    ft = j if d == 0 else 5 - j
                    sl = slot_scan(d, j)
                    va, b_va = build_vaug(vf[:, ft, 0:257], b_vf, sl)
                    yield None
                    yield from state_update(d, lambda dc, ft=ft: kf[:, ft, dc * 128:(dc + 1) * 128], b_kf, va, b_va, sl, True)
        P.op("pool", lambda: nc.gpsimd.memset(hm[:], 0.0), writes=[b_hm])

        def step_ops(d, step, X):
            sbf, b_sbf = X["sbf"]
            sT, b_sT = X["sT"]
            csb, b_csb = X["csb"]
            x2s, b_x2s = X["x2s"]
            nums, b_nums = X["nums"]
            sc1, b_sc1 = X["sc1"]
            C, b_C = cst[d]
            c = step if d == 0 else 1 - step
            sl = slot_p(u, d, step) if u < 2 else slot_s(d, step)
            oi = own_index(sl)
            pq, b_pq = k.next_ps()

            def mmqk():
                nc.tensor.matmul(pq[:, 0:128], qT[:, 0, c * 128:(c + 1) * 128], kT[:, 0, c * 128:(c + 1) * 128],
                                 start=True, stop=False)
                return nc.tensor.matmul(pq[:, 0:128], qT[:, 1, c * 128:(c + 1) * 128], kT[:, 1, c * 128:(c + 1) * 128],
                                        start=False, stop=True)
            P.op("pe", mmqk, reads=[b_kT, b_qT], writes=[b_pq])
            yield P.op("dve", lambda: nc.vector.tensor_tensor(sbf[:], pq[:, 0:128], dmat[:, oi, h, :], ALU.mult),
                       reads=[b_pq, b_dmat], writes=[b_sbf])
            pt, b_pt = k.next_pst()
            P.op("pe", lambda: nc.tensor.transpose(pt[:, 0:128], sbf[:], idb[:]), reads=[b_sbf, b_idb], writes=[b_pt])
            yield P.op("act", lambda: nc.scalar.copy(sT[:], pt[:, 0:128]), reads=[b_pt], writes=[b_sT])
            hs = has_state[d]
            if hs:
                yield P.op("act", lambda: nc.scalar.copy(csb[:, :, 0:257], C[:, :, 0:257]), reads=[b_C], writes=[b_csb])
                p2, b_p2 = k.next_ps()

                def mm2():
                    nc.tensor.matmul(p2[:, 0:257], qT[:, 0, c * 128:(c + 1) * 128], csb[:, 0, 0:257], start=True, stop=False)
                    return nc.tensor.matmul(p2[:, 0:257], qT[:, 1, c * 128:(c + 1) * 128], csb[:, 1, 0:257],
                                            start=False, stop=True)
                P.op("pe", mm2, reads=[b_qT, b_csb], writes=[b_p2])
                yield P.op("act", lambda: nc.scalar.activation(out=x2s[:], in_=p2[:, 0:257], func=AF.Copy, scale=ecol[:, sl, 12 + h:13 + h]),
                           reads=[b_p2, b_ecol], writes=[b_x2s])
            px, b_px = k.next_ps()
            P.op("pe", lambda: nc.tensor.matmul(px[:, 0:257], sT[:], vb[:, c, 0:257], start=True, stop=True),
                 reads=[b_sT, b_vb], writes=[b_px])
            if hs:
                yield P.op("dve", lambda: nc.vector.tensor_tensor(nums[:, 0:257], px[:, 0:257], x2s[:, 0:257], ALU.add),
                           reads=[b_px, b_x2s], writes=[b_nums])
            else:
                yield P.op("act", lambda: nc.scalar.copy(nums[:, 0:257], px[:, 0:257]), reads=[b_px], writes=[b_nums])
            yield P.op("dve", lambda: nc.vector.tensor_tensor(sc1[:, 0:1], nums[:, 256:257], ecol[:, sl, 8 + h:9 + h], ALU.max),
                       reads=[b_nums, b_ecol], writes=[b_sc1])
            yield P.op("dve", lambda: nc.vector.scalar_tensor_tensor(out=sc1[:, 0:1], in0=nums[:, 256:257], scalar=-1.0, in1=sc1[:, 0:1],
                                                                     op0=ALU.mult, op1=ALU.max),
                       reads=[b_nums, b_sc1], writes=[b_sc1])
            yield P.op("dve", lambda: nc.vector.reciprocal(sc1[:, 0:1], sc1[:, 0:1]), reads=[b_sc1], writes=[b_sc1])
            yield P.op("dve", lambda: nc.vector.scalar_tensor_tensor(
                out=hm[:, c, :], in0=nums[:, 0:256], scalar=sc1[:, 0:1], in1=hm[:, c, :], op0=ALU.mult, op1=ALU.add),
                reads=[b_nums, b_sc1, b_hm], writes=[b_hm])
            if step == 0 or u < 2:
                va, b_va = B["vaug%d" % d]
                P.op("dve", lambda: nc.vector.tensor_scalar(va[:, 0:257], vb[:, c, 0:257], ecol[:, sl, h:h + 1], None, ALU.mult),
                     reads=[b_vb, b_ecol], writes=[b_va])
                yield None
                yield from state_update(d, lambda dc: ktok[:, c, dc * 128:(dc + 1) * 128], b_ktok, va, b_va, sl, hs)
                has_state[d] = True

        Xs = dict(sbf=B["sbf"], sT=B["sT"], csb=B["csb"], x2s=B["x2s"], nums=B["nums"], sc1=B["sc1"])
        if u < 2:
            X1 = dict(sbf=B["sbf1"], sT=B["sT1"], csb=B["csb1"], x2s=B["x2s1"], nums=B["nums1"], sc1=B["sc11"])
        else:
            (n_sq0, _), (n_sq1, _) = k.nm[0]
            n_rs, _ = k.nm[1]
            (n_t0, _), (n_t1, _) = k.nm[2]
            t0b = n_t0[:].bitcast(BF16)
            X1 = dict(sbf=(t0b[:, 0:128], B["y_sbf"]), sT=(t0b[:, 128:256], B["y_sT"]),
                      csb=(n_rs[:].bitcast(BF16)[:, 0:544].rearrange("p (c n) -> p c n", c=2), B["y_csb"]),
                      x2s=(n_sq1[:, 0:257], B["y_x2s"]), nums=(n_sq0[:, 0:272], B["y_nums"]), sc1=(n_t1[:, 0:4], B["y_sc1"]))
        if True:

            def dir_stream(d, X):
                for step in range(2):
                    yield from step_ops(d, step, X)
            streams = [dir_stream(0, Xs), dir_stream(1, X1)]
            while streams:
                alive = []
                for g in streams:
                    try:
                        yield next(g)
                        alive.append(g)
                    except StopIteration:
                        pass
                streams = alive
        if u < 2:
            for d in range(2):
                C, b_C = cst[d]
                P.dma("sp", O["nC"][u, d, h].rearrange("(dc p) v -> p dc v", p=128), C[:, :, 0:256], reads=[b_C], is_output=True)
                with nc.allow_non_contiguous_dma(reason="state_n column"):
                    P.dma("sp", O["nn"][u, d, h].rearrange("(dc p one) -> p dc one", p=128, one=1), C[:, :, 256:257],
                          reads=[b_C], is_output=True)
        for c in range(2):
            yield P.op("act", lambda c=c: nc.scalar.activation(out=tmp, in_=hm[:, c, :], func=AF.Square), reads=[b_hm], writes=[b_tmp])
            yield P.op("dve", lambda: nc.vector.tensor_reduce(sc1[:, 1:2], tmp, AX.X, ALU.add), reads=[b_tmp], writes=[b_sc1])
            yield P.op("act", lambda: nc.scalar.activation(out=sc1[:, 1:2], in_=sc1[:, 1:2], func=AF.Sqrt, scale=1.0 / 256.0,
                                                           bias=epst[:, 0:1]), reads=[b_sc1, b_eps], writes=[b_sc1])
            yield P.op("dve", lambda: nc.vector.reciprocal(sc1[:, 1:2], sc1[:, 1:2]), reads=[b_sc1], writes=[b_sc1])
            yield P.op("dve", lambda c=c: nc.vector.scalar_tensor_tensor(
                out=tmp, in0=hm[:, c, :], scalar=sc1[:, 1:2], in1=gmh[:, h * 256:(h + 1) * 256], op0=ALU.mult, op1=ALU.mult),
                reads=[b_hm, b_sc1, b_gmh], writes=[b_tmp])
            yield P.op("dve", lambda c=c: nc.vector.tensor_tensor(hb[:], tmp, sgo[:, c, :], ALU.mult),
                       reads=[b_tmp, b_sgo], writes=[b_hb])
            for dc in range(2):
                pt, b_pt = k.next_pst()
                P.op("pe", lambda pt=pt, dc=dc: nc.tensor.transpose(pt[:, 0:128], hb[:, dc * 128:(dc + 1) * 128], idb[:]),
                     reads=[b_hb, b_idb], writes=[b_pt])
                yield P.op("act", lambda pt=pt, dc=dc, c=c: nc.scalar.copy(
                    hmT[:, 2 * h + dc, tok0 + c * 128:tok0 + (c + 1) * 128], pt[:, 0:128]), reads=[b_pt], writes=[b_hmT])

    BP0, BP1, BS = bufset("p0", False), bufset("p1", False), bufset("s", True)
    for h in range(4):
        (rA, b_wmA), (rB, b_wmB) = k.ring[(h % 2) * 2], k.ring[(h % 2) * 2 + 1]
        wmA = rA[:].rearrange("p (j kc n) -> p j kc n", j=2, kc=KC)
        wmB = rB[:].rearrange("p (j kc n) -> p j kc n", j=2, kc=KC)
        for (wt, b_wt, j, off) in ((wmA, b_wmA, 0, O_MK), (wmA, b_wmA, 1, O_MV), (wmB, b_wmB, 0, O_MQ), (wmB, b_wmB, 1, O_MO)):
            if h == 0 and getattr(k, "mlstm_w0_loaded", False):
                break
            P.dma("pool", wt[:, j], win[:, :, off + h * 256:off + (h + 1) * 256], writes=[b_wt])
        run_lanes([mlstm_iter(h, 0, BP0, wmA, b_wmA, wmB, b_wmB), mlstm_iter(h, 1, BP1, wmA, b_wmA, wmB, b_wmB),
                   mlstm_iter(h, 2, BS, wmA, b_wmA, wmB, b_wmB)], [2, 2, 3])


def run_lanes(lanes, weights):
    active = list(zip(lanes, weights))
    while active:
        nxt = []
        for g, w in active:
            alive = True
            for _ in range(w):
                try:
                    next(g)
                except StopIteration:
                    alive = False
                    break
            if alive:
                nxt.append((g, w))
        active = nxt


def rope_apply(k, dst, src, tab, nt, bufs_r, bufs_w, t1, b_t1, t2, b_t2):
    nc, P = k.nc, k.P
    sv = src.rearrange("p t (q h f) -> p t q h f", h=2, f=16)
    dv = dst.rearrange("p t (q h f) -> p t q h f", h=2, f=16)
    cos = tab[:, :, 0:64].rearrange("p t (q f) -> p t q f", f=16)
    sin = tab[:, :, 64:128].rearrange("p t (q f) -> p t q f", f=16)
    x1, x2 = sv[:, :, :, 0, :], sv[:, :, :, 1, :]
    o1, o2 = dv[:, :, :, 0, :], dv[:, :, :, 1, :]
    a = t1[:, 0:nt * 64].rearrange("p (t q f) -> p t q f", t=nt, f=16)
    b = t2[:, 0:nt * 64].rearrange("p (t q f) -> p t q f", t=nt, f=16)
    a2 = t1[:, nt * 64:nt * 128].rearrange("p (t q f) -> p t q f", t=nt, f=16)
    b2 = t2[:, nt * 64:nt * 128].rearrange("p (t q f) -> p t q f", t=nt, f=16)
    yield P.op("dve", lambda: nc.vector.tensor_tensor(a, x1, cos, ALU.mult), reads=bufs_r, writes=[b_t1])
    yield P.op("dve", lambda: nc.vector.tensor_tensor(b, x2, sin, ALU.mult), reads=bufs_r, writes=[b_t2])
    yield P.op("dve", lambda: nc.vector.tensor_tensor(a2, x2, cos, ALU.mult), reads=bufs_r, writes=[b_t1])
    yield P.op("dve", lambda: nc.vector.tensor_tensor(b2, x1, sin, ALU.mult), reads=bufs_r, writes=[b_t2])
    yield P.op("dve", lambda: nc.vector.tensor_tensor(o1, a, b, ALU.subtract), reads=[b_t1, b_t2], writes=bufs_w)
    yield P.op("dve", lambda: nc.vector.tensor_tensor(o2, a2, b2, ALU.add), reads=[b_t1, b_t2], writes=bufs_w)


def emit_attn(k, ph, hh2, b_hh2, attT, b_attT):
    nc, P, I, O, sb = k.nc, k.P, k.I, k.O, k.sb
    idb, b_idb = k.c["idb"]
    epst, b_eps = k.c["eps"]
    hh2f, b_hh2f = k.hh2f
    win = I["w_in"].rearrange("(kc p) n -> p kc n", p=128)
    was = [sb("wa%d" % i, [128, KC, 384], BF16, stack=ph) for i in range(2)]
    gqk, b_gqk = sb("gqk", [128, 256], stack=ph)
    for j, nm in enumerate(("g_qn", "g_qn", "g_kn", "g_kn")):
        P.dma("sp", gqk[:, j * 64:(j + 1) * 64], I[nm].partition_broadcast(128), writes=[b_gqk])
    gsub, b_gsub = sb("gsub", [128, 128], stack=ph)
    P.dma("sp", gsub[:], I["g_sub"].partition_broadcast(128), writes=[b_gsub])
    P.op("dve", lambda: nc.vector.tensor_scalar(gsub[:], gsub[:], 1.0 - LAM_INIT, None, ALU.mult), reads=[b_gsub], writes=[b_gsub])
    lamt, b_lamt = sb("lamt", [128, 256], stack=ph)
    P.dma("sp", lamt[:], I["lamv"].rearrange("a b -> (a b)").partition_broadcast(128), writes=[b_lamt])
    lsc, b_lsc = sb("lsc", [128, 4], stack=ph)
    lpr, b_lpr = sb("lpr", [128, 128], stack=ph)
    for i in range(2):
        P.op("dve", lambda i=i: nc.vector.tensor_tensor(lpr[:, i * 64:(i + 1) * 64], lamt[:, i * 128:i * 128 + 64],
                                                        lamt[:, i * 128 + 64:i * 128 + 128], ALU.mult),
             reads=[b_lamt], writes=[b_lpr])
    P.op("dve", lambda: nc.vector.tensor_reduce(lsc[:, 0:2], lpr[:].rearrange("p (a b) -> p a b", a=2), AX.X, ALU.add),
         reads=[b_lpr], writes=[b_lsc])
    P.op("act", lambda: nc.scalar.activation(out=lsc[:, 0:2], in_=lsc[:, 0:2], func=AF.Exp), reads=[b_lsc], writes=[b_lsc])
    P.op("dve", lambda: nc.vector.tensor_tensor(lsc[:, 2:3], lsc[:, 1:2], lsc[:, 0:1], ALU.subtract), reads=[b_lsc], writes=[b_lsc])
    P.op("dve", lambda: nc.vector.tensor_scalar(lsc[:, 2:3], lsc[:, 2:3], -LAM_INIT, None, ALU.add), reads=[b_lsc], writes=[b_lsc])
    ropo, b_ropo = sb("rope_o", [128, 2, 128], stack=ph)
    P.dma("sp", ropo[:], I["rope_own"].rearrange("(t p) c -> p t c", p=128), writes=[b_ropo])
    ropf, b_ropf = sb("rope_f", [128, 8, 128], stack=ph)
    P.dma("sp", ropf[:], I["rope_full"].rearrange("(t p) c -> p t c", p=128), writes=[b_ropf])

    def bufset(tag, sample):
        B = {}
        nkt = 12 if sample else 2
        spec = [("ssc", [128, 16], F32), ("qT", [64, 2, 256], BF16), ("kT", [64, 2, nkt * 128], BF16),
                ("vaug", [128, nkt, 144], BF16), ("osb", [128, 256], F32), ("ot2", [128, 256], F32), ("ab", [128, 256], BF16)]
        if sample:
            spec += [("qs", [128, 2, 128], F32), ("qr", [128, 2, 128], F32), ("qb", [128, 2, 128], BF16),
                     ("ck4", [128, 4, 128], F32), ("cv4", [128, 4, 128], F32), ("kb4", [128, 4, 128], BF16),
                     ("kb4b", [128, 4, 128], BF16), ("kbq", [128, 4, 128], BF16), ("ssc2", [128, 16], F32), ("sscq", [128, 16], F32)]
        else:
            spec += [("qkv", [128, 2, 384], F32), ("sq", [128, 2, 256], F32), ("qkn", [128, 2, 256], F32),
                     ("qkb", [128, 2, 256], BF16), ("E", [128, 2, 512], BF16)]
        for nm, shape, dt in spec:
            B[nm] = sb("a_%s_%s" % (nm, tag), shape, dt, stack=ph)
        va, b_va = B["vaug"]
        P.op("pool", lambda: nc.gpsimd.memset(va[:], 1.0), writes=[b_va])
        if sample:
            for nm in ("x_r2a", "x_r2b", "x_r2c", "x_r2d", "x_r3a", "x_r3b", "x_r3c"):
                B[nm] = Buf(nm)
        return B

    def rstd_inplace(ssc, b_ssc, n, inv):
        yield P.op("act", lambda: nc.scalar.activation(out=ssc[:, 0:n], in_=ssc[:, 0:n], func=AF.Ln, scale=inv,
                                                       bias=epst[:, 0:1]), reads=[b_ssc, b_eps], writes=[b_ssc])
        yield P.op("act", lambda: nc.scalar.activation(out=ssc[:, 0:n], in_=ssc[:, 0:n], func=AF.Exp, scale=-0.5),
                   reads=[b_ssc], writes=[b_ssc])

    def transposes(srcs, b_src, dst_ap, b_dst, nblk):
        pt, b_pt = k.next_pst()

        def tr():
            inst = None
            for i, s_ in enumerate(srcs):
                inst = nc.tensor.transpose(pt[0:64, i * 128:(i + 1) * 128], s_, idb[:])
            return inst
        P.op("pe", tr, reads=[b_src, b_idb], writes=[b_pt])
        yield P.op("act", lambda: nc.scalar.copy(dst_ap, pt[0:64, 0:nblk * 128]), reads=[b_pt], writes=[b_dst])

    def tail(h, u, B, Eget, nkt):
        ssc, b_ssc = B["ssc"]
        qT, b_qT = B["qT"]
        kT, b_kT = B["kT"]
        vaug, b_vaug = B["vaug"]
        osb, b_osb = B["osb"]
        ot2, b_ot2 = B["ot2"]
        ab, b_ab = B["ab"]
        tok0 = u * 256
        for kt in range(nkt):
            ps, b_ps = k.next_ps()
            Et, b_E = Eget(kt)

            def mm(ps=ps, kt=kt):
                nc.tensor.matmul(ps[:, 0:256], kT[:, 0, kt * 128:(kt + 1) * 128], qT[:, 0, :], start=True, stop=True)
                return nc.tensor.matmul(ps[:, 256:512], kT[:, 1, kt * 128:(kt + 1) * 128], qT[:, 1, :], start=True, stop=True)
            P.op("pe", mm, reads=[b_kT, b_qT], writes=[b_ps])
            yield P.op("act", lambda ps=ps, Et=Et: nc.scalar.activation(out=Et, in_=ps[:, 0:512], func=AF.Exp, scale=0.125),
                       reads=[b_ps], writes=[b_E])
        psA, b_psA = k.next_ps()
        psB, b_psB = k.next_ps()

        def mm():
            inst = None
            for c, pb in ((0, psA), (1, psB)):
                for qt in range(2):
                    for kt in range(nkt):
                        Et, _ = Eget(kt)
                        inst = nc.tensor.matmul(pb[:, qt * 256:qt * 256 + 129], Et[:, c * 256 + qt * 128:c * 256 + (qt + 1) * 128],
                                                vaug[:, kt, 0:129], start=(kt == 0), stop=(kt == nkt - 1))
            return inst
        P.op("pe", mm, reads=[Eget(0)[1], Eget(nkt - 1)[1], b_vaug], writes=[b_psA, b_psB])
        vA = psA[:, 0:512].rearrange("p (t n) -> p t n", t=2)
        vB = psB[:, 0:512].rearrange("p (t n) -> p t n", t=2)
        o3 = osb[:].rearrange("p (t n) -> p t n", t=2)
        t3 = ot2[:].rearrange("p (t n) -> p t n", t=2)
        a3 = ab[:].rearrange("p (t n) -> p t n", t=2)
        P.op("dve", lambda: nc.vector.reciprocal(ssc[:, 8:10], vA[:, :, 128]), reads=[b_psA], writes=[b_ssc])
        P.op("dve", lambda: nc.vector.reciprocal(ssc[:, 10:12], vB[:, :, 128]), reads=[b_psB], writes=[b_ssc])
        P.op("dve", lambda: nc.vector.tensor_tensor(ssc[:, 10:12], ssc[:, 10:12], lsc[:, 2:3].to_broadcast([128, 2]), ALU.mult),
             reads=[b_ssc, b_lsc], writes=[b_ssc])
        P.op("dve", lambda: nc.vector.tensor_tensor(t3, vB[:, :, 0:128], ssc[:, 10:12].unsqueeze(2).to_broadcast([128, 2, 128]), ALU.mult),
             reads=[b_psB, b_ssc], writes=[b_ot2])
        yield P.op("dve", lambda: nc.vector.tensor_tensor(o3, vA[:, :, 0:128], ssc[:, 8:10].unsqueeze(2).to_broadcast([128, 2, 128]), ALU.mult),
                   reads=[b_psA, b_ssc], writes=[b_osb])
        yield P.op("dve", lambda: nc.vector.tensor_tensor(osb[:], osb[:], ot2[:], ALU.add), reads=[b_osb, b_ot2], writes=[b_osb])
        yield P.op("act", lambda: nc.scalar.activation(out=ot2[:], in_=osb[:], func=AF.Square), reads=[b_osb], writes=[b_ot2])
        yield P.op("dve", lambda: nc.vector.tensor_reduce(ssc[:, 12:14], t3, AX.X, ALU.add), reads=[b_ot2], writes=[b_ssc])
        yield P.op("act", lambda: nc.scalar.activation(out=ssc[:, 12:14], in_=ssc[:, 12:14], func=AF.Ln, scale=1.0 / 128.0,
                                                       bias=epst[:, 0:1]), reads=[b_ssc, b_eps], writes=[b_ssc])
        yield P.op("act", lambda: nc.scalar.activation(out=ssc[:, 12:14], in_=ssc[:, 12:14], func=AF.Exp, scale=-0.5),
                   reads=[b_ssc], writes=[b_ssc])
        yield P.op("dve", lambda: nc.vector.tensor_tensor(o3, o3, ssc[:, 12:14].unsqueeze(2).to_broadcast([128, 2, 128]), ALU.mult),
                   reads=[b_osb, b_ssc], writes=[b_osb])
        yield P.op("dve", lambda: nc.vector.tensor_tensor(a3, o3, gsub[:].unsqueeze(1).to_broadcast([128, 2, 128]), ALU.mult),
                   reads=[b_osb, b_gsub], writes=[b_ab])
        pt, b_pt = k.next_pst()

        def tr():
            nc.tensor.transpose(pt[:, 0:128], ab[:, 0:128], idb[:])
            return nc.tensor.transpose(pt[:, 128:256], ab[:, 128:256], idb[:])
        P.op("pe", tr, reads=[b_ab, b_idb], writes=[b_pt])
        yield P.op("act", lambda: nc.scalar.copy(attT[:, h, tok0:tok0 + 256], pt[:, 0:256]), reads=[b_pt], writes=[b_attT])

    def prompt_iter(h, u, B, wa, b_wa):
        qkv, b_qkv = B["qkv"]
        sq, b_sq = B["sq"]
        qkn, b_qkn = B["qkn"]
        qkb, b_qkb = B["qkb"]
        ssc, b_ssc = B["ssc"]
        qT, b_qT = B["qT"]
        kT, b_kT = B["kT"]
        vaug, b_vaug = B["vaug"]
        E, b_E = B["E"]
        tok0 = u * 256
        for t in range(2):
            ps, b_ps = k.next_ps()

            def mm(ps=ps, t=t):
                inst = None
                for kc in range(KC):
                    inst = nc.tensor.matmul(ps[:, 0:384], hh2[:, kc, tok0 + t * 128:tok0 + (t + 1) * 128], wa[:, kc, :],
                                            start=(kc == 0), stop=(kc == KC - 1))
                return inst
            P.op("pe", mm, reads=[b_wa, b_hh2], writes=[b_ps])
            yield P.op("act", lambda ps=ps, t=t: nc.scalar.copy(qkv[:, t, :], ps[:, 0:384]), reads=[b_ps], writes=[b_qkv])
        yield P.op("act", lambda: nc.scalar.activation(out=sq[:], in_=qkv[:, :, 0:256], func=AF.Square), reads=[b_qkv], writes=[b_sq])
        yield P.op("dve", lambda: nc.vector.tensor_reduce(ssc[:, 0:8], sq[:].rearrange("p t (g d) -> p (t g) d", d=64), AX.X, ALU.add),
                   reads=[b_sq], writes=[b_ssc])
        yield from rstd_inplace(ssc, b_ssc, 8, 1.0 / 64.0)
        yield P.op("dve", lambda: nc.vector.tensor_tensor(
            qkn[:].rearrange("p t (g d) -> p t g d", d=64), qkv[:, :, 0:256].rearrange("p t (g d) -> p t g d", d=64),
            ssc[:, 0:8].rearrange("p (t g) -> p t g", t=2).unsqueeze(3).to_broadcast([128, 2, 4, 64]), ALU.mult),
            reads=[b_qkv, b_ssc], writes=[b_qkn])
        yield P.op("dve", lambda: nc.vector.tensor_tensor(qkn[:], qkn[:], gqk[:].unsqueeze(1).to_broadcast([128, 2, 256]), ALU.mult),
                   reads=[b_qkn, b_gqk], writes=[b_qkn])
        r0 = u * 256
        P.dma("sp", O["nk"][r0:r0 + 256, h * 128:(h + 1) * 128].rearrange("(t p) c -> p t c", p=128), qkn[:, :, 128:256],
              reads=[b_qkn], is_output=True)
        P.dma("sp", O["nv"][r0:r0 + 256, h * 128:(h + 1) * 128].rearrange("(t p) c -> p t c", p=128), qkv[:, :, 256:384],
              reads=[b_qkv], is_output=True)
        yield P.op("dve", lambda: nc.vector.tensor_copy(qkb[:], qkn[:]), reads=[b_qkn], writes=[b_qkb])
        srcs = [qkb[:, t, w * 128 + c * 64:w * 128 + (c + 1) * 64] for w in range(2) for c in range(2) for t in range(2)]
        pt, b_pt = k.next_pst()

        def tr():
            inst = None
            for i, s_ in enumerate(srcs):
                inst = nc.tensor.transpose(pt[0:64, i * 128:(i + 1) * 128], s_, idb[:])
            return inst
        P.op("pe", tr, reads=[b_qkb, b_idb], writes=[b_pt])
        P.op("act", lambda: nc.scalar.copy(qT[:].rearrange("p c n -> p (c n)"), pt[0:64, 0:512]), reads=[b_pt], writes=[b_qT])
        yield P.op("dve", lambda: nc.vector.tensor_copy(kT[:].rearrange("p c n -> p (c n)"), pt[0:64, 512:1024]), reads=[b_pt], writes=[b_kT])
        yield P.op("dve", lambda: nc.vector.tensor_copy(vaug[:, :, 0:128], qkv[:, :, 256:384]), reads=[b_qkv], writes=[b_vaug])
        yield from tail(h, u, B, lambda kt: (E[:, kt, :], b_E), 2)

    def interleave(gens):
        gens = list(gens)
        while gens:
            nxt = []
            for g in gens:
                try:
                    yield next(g)
                    nxt.append(g)
                except StopIteration:
                    pass
            gens = nxt

    def sample_iter(h, B, wa, b_wa):
        u = 2
        tok0 = 512
        qT, b_qT = B["qT"]
        kT, b_kT = B["kT"]
        vaug, b_vaug = B["vaug"]
        qs, b_qs = B["qs"]
        qr, b_qr = B["qr"]
        qb, b_qb = B["qb"]
        ck4, b_ck4 = B["ck4"]
        cv4, b_cv4 = B["cv4"]
        (E0r, b_E0), (E1r, b_E1) = k.ring[0], k.ring[1]
        E0 = E0r[:, 0:3072].rearrange("p (t n) -> p t n", t=6)
        E1 = E1r[:, 0:3072].rearrange("p (t n) -> p t n", t=6)
        Eget = lambda kt: ((E0[:, kt, :], b_E0) if kt < 6 else (E1[:, kt - 6, :], b_E1))
        r2 = k.ring[2][0][:].bitcast(F32)
        r3 = k.ring[3][0][:].bitcast(F32)
        (sqt, b_sqt), (knt, b_knt) = k.nm[0]
        krt, b_krt = k.nm[1]
        (rtt, b_rtt), (qnt, b_qnt) = k.nm[2]
        KS = [dict(kv=k.stage[0], sq=(sqt[:], b_sqt), kn=(knt[:], b_knt), kr=(krt[:], b_krt), t1=(rtt[:], b_rtt), t2=(qnt[:], b_qnt),
                   kb=B["kb4"], ssc=B["ssc"]),
              dict(kv=k.stage[1], sq=(r2[:, 0:512], B["x_r2a"]), kn=(r2[:, 512:1024], B["x_r2b"]), kr=(r2[:, 1024:1536], B["x_r2c"]),
                   t1=(r2[:, 1536:2048], B["x_r2d"]), t2=(r3[:, 0:512], B["x_r3a"]), kb=B["kb4b"], ssc=B["ssc2"])]

        def q_lane():
            ssc, b_ssc = B["sscq"]
            kbq, b_kbq = B["kbq"]
            sqq = r3[:, 512:768].rearrange("p (t n) -> p t n", t=2)
            qn = r3[:, 768:1024].rearrange("p (t n) -> p t n", t=2)
            b_qn = B["x_r3b"]
            yield P.dma("sp", ck4[:], I["ck"][:, h * 128:(h + 1) * 128].rearrange("(t p) c -> p t c", p=128), writes=[b_ck4])
            yield P.dma("sp", cv4[:], I["cvv"][:, h * 128:(h + 1) * 128].rearrange("(t p) c -> p t c", p=128), writes=[b_cv4])
            for t in range(2):
                ps, b_ps = k.next_ps()

                def mm(ps=ps, t=t):
                    inst = None
                    for kc in range(KC):
                        inst = nc.tensor.matmul(ps[:, 0:128], hh2[:, kc, tok0 + t * 128:tok0 + (t + 1) * 128], wa[:, kc, 0:128],
                                                start=(kc == 0), stop=(kc == KC - 1))
                    return inst
                P.op("pe", mm, reads=[b_wa, b_hh2], writes=[b_ps])
                yield P.op("act", lambda ps=ps, t=t: nc.scalar.copy(qs[:, t, :], ps[:, 0:128]), reads=[b_ps], writes=[b_qs])
            yield P.op("act", lambda: nc.scalar.activation(out=sqq, in_=qs[:], func=AF.Square), reads=[b_qs], writes=[b_qn])
            yield P.op("dve", lambda: nc.vector.tensor_reduce(ssc[:, 0:4], sqq.rearrange("p t (g d) -> p (t g) d", d=64), AX.X, ALU.add),
                       reads=[b_qn], writes=[b_ssc])
            yield from rstd_inplace(ssc, b_ssc, 4, 1.0 / 64.0)
            yield P.op("dve", lambda: nc.vector.tensor_tensor(
                qn.rearrange("p t (g d) -> p t g d", d=64), qs[:].rearrange("p t (g d) -> p t g d", d=64),
                ssc[:, 0:4].rearrange("p (t g) -> p t g", t=2).unsqueeze(3).to_broadcast([128, 2, 2, 64]), ALU.mult),
                reads=[b_qs, b_ssc, b_qn], writes=[b_qn])
            yield P.op("dve", lambda: nc.vector.tensor_tensor(qn, qn, gqk[:, 0:128].unsqueeze(1).to_broadcast([128, 2, 128]), ALU.mult),
                       reads=[b_qn, b_gqk], writes=[b_qn])
            yield from rope_apply(k, qr[:], qn, ropo[:], 2, [b_qn, b_ropo], [b_qr], r3[:, 1024:1280], B["x_r3c"], r3[:, 1280:1536], B["x_r3c"])
            yield P.op("dve", lambda: nc.vector.tensor_copy(qb[:], qr[:]), reads=[b_qr], writes=[b_qb])
            yield from transposes([qb[:, t, c * 64:(c + 1) * 64] for c in range(2) for t in range(2)], b_qb,
                                  qT[:].rearrange("p c n -> p (c n)"), b_qT, 4)
            yield P.op("dve", lambda: nc.vector.tensor_copy(kbq[:], ck4[:]), reads=[b_ck4], writes=[b_kbq])
            yield from transposes([kbq[:, j, c * 64:(c + 1) * 64] for c in range(2) for j in range(4)], b_kbq,
                                  kT[:, :, 1024:1536], b_kT, 8)
            yield P.op("dve", lambda: nc.vector.tensor_copy(vaug[:, 8:12, 0:128], cv4[:]), reads=[b_cv4], writes=[b_vaug])

        def k_lane(half):
            S_ = KS[half]
            kv4t, b_kv4 = S_["kv"]
            kv4 = kv4t[:].rearrange("p (t n) -> p t n", t=4)
            (sqa, b_sq), (kna, b_kn), (kra, b_kr) = S_["sq"], S_["kn"], S_["kr"]
            (t1a, b_t1), (t2a, b_t2) = S_["t1"], S_["t2"]
            kb4, b_kb4 = S_["kb"]
            ssc, b_ssc = S_["ssc"]
            for j in range(4):
                ft = half * 4 + j
                ps, b_ps = k.next_ps()

                def mm(ps=ps, ft=ft):
                    inst = None
                    for kc in range(KC):
                        src = hh2[:, kc, tok0 + ft * 128:tok0 + (ft + 1) * 128] if ft < 2 else \
                            hh2f[:, kc, (ft - 2) * 128:(ft - 1) * 128]
                        inst = nc.tensor.matmul(ps[:, 0:256], src, wa[:, kc, 128:384],
                                                start=(kc == 0), stop=(kc == KC - 1))
                    return inst
                P.op("pe", mm, reads=[b_wa, b_hh2f, b_hh2], writes=[b_ps])
                yield P.op("act", lambda ps=ps, j=j: nc.scalar.copy(kv4[:, j, :], ps[:, 0:256]), reads=[b_ps], writes=[b_kv4])
            sq4 = sqa.rearrange("p (t n) -> p t n", t=4)
            kn4 = kna.rearrange("p (t n) -> p t n", t=4)
            kr4 = kra.rearrange("p (t n) -> p t n", t=4)
            yield P.op("act", lambda: nc.scalar.activation(out=sq4, in_=kv4[:, :, 0:128], func=AF.Square), reads=[b_kv4], writes=[b_sq])
            yield P.op("dve", lambda: nc.vector.tensor_reduce(ssc[:, 0:8], sq4.rearrange("p t (g d) -> p (t g) d", d=64), AX.X, ALU.add),
                       reads=[b_sq], writes=[b_ssc])
            yield from rstd_inplace(ssc, b_ssc, 8, 1.0 / 64.0)
            yield P.op("dve", lambda: nc.vector.tensor_tensor(
                kn4.rearrange("p t (g d) -> p t g d", d=64), kv4[:, :, 0:128].rearrange("p t (g d) -> p t g d", d=64),
                ssc[:, 0:8].rearrange("p (t g) -> p t g", t=4).unsqueeze(3).to_broadcast([128, 4, 2, 64]), ALU.mult),
                reads=[b_kv4, b_ssc], writes=[b_kn])
            yield P.op("dve", lambda: nc.vector.tensor_tensor(kn4, kn4, gqk[:, 128:256].unsqueeze(1).to_broadcast([128, 4, 128]), ALU.mult),
                       reads=[b_kn, b_gqk], writes=[b_kn])
            yield from rope_apply(k, kr4, kn4, ropf[:, half * 4:(half + 1) * 4, :], 4, [b_kn, b_ropf], [b_kr], t1a, b_t1, t2a, b_t2)
            yield P.op("dve", lambda: nc.vector.tensor_copy(kb4[:], kr4), reads=[b_kr], writes=[b_kb4])
            yield from transposes([kb4[:, j, c * 64:(c + 1) * 64] for c in range(2) for j in range(4)], b_kb4,
                                  kT[:, :, half * 512:(half + 1) * 512], b_kT, 8)
            yield P.op("dve", lambda: nc.vector.tensor_copy(vaug[:, half * 4:(half + 1) * 4, 0:128], kv4[:, :, 128:256]),
                       reads=[b_kv4], writes=[b_vaug])

        yield from interleave([k_lane(0), k_lane(1), q_lane()])
        yield from tail(h, u, B, Eget, 12)

    BP0, BP1, BS = bufset("p0", False), bufset("p1", False), bufset("s", True)
    for h in range(8):
        wa, b_wa = was[h % 2]
        for j, off in enumerate((O_DQ, O_DK, O_DV)):
            P.dma("pool", wa[:, :, j * 128:(j + 1) * 128], win[:, :, off + h * 128:off + (h + 1) * 128], writes=[b_wa])
        run_lanes([sample_iter(h, BS, wa, b_wa), prompt_iter(h, 0, BP0, wa, b_wa), prompt_iter(h, 1, BP1, wa, b_wa)], [3, 1, 1])


def emit_merge(k, ph, hh2, b_hh2, attT, b_attT, hmT, b_hmT):
    nc, P, I, sb = k.nc, k.P, k.I, k.sb
    xT, b_xT = k.xT
    mG, b_mG = k.mod["G"]
    win = I["w_in"].rearrange("(kc p) n -> p kc n", p=128)
    yT, b_yT = sb("yT", [128, KC, NOWN], BF16, stack=ph)
    sg = [sb("mg_sg%d" % i, [128, 512], stack=ph) for i in range(2)]
    t1, b_t1 = sb("mg_t1", [128, 512], stack=ph)
    t2, b_t2 = sb("mg_t2", [128, 512], stack=ph)
    vw = lambda t: t[:].rearrange("p (kc n) -> p kc n", kc=KC)
    for og in range(2):
        tiles = []
        for src in (I["w_br_m"].rearrange("(kc p) n -> p kc n", p=128)[:, :, og * 512:(og + 1) * 512],
                    I["w_br_d"].rearrange("(kc p) n -> p kc n", p=128)[:, :, og * 512:(og + 1) * 512],
                    win[:, :, O_GM + og * 512:O_GM + (og + 1) * 512],
                    win[:, :, O_GD + og * 512:O_GD + (og + 1) * 512]):
            t, b = ring_load(k, vw, src)
            tiles.append((vw(t), b))
        (wmv, b_wmv), (wdv, b_wdv), (wgm, b_wgm), (wgd, b_wgd) = tiles
        for j in range(4):
            oc = og * 4 + j
            for (t0, n, which) in OWN_GROUPS:
                pss = []
                for (wv, b_w, act, b_act) in ((wgm, b_wgm, hh2, b_hh2), (wmv, b_wmv, hmT, b_hmT),
                                             (wgd, b_wgd, hh2, b_hh2), (wdv, b_wdv, attT, b_attT)):
                    ps, b_ps = k.next_ps()

                    def mm(ps=ps, wv=wv, act=act):
                        inst = None
                        for kc in range(KC):
                            inst = nc.tensor.matmul(ps[:, 0:n], wv[:, kc, j * 128:(j + 1) * 128], act[:, kc, t0:t0 + n],
                                                    start=(kc == 0), stop=(kc == KC - 1))
                        return inst
                    P.op("pe", mm, reads=[b_w, b_act], writes=[b_ps])
                    pss.append((ps, b_ps))
                (pgm, b_pgm), (pm, b_pm), (pgd, b_pgd), (pd, b_pd) = pss
                (s0, b_s0), (s1, b_s1) = sg
                P.op("act", lambda: nc.scalar.activation(out=s0[:, 0:n], in_=pgm[:, 0:n], func=AF.Sigmoid), reads=[b_pgm], writes=[b_s0])
                P.op("act", lambda: nc.scalar.activation(out=s1[:, 0:n], in_=pgd[:, 0:n], func=AF.Sigmoid), reads=[b_pgd], writes=[b_s1])
                P.op("dve", lambda: nc.vector.tensor_tensor(t1[:, 0:n], s0[:, 0:n], pm[:, 0:n], ALU.mult), reads=[b_s0, b_pm], writes=[b_t1])
                P.op("dve", lambda: nc.vector.tensor_tensor(t2[:, 0:n], s1[:, 0:n], pd[:, 0:n], ALU.mult), reads=[b_s1, b_pd], writes=[b_t2])
                P.op("dve", lambda: nc.vector.tensor_tensor(yT[:, oc, t0:t0 + n], t1[:, 0:n], t2[:, 0:n], ALU.add),
                     reads=[b_t1, b_t2], writes=[b_yT])
    wo = I["w_out"].rearrange("(kc p) n -> p kc n", p=128)
    for og in range(2):
        t, b_wo = ring_load(k, vw, wo[:, :, og * 512:(og + 1) * 512])
        wov = vw(t)
        for j in range(4):
            oc = og * 4 + j
            for (t0, n, which) in OWN_GROUPS:
                ps, b_ps = k.next_ps()

                def mm(ps=ps):
                    inst = None
                    for kc in range(KC):
                        inst = nc.tensor.matmul(ps[:, 0:n], wov[:, kc, j * 128:(j + 1) * 128], yT[:, kc, t0:t0 + n],
                                                start=(kc == 0), stop=(kc == KC - 1))
                    return inst
                P.op("pe", mm, reads=[b_wo, b_yT], writes=[b_ps])
                P.op("dve", lambda ps=ps: nc.vector.scalar_tensor_tensor(
                    out=xT[:, oc, t0:t0 + n], in0=ps[:, 0:n], scalar=mG[:, 1, oc, which:which + 1], in1=xT[:, oc, t0:t0 + n],
                    op0=ALU.mult, op1=ALU.add), reads=[b_ps, b_mG, b_xT], writes=[b_xT])


def emit_mixer(k, ph):
    P, sb = k.P, k.sb
    xT, b_xT = k.xT
    hh2, b_hh2 = sb("hh2", [128, KC, NOWN], BF16, stack=ph)
    norm_mod(k, ph, xT, b_xT, OWN_GROUPS, 1, hh2, b_hh2, "o1")
    k.dbg_dump("hh2", hh2[:], b_hh2, [128, KC, NOWN])
    attT, b_attT = sb("attT", [128, KC, NOWN], BF16, stack=ph)
    hmT, b_hmT = sb("hmT", [128, KC, NOWN], BF16, stack=ph)
    with ExitStack() as ph2:
        if "nomlstm" not in k.stages:
            k.gate_keep = [sb("ecol", [128, NSLOT, 16], stack=ph2), sb("dmat", [128, 12, 4, 128], BF16, stack=ph2),
                           sb("mslots", [4, NSLOT + 3], stack=ph2)]
            if "gates_only" not in k.stages:
                win_ = k.I["w_in"].rearrange("(kc p) n -> p kc n", p=128)
                (rA_, b_rA_), (rB_, b_rB_) = k.ring[0], k.ring[1]
                wA_ = rA_[:].rearrange("p (j kc n) -> p j kc n", j=2, kc=KC)
                wB_ = rB_[:].rearrange("p (j kc n) -> p j kc n", j=2, kc=KC)
                for (wt_, b_wt_, j_, off_) in ((wA_, b_rA_, 0, O_MK), (wA_, b_rA_, 1, O_MV), (wB_, b_rB_, 0, O_MQ), (wB_, b_rB_, 1, O_MO)):
                    P.dma("pool", wt_[:, j_], win_[:, :, off_:off_ + 256], writes=[b_wt_])
                k.mlstm_w0_loaded = True
            with ExitStack() as ph3:
                emit_gates(k, ph2, ph3, hh2, b_hh2)
            P.barrier()
            if "gates_only" not in k.stages:
                try:
                    emit_mlstm(k, ph2, hh2, b_hh2, hmT, b_hmT)
                except StopEmit:
                    pass
    P.barrier()
    k.dbg_dump("hmT", hmT[:], b_hmT, [128, KC, NOWN])
    with ExitStack() as ph2:
        if "noattn" not in k.stages:
            emit_attn(k, ph2, hh2, b_hh2, attT, b_attT)
    P.barrier()
    k.dbg_dump("attT", attT[:], b_attT, [128, KC, NOWN])
    with ExitStack() as ph2:
        if "nomerge" not in k.stages:
            emit_merge(k, ph2, hh2, b_hh2, attT, b_attT, hmT, b_hmT)
    k.dbg_dump("x2", xT[:], b_xT, [128, KC, NOWN])


def _rope_table(pos_tokens):
    t = np.asarray(pos_tokens)
    row = (t // 64).astype(np.float32)
    col = (t % 64).astype(np.float32)
    freqs = np.power(np.float32(10000.0), -np.arange(16, dtype=np.float32) / np.float32(16))
    ar = row[:, None] * freqs[None, :]
    ac = col[:, None] * freqs[None, :]
    cos4 = np.concatenate([np.cos(ar), np.cos(ac), np.cos(ar), np.cos(ac)], axis=1)
    sin4 = np.concatenate([np.sin(ar), np.sin(ac), np.sin(ar), np.sin(ac)], axis=1)
    return np.ascontiguousarray(np.concatenate([cos4, sin4], axis=1).astype(np.float32))


def make_in_maps(inp):
    f = lambda a: np.ascontiguousarray(np.asarray(a, dtype=np.float32))
    xp, xs = f(inp["x_prompt"]), f(inp["x_sample"])
    shared = dict(
        w_ada=f(inp["w_ada"][0]), b_ada=f(inp["b_ada"][0]).reshape(72, 128),
        g_norm=f(inp["g_norm"][0]).reshape(24, 128),
        f1w1=f(inp["ffn1_w1"][0]), f1w3=f(inp["ffn1_w3"][0]), f1w2=f(inp["ffn1_w2"][0]),
        f2w1=f(inp["ffn2_w1"][0]), f2w3=f(inp["ffn2_w3"][0]), f2w2=f(inp["ffn2_w2"][0]),
        w_in=f(inp["w_in"][0]), b_gate=f(inp["b_gate"][0]).reshape(1, 16),
        g_qn=f(inp["g_qn"][0]), g_kn=f(inp["g_kn"][0]),
        lamv=f(np.stack([inp["lam_q1"][0], inp["lam_k1"][0], inp["lam_q2"][0], inp["lam_k2"][0]])),
        g_sub=f(inp["g_sub"][0]), g_mh=f(inp["g_mh"][0]).reshape(1024),
        w_br_m=f(inp["w_br_m"][0]), w_br_d=f(inp["w_br_d"][0]), w_out=f(inp["w_out"][0]),
    )
    maps = []
    for i in range(8):
        b, r = i // 4, i % 4
        m = dict(shared)
        m["xo"] = np.ascontiguousarray(np.concatenate(
            [xp[2 * i], xp[2 * i + 1], xs[b, r * 256:(r + 1) * 256]], axis=0))
        others = [q for q in range(4) if q != r]
        m["xf"] = np.ascontiguousarray(np.concatenate([xs[b, q * 256:(q + 1) * 256] for q in others], axis=0))
        m["rope_full"] = _rope_table(np.concatenate([np.arange(q * 256, (q + 1) * 256) for q in [r] + others]))
        m["cv"] = np.ascontiguousarray(np.stack([f(inp["c_ctx"]), f(inp["c"])[b]]))
        m["ck"] = f(inp["cache_k"][b, 0]).reshape(512, 1024)
        m["cvv"] = f(inp["cache_v"][b, 0]).reshape(512, 1024)
        m["sC"] = f(inp["state_C"][b, 0])
        m["sn"] = f(inp["state_n"][b, 0])
        m["sm"] = f(inp["state_m"][b, 0])
        m["rope_own"] = _rope_table(np.arange(r * 256, (r + 1) * 256))
        mu = np.zeros(24, np.float32)
        for j in range(6):
            mu[j] = 1.0 if j < 2 * r else 0.0
            mu[6 + j] = 1.0 if (5 - j) >= 2 * r else 0.0
        mu[12:] = (mu[:12] - 1.0) * BIG
        m["mu"] = mu
        maps.append(m)
    return maps


_CACHE = {}


def kernel(**inputs):
    if "nc" not in _CACHE:
        _CACHE["nc"] = build_program()[0]
    nc = _CACHE["nc"]
    maps = make_in_maps(inputs)
    res = run_bass_kernel_spmd(nc, maps, core_ids=list(range(8)))
    R = res.results
    y_p = np.zeros((16, 256, 1024), np.float32)
    y_s = np.zeros((2, 1024, 1024), np.float32)
    nk = np.zeros((16, 1, 256, 8, 2, 64), np.float32)
    nv = np.zeros((16, 1, 256, 8, 128), np.float32)
    nC = np.zeros((16, 1, 2, 4, 256, 256), np.float32)
    nn = np.zeros((16, 1, 2, 4, 256), np.float32)
    nm = np.zeros((16, 1, 2, 4), np.float32)
    for i in range(8):
        b, r = i // 4, i % 4
        yo = R[i]["yo"]
        y_p[2 * i] = yo[0:256]
        y_p[2 * i + 1] = yo[256:512]
        y_s[b, r * 256:(r + 1) * 256] = yo[512:768]
        nk[2 * i:2 * i + 2, 0] = R[i]["nk"].reshape(2, 256, 8, 2, 64)
        nv[2 * i:2 * i + 2, 0] = R[i]["nv"].reshape(2, 256, 8, 128)
        nC[2 * i:2 * i + 2, 0] = R[i]["nC"]
        nn[2 * i:2 * i + 2, 0] = R[i]["nn"]
        nm[2 * i:2 * i + 2, 0] = R[i]["nm"]
    return (y_p, y_s, nk, nv, nC, nn, nm)
```
